# Optimizing a Trainium2 kernel written in Bass

```python
import jax, jax.numpy as jnp
from jax import lax
import numpy as np

D_MODEL = 1024
BATCH = 4
SEQ = 8192
DEPTH = 1

HEAD_DIM = 64
N_HEADS = 8
DILATED_GROUPS = ((128, 1), (512, 4), (2048, 16))
N_GROUPS = len(DILATED_GROUPS)
ATTN_WIDTH = N_HEADS * HEAD_DIM
CONV_CHANNELS = 512
CONV_WIDTH = 31
D_FF = -(-8 * D_MODEL // (3 * 256)) * 256
ROPE_THETA = 10000.0
BLOCK = 128
LN_EPS = 1e-5
QKV_COLS = N_GROUPS * 3 * ATTN_WIDTH
IN_COLS = QKV_COLS + 2 * CONV_CHANNELS + 2 * D_MODEL

kernel_name = "hybrid_dilated_attn_conformer_conv_deepnorm"


def layer_norm(x, g, b):
    xf = x.astype(jnp.float32)
    mu = jnp.mean(xf, axis=-1, keepdims=True)
    var = jnp.mean(jnp.square(xf - mu), axis=-1, keepdims=True)
    y = (xf - mu) * lax.rsqrt(var + LN_EPS)
    return (y * g.astype(jnp.float32) + b.astype(jnp.float32)).astype(x.dtype)


def rotary_tables(seq_len):
    inv_freq = 1.0 / (ROPE_THETA ** (jnp.arange(0, HEAD_DIM, 2, dtype=jnp.float32) / HEAD_DIM))
    ang = jnp.arange(seq_len, dtype=jnp.float32)[:, None] * inv_freq[None, :]
    return jnp.cos(ang)[:, None, :], jnp.sin(ang)[:, None, :]


def apply_rotary(x, cos, sin):
    xf = x.astype(jnp.float32)
    x1, x2 = jnp.split(xf, 2, axis=-1)
    return jnp.concatenate([x1 * cos - x2 * sin, x2 * cos + x1 * sin], axis=-1).astype(x.dtype)


def dilated_window_attention(q, k, v, window, dilation):
    B, S, H, Dh = q.shape
    n_keys = window // dilation
    L = S // dilation
    n_blk = -(-L // BLOCK)
    Lp = n_blk * BLOCK

    def by_residue(a):
        return a.reshape(B, L, dilation, H, Dh).transpose(0, 2, 1, 3, 4)

    qb = jnp.pad(by_residue(q), ((0, 0), (0, 0), (0, Lp - L), (0, 0), (0, 0)))
    qb = qb.reshape(B, dilation, n_blk, BLOCK, H, Dh)
    pad_k = ((0, 0), (0, 0), (BLOCK, Lp - L), (0, 0), (0, 0))
    kp = jnp.pad(by_residue(k), pad_k)
    vp = jnp.pad(by_residue(v), pad_k)

    def band(a):
        prev = a[:, :, :Lp].reshape(B, dilation, n_blk, BLOCK, H, Dh)
        cur = a[:, :, BLOCK:].reshape(B, dilation, n_blk, BLOCK, H, Dh)
        return jnp.concatenate([prev, cur], axis=3)

    kb, vb = band(kp), band(vp)
    s = jnp.einsum('brnqhd,brnkhd->brnhqk', qb, kb,
                   preferred_element_type=jnp.float32) * (Dh ** -0.5)
    iq = jnp.arange(BLOCK)[:, None]
    ik = jnp.arange(2 * BLOCK)[None, :]
    dist = iq + BLOCK - ik
    k_pos = jnp.arange(n_blk)[:, None, None] * BLOCK - BLOCK + ik[None]
    mask = (dist >= 0) & (dist <= n_keys) & (k_pos >= 0)
    s = jnp.where(mask[None, None, :, None], s, -jnp.inf)
    m = jnp.max(s, axis=-1, keepdims=True)
    p = jnp.exp(s - m)
    den = jnp.sum(p, axis=-1, keepdims=True)
    o = jnp.einsum('brnhqk,brnkhd->brnqhd', p, vb.astype(jnp.float32))
    o = o / den.transpose(0, 1, 2, 4, 3, 5)
    lse = (m + jnp.log(den))[..., 0].transpose(0, 1, 2, 4, 3)
    o = o.reshape(B, dilation, Lp, H, Dh)[:, :, :L].transpose(0, 2, 1, 3, 4).reshape(B, S, H, Dh)
    lse = lse.reshape(B, dilation, Lp, H)[:, :, :L].transpose(0, 2, 1, 3).reshape(B, S, H)
    return o, lse


def hybrid_mixer(x, w_in, conv_w, conv_b, conv_ln_g, conv_ln_b, w_attn_out, w_conv_out, w_o):
    B, S, _ = x.shape
    h = x @ w_in
    qkv = h[..., :QKV_COLS].reshape(B, S, N_GROUPS, 3, N_HEADS, HEAD_DIM)
    conv_in = h[..., QKV_COLS:QKV_COLS + 2 * CONV_CHANNELS]
    gates = jax.nn.sigmoid(h[..., QKV_COLS + 2 * CONV_CHANNELS:].astype(jnp.float32)).astype(x.dtype)
    g_attn, g_conv = jnp.split(gates, 2, axis=-1)

    cos, sin = rotary_tables(S)
    outs, lses = [], []
    for g, (window, dilation) in enumerate(DILATED_GROUPS):
        q = apply_rotary(qkv[:, :, g, 0], cos, sin)
        k = apply_rotary(qkv[:, :, g, 1], cos, sin)
        o, lse = dilated_window_attention(q, k, qkv[:, :, g, 2], window, dilation)
        outs.append(o)
        lses.append(lse)
    wts = jax.nn.softmax(jnp.stack(lses, axis=0), axis=0)
    o = jnp.sum(wts[..., None] * jnp.stack(outs, axis=0), axis=0).astype(x.dtype)
    y_attn = o.reshape(B, S, ATTN_WIDTH) @ w_attn_out

    a, b = jnp.split(conv_in, 2, axis=-1)
    u = a * jax.nn.sigmoid(b)
    u = lax.conv_general_dilated(u, conv_w[:, None, :].astype(u.dtype), window_strides=(1,),
                                 padding=[(CONV_WIDTH - 1, 0)],
                                 dimension_numbers=('NWC', 'WIO', 'NWC'),
                                 feature_group_count=CONV_CHANNELS) + conv_b
    u = jax.nn.silu(layer_norm(u, conv_ln_g, conv_ln_b))
    y_conv = u @ w_conv_out

    merged = g_attn * y_attn + g_conv * y_conv
    return merged @ w_o


def swiglu_ffn(x, w_gate, w_up, w_down):
    return (jax.nn.silu(x @ w_gate) * (x @ w_up)) @ w_down


def setup_inputs(seed: int = 0) -> dict:
    key = jax.random.key(seed)
    ks = jax.random.split(key, 17)
    f32 = jnp.float32
    beta = (8 * DEPTH) ** -0.25
    nrm = lambda k, shape, scale: jax.random.normal(k, shape, f32) * scale
    return {
        "x": jax.random.normal(ks[0], (BATCH, SEQ, D_MODEL), f32),
        "w_in": nrm(ks[1], (DEPTH, D_MODEL, IN_COLS), D_MODEL ** -0.5),
        "conv_w": nrm(ks[2], (DEPTH, CONV_WIDTH, CONV_CHANNELS), CONV_WIDTH ** -0.5),
        "conv_b": nrm(ks[3], (DEPTH, CONV_CHANNELS), 0.02),
        "conv_ln_g": 1.0 + nrm(ks[4], (DEPTH, CONV_CHANNELS), 0.05),
        "conv_ln_b": nrm(ks[5], (DEPTH, CONV_CHANNELS), 0.02),
        "w_attn_out": nrm(ks[6], (DEPTH, ATTN_WIDTH, D_MODEL), ATTN_WIDTH ** -0.5),
        "w_conv_out": nrm(ks[7], (DEPTH, CONV_CHANNELS, D_MODEL), CONV_CHANNELS ** -0.5),
        "w_o": nrm(ks[8], (DEPTH, D_MODEL, D_MODEL), beta * D_MODEL ** -0.5),
        "ln1_g": 1.0 + nrm(ks[9], (DEPTH, D_MODEL), 0.05),
        "ln1_b": nrm(ks[10], (DEPTH, D_MODEL), 0.02),
        "w_ffn_gate": nrm(ks[11], (DEPTH, D_MODEL, D_FF), D_MODEL ** -0.5),
        "w_ffn_up": nrm(ks[12], (DEPTH, D_MODEL, D_FF), D_MODEL ** -0.5),
        "w_ffn_down": nrm(ks[13], (DEPTH, D_FF, D_MODEL), beta * D_FF ** -0.5),
        "ln2_g": 1.0 + nrm(ks[14], (DEPTH, D_MODEL), 0.05),
        "ln2_b": nrm(ks[15], (DEPTH, D_MODEL), 0.02),
    }


def reference(x, w_in, conv_w, conv_b, conv_ln_g, conv_ln_b, w_attn_out, w_conv_out, w_o,
              ln1_g, ln1_b, w_ffn_gate, w_ffn_up, w_ffn_down, ln2_g, ln2_b):
    alpha = (2 * DEPTH) ** 0.25
    for l in range(DEPTH):
        mix = hybrid_mixer(x, w_in[l], conv_w[l], conv_b[l], conv_ln_g[l], conv_ln_b[l],
                           w_attn_out[l], w_conv_out[l], w_o[l])
        x = layer_norm(alpha * x + mix, ln1_g[l], ln1_b[l])
        ff = swiglu_ffn(x, w_ffn_gate[l], w_ffn_up[l], w_ffn_down[l])
        x = layer_norm(alpha * x + ff, ln2_g[l], ln2_b[l])
    return x
```

```python
import numpy as np
from contextlib import ExitStack
import concourse.bass as bass
import concourse.mybir as mybir
from concourse.bass_utils import run_bass_kernel_spmd

F32 = mybir.dt.float32
BF16 = mybir.dt.bfloat16
AF = mybir.ActivationFunctionType
ALU = mybir.AluOpType

D = 1024
SEQ = 8192
NBATCH = 4
DFF = 2816
NF = DFF // 128
ALPHA = float(2.0 ** 0.25)
EPS = 1e-5
GROUP_D = (1, 4, 16)
TOK = 4096
PASS = 2048
HALO = 2048
KB = 1024
ARENA_BYTES = 201 * KB


class Buf:
    __slots__ = ("name", "w", "r")

    def __init__(self, name):
        self.name = name
        self.w = None
        self.r = []


class DmaSem:
    __slots__ = ("name", "issued", "handle", "last")

    def __init__(self, name):
        self.name = name
        self.issued = 0
        self.handle = None
        self.last = None


class Op:
    __slots__ = ("eng", "fn", "idx", "waits", "signal", "dma", "vc")


COMPUTE = ("pe", "act", "dve", "pool")
ALL_ENG = ("pe", "act", "dve", "pool", "sp")


class Prog:
    def __init__(self):
        self.ops = {e: [] for e in ALL_ENG}
        self.know = {e: {} for e in ALL_ENG}
        self.pending = {e: [] for e in ALL_ENG}
        self.dma_sems = {}

    def dsem(self, name):
        s = self.dma_sems.get(name)
        if s is None:
            s = DmaSem(name)
            self.dma_sems[name] = s
        return s

    def _dep(self, Y, deps):
        if Y.dma is not None:
            key = ("dma", Y.dma)
            idx = Y.dma.issued
            Y = Y.dma.last
        else:
            key = Y.eng
            idx = Y.idx
        cur = deps.get(key)
        if cur is None or cur[0] < idx:
            deps[key] = (idx, Y)

    def barrier(self):
        lst = []
        for e in COMPUTE:
            for o in reversed(self.ops[e]):
                if o.dma is None:
                    lst.append(o)
                    break
        for s in self.dma_sems.values():
            if s.last is not None:
                lst.append(s.last)
        for e in ALL_ENG:
            self.pending[e] = list(lst)

    def op(self, eng, fn, reads=(), writes=(), dma=None):
        X = Op()
        X.eng = eng
        X.fn = fn
        X.idx = len(self.ops[eng])
        X.signal = False
        X.dma = dma
        deps = {}
        if self.pending[eng]:
            for Y in self.pending[eng]:
                if not (Y.dma is None and Y.eng == eng and dma is None):
                    self._dep(Y, deps)
            self.pending[eng] = []
        for b in reads:
            Y = b.w
            if Y is not None:
                if (Y.dma is None and dma is None and Y.eng == eng and eng in ("dve", "act")
                        and X.idx - Y.idx >= 2):
                    continue
                self._dep(Y, deps)
        for b in writes:
            Y = b.w
            if Y is not None:
                if not (Y.dma is None and dma is None and Y.eng == eng):
                    self._dep(Y, deps)
            for Y in b.r:
                if not (Y.dma is None and dma is None and Y.eng == eng):
                    self._dep(Y, deps)
        know = self.know[eng]
        waits = []
        for key, (idx, Y) in deps.items():
            if know.get(key, -1) >= idx:
                continue
            waits.append((key, idx))
            if not isinstance(key, tuple):
                self.ops[key][idx].signal = True
            for k2, v2 in Y.vc.items():
                if know.get(k2, -1) < v2:
                    know[k2] = v2
            if know.get(key, -1) < idx:
                know[key] = idx
        X.waits = waits
        vc = dict(know)
        if dma is not None:
            dma.issued += 1
            dma.last = X
            vc[("dma", dma)] = dma.issued
        else:
            vc[eng] = X.idx
        X.vc = vc
        self.ops[eng].append(X)
        for b in reads:
            b.r.append(X)
        for b in writes:
            b.w = X
            b.r = []
        return X

    def emit(self, nc, st, final_wait_sems=()):
        esem = {}
        for e in COMPUTE:
            esem[e] = st.enter_context(nc.semaphore("sem_" + e))
        for s in self.dma_sems.values():
            s.handle = st.enter_context(nc.semaphore("d_" + s.name))
        sigcnt = {}
        for e in COMPUTE:
            c = 0
            lst = []
            for o in self.ops[e]:
                if o.signal:
                    c += 1
                lst.append(c)
            sigcnt[e] = lst
        block = st.enter_context(nc.Block())

        def run(engname):
            def body(engine):
                for o in self.ops[engname]:
                    for key, idx in o.waits:
                        if isinstance(key, tuple):
                            engine.wait_ge(key[1].handle, 16 * idx)
                        else:
                            engine.wait_ge(esem[key], sigcnt[key][idx])
                    ins = o.fn(engine)
                    if o.dma is not None:
                        ins.then_inc(o.dma.handle, 16)
                    elif o.signal:
                        ins.then_inc(esem[engname], 1)
                if engname == "sp":
                    for s in final_wait_sems:
                        engine.wait_ge(s.handle, 16 * s.issued)
            return body

        block.tensor(run("pe"))
        block.scalar(run("act"))
        block.vector(run("dve"))
        block.gpsimd(run("pool"))
        block.sync(run("sp"))


def build_program(debug=None, phases=("A", "B", "C"), npass=2, a_iters=None):
    nc = bass.Bass("TRN2", target_bir_lowering=False)
    P = Prog()

    def dt_in(name, shape):
        return nc.dram_tensor(name, list(shape), F32, kind="ExternalInput").ap()

    xin = dt_in("xin", [HALO + TOK, D])
    wqk_d = dt_in("wqk", [3, 2, 128, 8 * 4 * 128])
    wv_d = dt_in("wv", [3, 2, 128, 8 * 256])
    wcv_d = dt_in("wcv", [128, 8 * 1024])
    wgate_d = dt_in("wgate", [128, 8 * 8 * 2 * 128])
    wao_d = dt_in("wao", [128, 4 * 1024])
    wco_d = dt_in("wco", [128, 4 * 1024])
    wo_d = dt_in("wo", [128, 8 * 1024])
    wg_d = dt_in("wg", [NF // 2, 128, 8 * 256])
    wu_d = dt_in("wu", [NF // 2, 128, 8 * 256])
    wd_d = dt_in("wd", [128, NF, 1024])
    cw_d = dt_in("cw", [128, 4 * 31])
    cv_d = dt_in("cv", [128, 12])
    lnv_d = dt_in("lnv", [4, 1024])
    rotn_d = dt_in("rotn", [2, 2, 128, HALO + PASS])
    mask_d = dt_in("mask4", [128, 4 * 256])
    mfirst_d = dt_in("mfirst4", [2, 128, 4 * 128])
    ident_d = dt_in("ident", [128, 128])
    y_d = nc.dram_tensor("y", [TOK, D], F32, kind="ExternalOutput").ap()
    x1d = nc.dram_tensor("x1d", [TOK, D], F32, kind="Internal").ap()
    dbg_d = {}
    if debug:
        for name, shape in debug.items():
            dtp = F32
            if isinstance(shape, tuple):
                shape, dtp = shape
            dbg_d[name] = nc.dram_tensor("dbg_" + name, list(shape), dtp, kind="ExternalOutput").ap()

    st = ExitStack()
    with st:
        arena = st.enter_context(nc.sbuf_tensor("arena", [128, ARENA_BYTES // 2], BF16))
        cst = st.enter_context(nc.sbuf_tensor("cst", [128, 2800], BF16))
        banks = [st.enter_context(nc.psum_tensor("bank%d" % i, [128, 512], F32)) for i in range(8)]
        BK = [Buf("bank%d" % i) for i in range(8)]

        def view(off_bytes, shape, dt, base=None):
            base = arena if base is None else base
            n = int(np.prod(shape))
            esz = 4 if dt == F32 else 2
            a = base[:, off_bytes // 2: off_bytes // 2 + n * esz // 2]
            if dt == F32:
                a = a.bitcast(F32)
            if len(shape) == 2:
                return a.rearrange("p (a b) -> p a b", a=shape[0])
            if len(shape) == 3:
                return a.rearrange("p (a b c) -> p a b c", a=shape[0], b=shape[1])
            return a

        identb = view(0, [128], BF16, cst)
        identf = view(256, [128], F32, cst)
        onesf = view(768, [128], F32, cst)
        onesb = view(1280, [64], BF16, cst)
        mask4 = view(1408, [4, 256], BF16, cst)
        mfirst4 = view(3456, [4, 128], BF16, cst)
        cw = view(4480, [4, 31], F32, cst)
        cv = view(4976, [3, 4], F32, cst)
        epsb = view(5024, [1], F32, cst)
        mv = view(5040, [16], F32, cst)
        bnst2 = [view(5104 + 48 * i, [12], F32, cst) for i in range(2)]
        Bc = {k: Buf(k) for k in ["identb", "identf", "ones", "mask4", "mfirst4", "cwv", "mv0", "mv1", "bnst0", "bnst1"]}

        XH = view(0, [8, 2048], BF16)
        XO = view(32 * KB, [8, 2048], BF16)
        OT = view(64 * KB, [4, 2048], BF16)
        B_XH, B_XO, B_OT = Buf("XH"), Buf("XO"), Buf("OT")
        X1DB = [Buf("x1d%d" % i) for i in range(TOK // 128)]

        def dma_cast(out, in_, writes, sem, reads=()):
            P.op("pool", lambda e: e.dma_start(out=out, in_=in_), reads=reads, writes=writes, dma=P.dsem(sem))

        def dma_sp(out, in_, writes, sem, reads=()):
            P.op("sp", lambda e: e.dma_start(out=out, in_=in_), reads=reads, writes=writes, dma=P.dsem(sem))

        def dump(name, ap, bufs):
            if debug and name in dbg_d:
                P.op("sp", lambda e: e.dma_start(out=dbg_d[name], in_=ap), reads=bufs, dma=P.dsem("dbg"))

        def mm(out, lhsT, rhs, start, stop, reads, bank, **kw):
            P.op("pe", lambda e: e.matmul(out, lhsT=lhsT, rhs=rhs, start=start, stop=stop, **kw), reads=reads, writes=[BK[bank]])

        dma_cast(identb, ident_d[:, :], [Bc["identb"]], "c0")
        dma_sp(identf, ident_d[:, :], [Bc["identf"]], "c1")
        dma_cast(mask4.rearrange("p a b -> p (a b)"), mask_d[:, :], [Bc["mask4"]], "c0")
        dma_sp(cw.rearrange("p a b -> p (a b)"), cw_d[:, :], [Bc["cwv"]], "c1")
        dma_sp(cv.rearrange("p a b -> p (a b)"), cv_d[:, :], [Bc["cwv"]], "c1")
        P.op("pool", lambda e: e.memset(onesf, 1.0), writes=[Bc["ones"]])
        P.op("pool", lambda e: e.memset(onesb, 1.0), writes=[Bc["ones"]])
        P.op("pool", lambda e: e.memset(epsb, EPS), writes=[Bc["ones"]])

        def layer_norm_tile(Z, Zb, OUT, OUTb, G, Bt, GBb, par=0):
            m_ = mv[:, 4 * par:4 * par + 4]
            bs_ = bnst2[par]
            bm, bb = Bc["mv%d" % par], Bc["bnst%d" % par]
            for h in range(2):
                P.op("dve", lambda e, h=h: e.bn_stats(out=bs_[:, 6 * h:6 * h + 6], in_=Z[:, 512 * h:512 * h + 512]),
                     reads=[Zb], writes=[bb])
            P.op("dve", lambda e: e.bn_aggr(out=m_[:, 0:2], in_=bs_.rearrange("p (a b) -> p a b", b=6)),
                 reads=[bb], writes=[bm])
            P.op("act", lambda e: e.activation(out=m_[:, 2:3], in_=m_[:, 1:2], func=AF.Sqrt, bias=epsb[:, 0:1], scale=1.0),
                 reads=[bm, Bc["ones"]], writes=[bm])
            P.op("dve", lambda e: e.reciprocal(out=m_[:, 2:3], in_=m_[:, 2:3]), reads=[bm], writes=[bm])
            P.op("dve", lambda e: e.scalar_tensor_tensor(out=m_[:, 3:4], in0=m_[:, 0:1], scalar=-1.0, in1=m_[:, 2:3],
                                                         op0=ALU.mult, op1=ALU.mult), reads=[bm], writes=[bm])
            P.op("act", lambda e: e.activation(out=Z, in_=Z, func=AF.Identity, bias=m_[:, 3:4], scale=m_[:, 2:3]),
                 reads=[bm, Zb], writes=[Zb])
            P.op("pool", lambda e: e.tensor_tensor(out=OUT, in0=Z, in1=G, op=ALU.mult), reads=[Zb, GBb], writes=[OUTb])
            P.op("pool", lambda e: e.tensor_tensor(out=OUT, in0=OUT, in1=Bt, op=ALU.add), reads=[OUTb, GBb], writes=[OUTb])

        for ps_i in range(npass):
            row0 = ps_i * PASS
            if "A" in phases or "B" in phases:
                P.barrier()
                XS = [view(188 * KB + 2048 * i, [1024], BF16) for i in range(2)]
                XSb = [Buf("XS0"), Buf("XS1")]
                if ps_i > 0:
                    P.op("dve", lambda e: e.tensor_copy(out=XH, in_=XO), reads=[B_XO], writes=[B_XH])
                first_tile = 16 if ps_i > 0 else 0
                for ti in range(first_tile, 32):
                    bsel = ti % 2
                    r0 = row0 + ti * 128
                    dma_cast(XS[bsel], xin[r0:r0 + 128, :], [XSb[bsel]], "xs%d" % bsel)
                    pb = banks[bsel][:, :].bitcast(BF16)
                    for k in range(8):
                        P.op("pe", lambda e, k=k, pb=pb, bsel=bsel: e.transpose(pb[:, k * 128:(k + 1) * 128], XS[bsel][:, k * 128:(k + 1) * 128], identb),
                             reads=[XSb[bsel], Bc["identb"]], writes=[BK[bsel]])
                    if ti < 16:
                        dst, dstb, c0 = XH, B_XH, ti * 128
                    else:
                        dst, dstb, c0 = XO, B_XO, (ti - 16) * 128
                    if ti % 2 == 0:
                        P.op("act", lambda e, dst=dst, c0=c0, pb=pb: e.copy(out=dst[:, :, c0:c0 + 128], in_=pb.rearrange("p (k c) -> p k c", k=8)),
                             writes=[BK[bsel], dstb])
                    else:
                        P.op("dve", lambda e, dst=dst, c0=c0, pb=pb: e.tensor_copy(out=dst[:, :, c0:c0 + 128], in_=pb.rearrange("p (k c) -> p k c", k=8)),
                             writes=[BK[bsel], dstb])

            if "A" in phases:
                ACC = view(80 * KB, [4, 2048], F32)
                QA = view(112 * KB, [2048], BF16)
                QB = view(116 * KB, [2048], BF16)
                KA = view(120 * KB, [4096], BF16)
                KBt = view(128 * KB, [4096], BF16)
                V = view(136 * KB, [32, 256], BF16)
                WQK = view(152 * KB, [8, 4, 128], BF16)
                WV = view(160 * KB, [8, 256], BF16)
                TAB = [view(164 * KB + 4096 * i, [2, 512], F32) for i in range(2)]
                RT = [view(172 * KB + 2048 * i, [512], F32) for i in range(6)]
                RT2 = [view(192 * KB + 2048 * i, [512], F32) for i in range(2)]
                PT = [view(184 * KB + 2048 * i, [4, 256], BF16) for i in range(2)]
                b_ACC, b_Q, b_K, b_V, b_WQK, b_WV = Buf("ACC"), Buf("Q"), Buf("K"), Buf("V"), Buf("WQK"), Buf("WV")
                b_TAB = [Buf("TAB0"), Buf("TAB1")]
                b_RT = [Buf("RT%d" % i) for i in range(6)]
                b_RT2 = [Buf("RT2_%d" % i) for i in range(2)]
                b_PT = [Buf("PT0"), Buf("PT1")]
                b_ST = Buf("ST")
                dma_cast(mfirst4.rearrange("p a b -> p (a b)"), mfirst_d[ps_i, :, :], [Bc["mfirst4"]], "c0")
                tabn = [0]
                iters = [(hq, g) for hq in range(2) for g in range(3)]
                if a_iters is not None:
                    iters = [it for it in iters if it in a_iters]

                def load_w(hq_, g_):
                    dma_cast(WQK.rearrange("p a b c -> p (a b c)"), wqk_d[g_, hq_, :, :], [b_WQK], "wqk")
                    dma_cast(WV.rearrange("p a b -> p (a b)"), wv_d[g_, hq_, :, :], [b_WV], "wv")

                load_w(*iters[0])
                for it_i, (hq, g) in enumerate(iters):
                    first_g = (g == min(gg for (h2, gg) in iters if h2 == hq))
                    last_g = (g == max(gg for (h2, gg) in iters if h2 == hq))
                    d = GROUP_D[g]
                    L = PASS // d
                    nb = L // 128
                    nh = 128 * d
                    XHv = XH.rearrange("p k (m d) -> p k d m", d=d)
                    XOv = XO.rearrange("p k (m d) -> p k d m", d=d)
                    uh = HALO - nh
                    tiles = []
                    if d == 1:
                        tiles.append((uh, 128, False))
                    else:
                        for u0 in range(uh, HALO, 512):
                            tiles.append((u0, 512, False))
                    for u0 in range(HALO, HALO + PASS, 512):
                        tiles.append((u0, 512, True))
                    KAh = KA[:, 0:nh].rearrange("p (r i) -> p i r", r=d)
                    KBh = KBt[:, 0:nh].rearrange("p (r i) -> p i r", r=d)
                    KAo = KA[:, nh:nh + PASS].rearrange("p (r m) -> p m r", r=d)
                    KBo = KBt[:, nh:nh + PASS].rearrange("p (r m) -> p m r", r=d)
                    QAo = QA.rearrange("p (r m) -> p m r", r=d)
                    QBo = QB.rearrange("p (r m) -> p m r", r=d)

                    pj = 0
                    for (u0, n, own) in tiles:
                        tsel = tabn[0] % 2
                        tabn[0] += 1
                        dma_sp(TAB[tsel][:, :, 0:n], rotn_d[ps_i, :, :, u0:u0 + n].rearrange("c p n -> p c n"),
                               [b_TAB[tsel]], "tab%d" % tsel)
                        C = TAB[tsel][:, 0, 0:n]
                        S = TAB[tsel][:, 1, 0:n]
                        nm = n // d
                        if own:
                            m0 = (u0 - HALO) // d
                        else:
                            m0 = (u0 - uh) // d
                        for which in ([1, 0] if own else [1]):
                            bk0 = 2 * (pj % 2)
                            pj += 1
                            for half in range(2):
                                tcol = which * 2 + half
                                for k in range(8):
                                    rhs = XO[:, k, u0 - HALO:u0 - HALO + n] if own else XH[:, k, u0:u0 + n]
                                    mm(banks[bk0 + half][:, 0:n], WQK[:, k, tcol, :], rhs, k == 0, k == 7, [b_WQK, B_XH, B_XO], bk0 + half)
                            a_sb, b_sb, t1, t2, t3, t4 = [r_[:, 0:n] for r_ in RT]
                            bA, bB = b_RT[0], b_RT[1]
                            if pj % 2 == 0:
                                a_sb, b_sb = RT2[0][:, 0:n], RT2[1][:, 0:n]
                                bA, bB = b_RT2[0], b_RT2[1]
                            P.op("act", lambda e, a_sb=a_sb, n=n, bk0=bk0: e.copy(out=a_sb, in_=banks[bk0][:, 0:n]), writes=[BK[bk0], bA])
                            P.op("act", lambda e, b_sb=b_sb, n=n, bk0=bk0: e.copy(out=b_sb, in_=banks[bk0 + 1][:, 0:n]), writes=[BK[bk0 + 1], bB])
                            if which == 1 and own:
                                dA, dB, db = KAo[:, m0:m0 + nm, :], KBo[:, m0:m0 + nm, :], b_K
                            elif which == 1:
                                dA, dB, db = KAh[:, m0:m0 + nm, :], KBh[:, m0:m0 + nm, :], b_K
                            else:
                                dA, dB, db = QAo[:, m0:m0 + nm, :], QBo[:, m0:m0 + nm, :], b_Q
                            v3 = lambda ap: ap.rearrange("p (m r) -> p m r", r=d)
                            P.op("dve", lambda e, t1=t1, a_sb=a_sb, C=C: e.tensor_tensor(out=t1, in0=a_sb, in1=C, op=ALU.mult),
                                 reads=[bA, b_TAB[tsel]], writes=[b_RT[2]])
                            P.op("dve", lambda e, t2=t2, b_sb=b_sb, S=S: e.tensor_tensor(out=t2, in0=b_sb, in1=S, op=ALU.mult),
                                 reads=[bB, b_TAB[tsel]], writes=[b_RT[3]])
                            P.op("dve", lambda e, t1=v3(t1), t2=v3(t2), dA=dA: e.tensor_tensor(out=dA, in0=t1, in1=t2, op=ALU.subtract),
                                 reads=[b_RT[2], b_RT[3]], writes=[db])
                            P.op("dve", lambda e, t3=t3, b_sb=b_sb, C=C: e.tensor_tensor(out=t3, in0=b_sb, in1=C, op=ALU.mult),
                                 reads=[bB, b_TAB[tsel]], writes=[b_RT[4]])
                            P.op("pool", lambda e, t4=t4, a_sb=a_sb, S=S: e.tensor_tensor(out=t4, in0=a_sb, in1=S, op=ALU.mult),
                                 reads=[bA, b_TAB[tsel]], writes=[b_RT[5]])
                            P.op("pool", lambda e, t3=v3(t3), t4=v3(t4), dB=dB: e.tensor_tensor(out=dB, in0=t3, in1=t4, op=ALU.add),
                                 reads=[b_RT[4], b_RT[5]], writes=[db])
                    nblk = d + d * nb
                    for bi in range(nblk):
                        if bi < d:
                            r = bi
                            if d == 1:
                                lf = lambda k: XH[:, k, 1920:2048]
                            elif d == 4:
                                lf = lambda k, r=r, XHv=XHv: XHv[:, k, r, 384:512]
                            else:
                                lf = lambda k, r=r, XHv=XHv: XHv[:, k, r, 0:128]
                        else:
                            r, kb = divmod(bi - d, nb)
                            lf = lambda k, r=r, kb=kb, XOv=XOv: XOv[:, k, r, kb * 128:(kb + 1) * 128]
                        bsel = bi % 2
                        for k in range(8):
                            mm(banks[bsel][:, 0:256], lf(k), WV[:, k, :], k == 0, k == 7, [b_WV, B_XH, B_XO], bsel)
                        P.op("act", lambda e, bi=bi, bsel=bsel: e.copy(out=V[:, bi, :], in_=banks[bsel][:, 0:256]), writes=[BK[bsel], b_V])
                    if it_i + 1 < len(iters):
                        load_w(*iters[it_i + 1])
                    if debug and hq == 0 and ps_i == 0:
                        dump("ka%d" % g, KA[:, 0:nh + PASS], [b_K])
                        dump("qa%d" % g, QA, [b_Q])
                        dump("v%d" % g, V.rearrange("p a b -> p (a b)"), [b_V])
                    ACCv = ACC.rearrange("p a (m d) -> p a d m", d=d)
                    def att_front(r, kb, par):
                        has_cur = kb >= 0
                        has_prev = kb + 1 < nb
                        if has_cur and has_prev:
                            c_lo, c_hi = 0, 256
                            q0 = r * L + kb * 128
                        elif has_cur:
                            c_lo, c_hi = 0, 128
                            q0 = r * L + kb * 128
                        else:
                            c_lo, c_hi = 128, 256
                            q0 = r * L
                        nq = c_hi - c_lo
                        kc = r * 128 if kb < 0 else nh + r * L + kb * 128
                        pt = PT[par]
                        for hl in range(4):
                            sb_ = banks[4 + hl][:, c_lo:c_hi]
                            kw = {"tile_position": (96, 0)} if hl == 3 else {}
                            for half, (Kt, Qt) in enumerate(((KA, QA), (KBt, QB))):
                                P.op("pe", lambda e, sb_=sb_, Kt=Kt, Qt=Qt, hl=hl, half=half, kw=kw:
                                     e.matmul(sb_, lhsT=Kt[32 * hl:32 * hl + 32, kc:kc + 128], rhs=Qt[32 * hl:32 * hl + 32, q0:q0 + nq],
                                              start=(half == 0), stop=(half == 1), **kw),
                                     reads=[b_K, b_Q], writes=[BK[4 + hl], b_ST])
                        for hl in range(4):
                            sb_ = banks[4 + hl][:, c_lo:c_hi]
                            P.op("act", lambda e, sb_=sb_, pt=pt, hl=hl, c_lo=c_lo, c_hi=c_hi:
                                 e.activation(out=pt[:, hl, c_lo:c_hi], in_=sb_, func=AF.Exp, scale=0.125),
                                 reads=[b_ST], writes=[BK[4 + hl], b_PT[par]])
                        if kb < 0:
                            mk, mkb = mfirst4[:, :, :], Bc["mfirst4"]
                        else:
                            mk, mkb = mask4[:, :, c_lo:c_hi], Bc["mask4"]
                        P.op("dve", lambda e, pt=pt, mk=mk, c_lo=c_lo, c_hi=c_hi:
                             e.tensor_tensor(out=pt[:, :, c_lo:c_hi], in0=pt[:, :, c_lo:c_hi], in1=mk, op=ALU.mult),
                             reads=[mkb], writes=[b_PT[par]])

                    def att_back(r, kb, par):
                        has_cur = kb >= 0
                        has_prev = kb + 1 < nb
                        vb = r if kb < 0 else d + r * nb + kb
                        pt = PT[par]
                        parts = []
                        if has_cur:
                            parts.append((kb, 0, False))
                        if has_prev:
                            parts.append((kb + 1, 128, True))
                        for (qb, pc, is_first) in parts:
                            nbk = 2 + (qb % 2)
                            for hl in range(4):
                                pair, hh = divmod(hl, 2)
                                mm(banks[nbk][64 * hh:64 * hh + 64, pair * 128:(pair + 1) * 128],
                                   V[:, vb, hl * 64:(hl + 1) * 64], pt[:, hl, pc:pc + 128],
                                   is_first and pair == 0, False, [b_V, b_PT[par]], nbk, skip_group_check=True)
                                mm(banks[nbk][64 * hh:64 * hh + 64, (2 + pair) * 128:(3 + pair) * 128],
                                   onesb, pt[:, hl, pc:pc + 128],
                                   False, (not is_first) and hl == 3, [Bc["ones"], b_PT[par]], nbk, skip_group_check=True)
                            if not is_first:
                                dst = ACCv[:, :, r, qb * 128:(qb + 1) * 128]
                                src = banks[nbk][:, :].rearrange("p (a b) -> p a b", a=4)
                                if first_g:
                                    P.op("act", lambda e, dst=dst, src=src: e.copy(out=dst, in_=src), writes=[BK[nbk], b_ACC])
                                else:
                                    P.op("dve", lambda e, dst=dst, src=src: e.tensor_tensor(out=dst, in0=dst, in1=src, op=ALU.add),
                                         writes=[BK[nbk], b_ACC])

                    seq = [(r, kb) for r in range(d) for kb in range(-1, nb)]
                    att_front(seq[0][0], seq[0][1], 0)
                    for ii, (r, kb) in enumerate(seq):
                        if ii + 1 < len(seq):
                            att_front(seq[ii + 1][0], seq[ii + 1][1], (ii + 1) % 2)
                        att_back(r, kb, ii % 2)
                    if debug and hq == 0 and ps_i == 0:
                        dump("acc%d" % g, ACC.rearrange("p a b -> p (a b)"), [b_ACC])
                    if last_g:
                        P.op("dve", lambda e: e.reciprocal(out=ACC[:, 2:4, :], in_=ACC[:, 2:4, :]), writes=[b_ACC])
                        P.op("dve", lambda e, hq=hq: e.tensor_tensor(out=OT[:, 2 * hq:2 * hq + 2, :], in0=ACC[:, 0:2, :], in1=ACC[:, 2:4, :], op=ALU.mult),
                             reads=[b_ACC], writes=[B_OT])

                if debug and ps_i == 0:
                    dump("ot", OT.rearrange("p a b -> p (a b)"), [B_OT])

            if "B" in phases:
                P.barrier()
                WCV = view(80 * KB, [8, 1024], BF16)
                WAO = view(96 * KB, [4, 1024], BF16)
                WCO = view(104 * KB, [4, 1024], BF16)
                WO = view(112 * KB, [8, 1024], BF16)
                U = view(128 * KB, [4, 544], BF16)
                HK = KB // 2
                CACC = view(265 * HK, [4, 512], F32)
                STAT = view(281 * HK, [2, 512], F32)
                UST = view(289 * HK, [4, 512], BF16)
                SIGC = view(297 * HK, [512], F32)
                SIG = [view(301 * HK + 2048 * i, [512], F32) for i in range(2)]
                MM = [view(309 * HK + 2048 * i, [512], F32) for i in range(2)]
                MRG = view(317 * HK, [8, 512], BF16)
                XT2 = [view(333 * HK + 4096 * i, [1024], F32) for i in range(2)]
                Z2 = [view(349 * HK, [1024], F32)] * 2
                X12 = [view(357 * HK, [1024], F32)] * 2
                GB = view(365 * HK, [2, 1024], F32)
                WG2 = [view(381 * HK + 4096 * i, [8, 256], BF16) for i in range(2)]
                SQ = [view(397 * HK, [512], F32)] * 2
                DIAG = view(0, [124, 128], BF16)
                b_WCV, b_WAO, b_WCO, b_WO = Buf("WCV"), Buf("WAO"), Buf("WCO"), Buf("WO")
                b_U = [Buf("U%d" % i) for i in range(4)]
                b_CACC = [Buf("CACC%d" % i) for i in range(4)]
                b_SQ1 = Buf("SQ")
                b_SQ = [b_SQ1, b_SQ1]
                b_STAT, b_UST, b_SIGC, b_DIAG = Buf("STAT"), Buf("UST"), Buf("SIGC"), Buf("DIAG")
                b_SIG = [Buf("SIG0"), Buf("SIG1")]
                b_MM = [Buf("MM0"), Buf("MM1")]
                b_WG2 = [Buf("WG0"), Buf("WG1")]
                b_MRG, b_GB = Buf("MRG"), Buf("GB")
                b_XT2 = [Buf("XT0"), Buf("XT1")]
                b_Z1, b_X11 = Buf("Z0"), Buf("X10")
                b_Z2 = [b_Z1, b_Z1]
                b_X12 = [b_X11, b_X11]
                dma_cast(WCV.rearrange("p a b -> p (a b)"), wcv_d[:, :], [b_WCV], "wcv")
                dma_cast(WAO.rearrange("p a b -> p (a b)"), wao_d[:, :], [b_WAO], "wao")
                dma_cast(WCO.rearrange("p a b -> p (a b)"), wco_d[:, :], [b_WCO], "wco")
                dma_cast(WO.rearrange("p a b -> p (a b)"), wo_d[:, :], [b_WO], "wo")
                dma_sp(GB[:, 0, :], lnv_d[0:1, :].partition_broadcast(128), [b_GB], "gb")
                dma_sp(GB[:, 1, :], lnv_d[1:2, :].partition_broadcast(128), [b_GB], "gb")

                def conv_in_cc(cc, rhsf, n, ucol0):
                    for which in range(2):
                        for k in range(8):
                            mm(banks[which][:, 0:n], WCV[:, k, which * 512 + cc * 128: which * 512 + (cc + 1) * 128], rhsf(k),
                               k == 0, k == 7, [b_WCV, B_XH, B_XO], which)
                    P.op("act", lambda e: e.activation(out=SIGC[:, 0:n], in_=banks[1][:, 0:n], func=AF.Sigmoid),
                         writes=[BK[1], b_SIGC])
                    P.op("dve", lambda e: e.tensor_tensor(out=U[:, cc, ucol0:ucol0 + n], in0=SIGC[:, 0:n], in1=banks[0][:, 0:n], op=ALU.mult),
                         reads=[b_SIGC], writes=[BK[0], b_U[cc]])

                for cc in range(4):
                    conv_in_cc(cc, lambda k: XH[:, k, 2018:2048], 30, 0)
                P.barrier()

                for cc in range(4):
                    for j in range(31):
                        P.op("dve", lambda e, cc=cc, j=j: e.tensor_scalar(out=DIAG[:, cc * 31 + j, :], in0=identb, scalar1=cw[:, cc, j:j + 1], scalar2=None,
                                                                        op0=ALU.mult),
                             reads=[Bc["identb"], Bc["cwv"]], writes=[b_DIAG])

                def stage1(s):
                    t0 = s * 512
                    if s > 0:
                        for cc in range(4):
                            eng = "dve" if cc < 2 else "pool"
                            P.op(eng, lambda e, cc=cc: e.tensor_copy(out=U[:, cc, 0:30], in_=U[:, cc, 512:542]), reads=[b_U[cc]], writes=[b_U[cc]])
                    for cc in range(4):
                        conv_in_cc(cc, lambda k, t0=t0: XO[:, k, t0:t0 + 512], 512, 30)
                        yield
                    for cc in range(4):
                        bank = cc % 2
                        for j in range(31):
                            mm(banks[bank][:, :], DIAG[:, cc * 31 + j, :], U[:, cc, j:j + 512], j == 0, j == 30, [b_DIAG, b_U[cc]], bank)
                            if j % 8 == 7:
                                yield
                        P.op("act", lambda e, cc=cc, bank=bank: e.activation(out=CACC[:, cc, :], in_=banks[bank][:, :], func=AF.Identity,
                                                                           bias=cv[:, 0, cc:cc + 1], scale=1.0),
                             reads=[Bc["cwv"]], writes=[BK[bank], b_CACC[cc]])
                        yield
                    if debug and ps_i == 0 and s == 0:
                        dump("cacc", CACC.rearrange("p a b -> p (a b)"), b_CACC)

                def stage2(s):
                    t0 = s * 512
                    bC = b_CACC
                    for cc in range(4):
                        mm(banks[2][:, :], onesf, CACC[:, cc, :], cc == 0, cc == 3, [Bc["ones"], bC[cc]], 2)
                    for cc in range(4):
                        sq = SQ[cc % 2]
                        P.op("act", lambda e, cc=cc, sq=sq: e.activation(out=sq, in_=CACC[:, cc, :], func=AF.Square), reads=[bC[cc]], writes=[b_SQ[cc % 2]])
                        mm(banks[3][:, :], onesf, sq, cc == 0, cc == 3, [Bc["ones"], b_SQ[cc % 2]], 3)
                    mean, rstd = STAT[:, 0, :], STAT[:, 1, :]
                    P.op("act", lambda e: e.mul(out=mean, in_=banks[2][:, :], mul=1.0 / 512), writes=[BK[2], b_STAT])
                    P.op("dve", lambda e: e.tensor_tensor(out=rstd, in0=mean, in1=mean, op=ALU.mult), reads=[b_STAT], writes=[b_STAT])
                    P.op("dve", lambda e: e.scalar_tensor_tensor(out=rstd, in0=banks[3][:, :], scalar=1.0 / 512, in1=rstd, op0=ALU.mult, op1=ALU.subtract),
                         reads=[b_STAT], writes=[BK[3], b_STAT])
                    P.op("act", lambda e: e.activation(out=rstd, in_=rstd, func=AF.Sqrt, bias=epsb[:, 0:1], scale=1.0), reads=[b_STAT, Bc["ones"]], writes=[b_STAT])
                    P.op("dve", lambda e: e.reciprocal(out=rstd, in_=rstd), reads=[b_STAT], writes=[b_STAT])
                    for cc in range(4):
                        P.op("dve", lambda e, cc=cc: e.tensor_tensor(out=CACC[:, cc, :], in0=CACC[:, cc, :], in1=mean, op=ALU.subtract),
                             reads=[bC[cc], b_STAT], writes=[bC[cc]])
                        P.op("dve", lambda e, cc=cc: e.tensor_tensor(out=CACC[:, cc, :], in0=CACC[:, cc, :], in1=rstd, op=ALU.mult),
                             reads=[bC[cc], b_STAT], writes=[bC[cc]])
                        P.op("act", lambda e, cc=cc: e.activation(out=UST[:, cc, :], in_=CACC[:, cc, :], func=AF.Silu, bias=cv[:, 2, cc:cc + 1], scale=cv[:, 1, cc:cc + 1]),
                             reads=[bC[cc], Bc["cwv"]], writes=[b_UST])
                    if debug and ps_i == 0 and s == 0:
                        dump("ust", UST.rearrange("p a b -> p (a b)"), [b_UST])
                    yield
                    for c in range(8):
                        wsel = c % 2
                        WG = WG2[wsel]
                        dma_cast(WG.rearrange("p a b -> p (a b)"), wgate_d[:, c * 2048:(c + 1) * 2048], [b_WG2[wsel]], "wg2%d" % wsel)
                        for which in range(2):
                            for k in range(8):
                                mm(banks[4 + which][:, :], WG[:, k, which * 128:(which + 1) * 128], XO[:, k, t0:t0 + 512],
                                   k == 0, k == 7, [b_WG2[wsel], B_XO], 4 + which)
                        yield
                        for k in range(4):
                            mm(banks[6][:, :], WAO[:, k, c * 128:(c + 1) * 128], OT[:, k, t0:t0 + 512], k == 0, k == 3, [b_WAO, B_OT], 6)
                        for k in range(4):
                            mm(banks[7][:, :], WCO[:, k, c * 128:(c + 1) * 128], UST[:, k, :], k == 0, k == 3, [b_WCO, b_UST], 7)
                        for which in range(2):
                            P.op("act", lambda e, which=which: e.activation(out=SIG[which], in_=banks[4 + which][:, :], func=AF.Sigmoid),
                                 writes=[BK[4 + which], b_SIG[which]])
                            P.op("dve", lambda e, which=which: e.tensor_tensor(out=MM[which], in0=SIG[which], in1=banks[6 + which][:, :], op=ALU.mult),
                                 reads=[b_SIG[which]], writes=[BK[6 + which], b_MM[which]])
                        P.op("pool", lambda e, c=c: e.tensor_tensor(out=MRG[:, c, :], in0=MM[0], in1=MM[1], op=ALU.add),
                             reads=[b_MM[0], b_MM[1]], writes=[b_MRG])
                        yield
                    if debug and ps_i == 0 and s == 0:
                        dump("mrg", MRG.rearrange("p a b -> p (a b)"), [b_MRG])
                    for j in range(4):
                        tok = ps_i * PASS + t0 + j * 128
                        jp = j % 2
                        XT, Z, X1 = XT2[jp], Z2[jp], X12[jp]
                        dma_sp(XT, xin[HALO + tok:HALO + tok + 128, :], [b_XT2[jp]], "xt%d" % jp)
                        for half in range(2):
                            for k in range(8):
                                mm(banks[2 + half][:, :], MRG[:, k, j * 128:(j + 1) * 128], WO[:, k, half * 512:(half + 1) * 512],
                                   k == 0, k == 7, [b_MRG, b_WO], 2 + half)
                        for half in range(2):
                            P.op("dve", lambda e, half=half, Z=Z, XT=XT: e.scalar_tensor_tensor(out=Z[:, half * 512:(half + 1) * 512], in0=XT[:, half * 512:(half + 1) * 512],
                                                                                               scalar=ALPHA, in1=banks[2 + half][:, :], op0=ALU.mult, op1=ALU.add),
                                 reads=[b_XT2[jp]], writes=[BK[2 + half], b_Z2[jp]])
                        if debug and ps_i == 0 and s == 0 and j == 0:
                            dump("z1", Z, [b_Z2[jp]])
                        layer_norm_tile(Z, b_Z2[jp], X1, b_X12[jp], GB[:, 0, :], GB[:, 1, :], b_GB, jp)
                        if debug and ps_i == 0 and s == 0 and j == 0:
                            dump("x1", X1, [b_X12[jp]])
                        dma_sp(x1d[tok:tok + 128, :], X1, [X1DB[tok // 128]], "x1s%d" % jp, reads=[b_X12[jp]])
                        yield

                for _ in stage1(0):
                    pass
                for s in range(4):
                    g2 = stage2(s)
                    g1 = stage1(s + 1) if s < 3 else iter(())
                    d1 = d2 = False
                    while not (d1 and d2):
                        if not d2:
                            try:
                                next(g2)
                            except StopIteration:
                                d2 = True
                        if not d1:
                            try:
                                next(g1)
                            except StopIteration:
                                d1 = True

        if "C" in phases:
            P.barrier()
            WG = view(0, [NF // 2, 8, 256], BF16)
            WU = view(44 * KB, [NF // 2, 8, 256], BF16)
            WD = view(88 * KB, [NF, 1024], BF16)
            X1A = [view(132 * KB + 4096 * i, [1024], F32) for i in range(2)]
            X1B = [view(140 * KB + 4096 * i, [1024], F32) for i in range(2)]
            X1T = view(148 * KB, [8, 512], BF16)
            HT = view(156 * KB, [NF, 512], BF16)
            SG = [view(178 * KB + 2048 * i, [512], F32) for i in range(2)]
            ZC = view(182 * KB, [1024], F32)
            OUTT = view(186 * KB, [1024], F32)
            GBC = view(190 * KB, [2, 1024], F32)
            b_X1A = [Buf("X1A0"), Buf("X1A1")]
            b_X1B = [Buf("X1B0"), Buf("X1B1")]
            b_X1T, b_HT = Buf("X1T"), Buf("HT")
            b_SG = [Buf("SG0"), Buf("SG1")]
            b_Z, b_OUTT, b_GB = Buf("Zc"), Buf("OUTT"), Buf("GBc")
            NGRP = 11
            b_WGU = [Buf("WGU%d" % i) for i in range(NGRP)]
            b_WD = [Buf("WD%d" % i) for i in range(NGRP)]
            for gi in range(NGRP):
                dma_cast(WG[:, gi, :, :].rearrange("p a b -> p (a b)"), wg_d[gi, :, :], [b_WGU[gi]], "wgu%d" % gi)
                dma_cast(WU[:, gi, :, :].rearrange("p a b -> p (a b)"), wu_d[gi, :, :], [b_WGU[gi]], "wgu%d" % gi)
            for gi in range(NGRP):
                dma_cast(WD[:, 2 * gi:2 * gi + 2, :], wd_d[:, 2 * gi:2 * gi + 2, :], [b_WD[gi]], "wd%d" % gi)
            dma_sp(GBC[:, 0, :], lnv_d[2:3, :].partition_broadcast(128), [b_GB], "gbc")
            dma_sp(GBC[:, 1, :], lnv_d[3:4, :].partition_broadcast(128), [b_GB], "gbc")
            n_ct = (TOK // 512) if npass == 2 else (PASS // 512)
            xa = 0
            xb = 0
            for ct in range(n_ct):
                for j in range(4):
                    ti = ct * 4 + j
                    sel = xa % 2
                    xa += 1
                    dma_sp(X1A[sel], x1d[ti * 128:(ti + 1) * 128, :], [b_X1A[sel]], "x1a%d" % sel, reads=[X1DB[ti]])
                    for half in range(2):
                        for kk in range(4):
                            k = half * 4 + kk
                            P.op("pe", lambda e, half=half, kk=kk, k=k, sel=sel: e.transpose(banks[half][:, kk * 128:(kk + 1) * 128], X1A[sel][:, k * 128:(k + 1) * 128], identf),
                                 reads=[b_X1A[sel], Bc["identf"]], writes=[BK[half]])
                    for half in range(2):
                        src = banks[half][:, :].rearrange("p (k c) -> p k c", k=4)
                        dst = X1T[:, half * 4:half * 4 + 4, j * 128:(j + 1) * 128]
                        if half == 0:
                            P.op("act", lambda e, src=src, dst=dst: e.copy(out=dst, in_=src), writes=[BK[half], b_X1T])
                        else:
                            P.op("dve", lambda e, src=src, dst=dst: e.tensor_copy(out=dst, in_=src), writes=[BK[half], b_X1T])
                for f in range(NF):
                    bg = 2 + 2 * (f % 2)
                    for which, Wt in enumerate((WG, WU)):
                        for k in range(8):
                            mm(banks[bg + which][:, :], Wt[:, f // 2, k, (f % 2) * 128:(f % 2 + 1) * 128], X1T[:, k, :], k == 0, k == 7,
                               [b_WGU[f // 2], b_X1T], bg + which)
                    sg = SG[f % 2]
                    P.op("act", lambda e, sg=sg, bg=bg: e.activation(out=sg, in_=banks[bg][:, :], func=AF.Silu), writes=[BK[bg], b_SG[f % 2]])
                    P.op("dve", lambda e, sg=sg, bg=bg, f=f: e.tensor_tensor(out=HT[:, f, :], in0=sg, in1=banks[bg + 1][:, :], op=ALU.mult),
                         reads=[b_SG[f % 2]], writes=[BK[bg + 1], b_HT])
                for j in range(4):
                    ti = ct * 4 + j
                    sel = xb % 2
                    xb += 1
                    dma_sp(X1B[sel], x1d[ti * 128:(ti + 1) * 128, :], [b_X1B[sel]], "x1b%d" % sel, reads=[X1DB[ti]])
                    for half in range(2):
                        for f in range(NF):
                            mm(banks[6 + half][:, :], HT[:, f, j * 128:(j + 1) * 128], WD[:, f, half * 512:(half + 1) * 512],
                               f == 0, f == NF - 1, [b_HT, b_WD[f // 2]], 6 + half)
                    for half in range(2):
                        P.op("dve", lambda e, half=half, sel=sel: e.scalar_tensor_tensor(out=ZC[:, half * 512:(half + 1) * 512], in0=X1B[sel][:, half * 512:(half + 1) * 512],
                                                                                        scalar=ALPHA, in1=banks[6 + half][:, :], op0=ALU.mult, op1=ALU.add),
                             reads=[b_X1B[sel]], writes=[BK[6 + half], b_Z])
                    layer_norm_tile(ZC, b_Z, OUTT, b_OUTT, GBC[:, 0, :], GBC[:, 1, :], b_GB)
                    dma_sp(y_d[ti * 128:(ti + 1) * 128, :], OUTT, [], "ys", reads=[b_OUTT])

        finals = [P.dsem(n) for n in ("ys", "x1s0", "x1s1", "dbg") if n in P.dma_sems]
        P.emit(nc, st, final_wait_sems=finals)
    return nc


def _rope_tables():
    inv_freq = (1.0 / (np.float32(10000.0) ** (np.arange(0, 64, 2, dtype=np.float32) / np.float32(64)))).astype(np.float32)
    pos = np.arange(-HALO, SEQ, dtype=np.float32)
    ang = (pos[:, None] * inv_freq[None, :]).astype(np.float32)
    return np.cos(ang).astype(np.float32), np.sin(ang).astype(np.float32)


def _pass_positions(g, p0):
    d = GROUP_D[g]
    L = PASS // d
    halo = np.array([p0 + r + d * (i - 128) for r in range(d) for i in range(128)], dtype=np.int64)
    own = np.array([p0 + r + d * m for r in range(d) for m in range(L)], dtype=np.int64)
    return np.concatenate([halo, own])


def prepare_inputs(x, w_in, conv_w, conv_b, conv_ln_g, conv_ln_b, w_attn_out, w_conv_out, w_o,
                   ln1_g, ln1_b, w_ffn_gate, w_ffn_up, w_ffn_down, ln2_g, ln2_b):
    f = lambda a: np.ascontiguousarray(np.asarray(a, dtype=np.float32))
    x = f(x)
    w_in = f(w_in)[0]

    def kmajor(w):
        K, N = w.shape
        return np.ascontiguousarray(w.reshape(K // 128, 128, N).transpose(1, 0, 2))

    wqk = np.empty((3, 2, 128, 8, 4, 128), np.float32)
    wv = np.empty((3, 2, 128, 8, 256), np.float32)
    for g in range(3):
        for hq in range(2):
            for qk in range(2):
                for half in range(2):
                    cols = np.array([g * 1536 + qk * 512 + (4 * hq + hl) * 64 + half * 32 + j for hl in range(4) for j in range(32)])
                    wqk[g, hq, :, :, qk * 2 + half, :] = kmajor(w_in[:, cols])
            c0 = g * 1536 + 1024 + 4 * hq * 64
            wv[g, hq] = kmajor(w_in[:, c0:c0 + 256])
    wcv = kmajor(w_in[:, 4608:5632])
    wgate = np.empty((128, 8, 8, 2, 128), np.float32)
    for which in range(2):
        wk = kmajor(w_in[:, 5632 + which * 1024: 5632 + (which + 1) * 1024])
        wgate[:, :, :, which, :] = wk.reshape(128, 8, 8, 128).transpose(0, 2, 1, 3)
    shared = {
        "wqk": wqk.reshape(3, 2, 128, -1), "wv": wv.reshape(3, 2, 128, -1),
        "wcv": wcv.reshape(128, -1), "wgate": wgate.reshape(128, -1),
        "wao": kmajor(f(w_attn_out)[0]).reshape(128, -1), "wco": kmajor(f(w_conv_out)[0]).reshape(128, -1),
        "wo": kmajor(f(w_o)[0]).reshape(128, -1),
        "wg": np.ascontiguousarray(kmajor(f(w_ffn_gate)[0]).reshape(128, 8, NF // 2, 256).transpose(2, 0, 1, 3)).reshape(NF // 2, 128, -1),
        "wu": np.ascontiguousarray(kmajor(f(w_ffn_up)[0]).reshape(128, 8, NF // 2, 256).transpose(2, 0, 1, 3)).reshape(NF // 2, 128, -1),
        "wd": kmajor(f(w_ffn_down)[0]),
        "cw": np.ascontiguousarray(f(conv_w)[0].T.reshape(4, 128, 31).transpose(1, 0, 2)).reshape(128, -1),
        "cv": np.ascontiguousarray(np.stack([f(conv_b)[0], f(conv_ln_g)[0], f(conv_ln_b)[0]]).reshape(3, 4, 128).transpose(2, 0, 1)).reshape(128, -1),
        "lnv": np.stack([f(ln1_g)[0], f(ln1_b)[0], f(ln2_g)[0], f(ln2_b)[0]]),
        "ident": np.eye(128, dtype=np.float32),
    }
    kk = np.arange(128)[:, None]
    qq = np.arange(128)[None, :]
    cur = (kk <= qq).astype(np.float32)
    prev = (kk >= qq).astype(np.float32)
    m2 = np.concatenate([cur, prev], axis=1)
    shared["mask4"] = np.ascontiguousarray(np.broadcast_to(m2[:, None, :], (128, 4, 256))).reshape(128, -1)
    prev4 = np.ascontiguousarray(np.broadcast_to(prev[:, None, :], (128, 4, 128))).reshape(128, -1)
    cosT, sinT = _rope_tables()
    in_maps = []
    for c in range(8):
        b, hf = divmod(c, 2)
        t0 = hf * TOK
        xin = np.zeros((HALO + TOK, D), np.float32)
        if hf == 0:
            xin[HALO:] = x[b, 0:TOK]
        else:
            xin[:] = x[b, t0 - HALO:t0 + TOK]
        m = dict(shared)
        m["xin"] = xin
        mf = np.zeros((2, 128, 512), np.float32)
        if hf == 1:
            mf[0] = prev4
        mf[1] = prev4
        m["mfirst4"] = mf
        rot = np.empty((2, 2, 128, HALO + PASS), np.float32)
        for ps_i in range(2):
            pos = np.arange(t0 + ps_i * PASS - HALO, t0 + ps_i * PASS + PASS) + HALO
            rot[ps_i, 0] = np.tile(cosT[pos].T, (4, 1))
            rot[ps_i, 1] = np.tile(sinT[pos].T, (4, 1))
        m["rotn"] = rot
        in_maps.append(m)
    return in_maps


_NC_CACHE = {}


def kernel(**inputs):
    in_maps = prepare_inputs(**inputs)
    if "nc" not in _NC_CACHE:
        _NC_CACHE["nc"] = build_program()
    nc = _NC_CACHE["nc"]
    res = run_bass_kernel_spmd(nc, in_maps, core_ids=list(range(8)))
    out = np.empty((NBATCH, SEQ, D), np.float32)
    for c in range(8):
        b, hf = divmod(c, 2)
        out[b, hf * TOK:(hf + 1) * TOK] = res.results[c]["y"]
    return out
```

```python
import numpy as np
from contextlib import ExitStack
import concourse.bass as bass
import concourse.mybir as mybir
from concourse.bass_utils import run_bass_kernel_spmd

F32 = mybir.dt.float32
BF16 = mybir.dt.bfloat16
AF = mybir.ActivationFunctionType
ALU = mybir.AluOpType

D = 1024
SEQ = 8192
NBATCH = 4
DFF = 2816
NF = DFF // 128
ALPHA = float(2.0 ** 0.25)
EPS = 1e-5
GROUP_D = (1, 4, 16)
TOK = 4096
PASS = 2048
HALO = 2048
KB = 1024
ARENA_BYTES = 201 * KB


class Buf:
    __slots__ = ("name", "w", "r")

    def __init__(self, name):
        self.name = name
        self.w = None
        self.r = []


class DmaSem:
    __slots__ = ("name", "issued", "handle", "last")

    def __init__(self, name):
        self.name = name
        self.issued = 0
        self.handle = None
        self.last = None


class Op:
    __slots__ = ("eng", "fn", "idx", "waits", "signal", "dma", "vc")


COMPUTE = ("pe", "act", "dve", "pool")
ALL_ENG = ("pe", "act", "dve", "pool", "sp")


class Prog:
    def __init__(self):
        self.ops = {e: [] for e in ALL_ENG}
        self.know = {e: {} for e in ALL_ENG}
        self.pending = {e: [] for e in ALL_ENG}
        self.dma_sems = {}

    def dsem(self, name):
        s = self.dma_sems.get(name)
        if s is None:
            s = DmaSem(name)
            self.dma_sems[name] = s
        return s

    def _dep(self, Y, deps):
        if Y.dma is not None:
            key = ("dma", Y.dma)
            idx = Y.dma.issued
            Y = Y.dma.last
        else:
            key = Y.eng
            idx = Y.idx
        cur = deps.get(key)
        if cur is None or cur[0] < idx:
            deps[key] = (idx, Y)

    def barrier(self):
        lst = []
        for e in COMPUTE:
            for o in reversed(self.ops[e]):
                if o.dma is None:
                    lst.append(o)
                    break
        for s in self.dma_sems.values():
            if s.last is not None:
                lst.append(s.last)
        for e in ALL_ENG:
            self.pending[e] = list(lst)

    def op(self, eng, fn, reads=(), writes=(), dma=None):
        X = Op()
        X.eng = eng
        X.fn = fn
        X.idx = len(self.ops[eng])
        X.signal = False
        X.dma = dma
        deps = {}
        if self.pending[eng]:
            for Y in self.pending[eng]:
                if not (Y.dma is None and Y.eng == eng and dma is None):
                    self._dep(Y, deps)
            self.pending[eng] = []
        for b in reads:
            Y = b.w
            if Y is not None:
                if (Y.dma is None and dma is None and Y.eng == eng and eng in ("dve", "act")
                        and X.idx - Y.idx >= 2):
                    continue
                self._dep(Y, deps)
        for b in writes:
            Y = b.w
            if Y is not None:
                if not (Y.dma is None and dma is None and Y.eng == eng):
                    self._dep(Y, deps)
            for Y in b.r:
                if not (Y.dma is None and dma is None and Y.eng == eng):
                    self._dep(Y, deps)
        know = self.know[eng]
        waits = []
        for key, (idx, Y) in deps.items():
            if know.get(key, -1) >= idx:
                continue
            waits.append((key, idx))
            if not isinstance(key, tuple):
                self.ops[key][idx].signal = True
            for k2, v2 in Y.vc.items():
                if know.get(k2, -1) < v2:
                    know[k2] = v2
            if know.get(key, -1) < idx:
                know[key] = idx
        X.waits = waits
        vc = dict(know)
        if dma is not None:
            dma.issued += 1
            dma.last = X
            vc[("dma", dma)] = dma.issued
        else:
            vc[eng] = X.idx
        X.vc = vc
        self.ops[eng].append(X)
        for b in reads:
            b.r.append(X)
        for b in writes:
            b.w = X
            b.r = []
        return X

    def emit(self, nc, st, final_wait_sems=()):
        esem = {}
        for e in COMPUTE:
            esem[e] = st.enter_context(nc.semaphore("sem_" + e))
        for s in self.dma_sems.values():
            s.handle = st.enter_context(nc.semaphore("d_" + s.name))
        sigcnt = {}
        for e in COMPUTE:
            c = 0
            lst = []
            for o in self.ops[e]:
                if o.signal:
                    c += 1
                lst.append(c)
            sigcnt[e] = lst
        block = st.enter_context(nc.Block())

        def run(engname):
            def body(engine):
                for o in self.ops[engname]:
                    for key, idx in o.waits:
                        if isinstance(key, tuple):
                            engine.wait_ge(key[1].handle, 16 * idx)
                        else:
                            engine.wait_ge(esem[key], sigcnt[key][idx])
                    ins = o.fn(engine)
                    if o.dma is not None:
                        ins.then_inc(o.dma.handle, 16)
                    elif o.signal:
                        ins.then_inc(esem[engname], 1)
                if engname == "sp":
                    for s in final_wait_sems:
                        engine.wait_ge(s.handle, 16 * s.issued)
            return body

        block.tensor(run("pe"))
        block.scalar(run("act"))
        block.vector(run("dve"))
        block.gpsimd(run("pool"))
        block.sync(run("sp"))


def build_program(debug=None, phases=("A", "B", "C"), npass=2, a_iters=None):
    nc = bass.Bass("TRN2", target_bir_lowering=False)
    P = Prog()

    def dt_in(name, shape):
        return nc.dram_tensor(name, list(shape), F32, kind="ExternalInput").ap()

    xin = dt_in("xin", [HALO + TOK, D])
    wqk_d = dt_in("wqk", [3, 2, 128, 8 * 4 * 128])
    wv_d = dt_in("wv", [3, 2, 128, 8 * 256])
    wcv_d = dt_in("wcv", [128, 8 * 1024])
    wgate_d = dt_in("wgate", [128, 8 * 8 * 2 * 128])
    wao_d = dt_in("wao", [128, 4 * 1024])
    wco_d = dt_in("wco", [128, 4 * 1024])
    wo_d = dt_in("wo", [128, 8 * 1024])
    wg_d = dt_in("wg", [NF // 2, 128, 8 * 256])
    wu_d = dt_in("wu", [NF // 2, 128, 8 * 256])
    wd_d = dt_in("wd", [128, NF, 1024])
    cw_d = dt_in("cw", [128, 4 * 31])
    cv_d = dt_in("cv", [128, 12])
    lnv_d = dt_in("lnv", [4, 1024])
    rotn_d = dt_in("rotn", [2, 2, 128, HALO + PASS])
    mask_d = dt_in("mask4", [128, 4 * 256])
    mfirst_d = dt_in("mfirst4", [2, 128, 4 * 128])
    ident_d = dt_in("ident", [128, 128])
    y_d = nc.dram_tensor("y", [TOK, D], F32, kind="ExternalOutput").ap()
    x1d = nc.dram_tensor("x1d", [TOK, D], F32, kind="Internal").ap()
    dbg_d = {}
    if debug:
        for name, shape in debug.items():
            dtp = F32
            if isinstance(shape, tuple):
                shape, dtp = shape
            dbg_d[name] = nc.dram_tensor("dbg_" + name, list(shape), dtp, kind="ExternalOutput").ap()

    st = ExitStack()
    with st:
        arena = st.enter_context(nc.sbuf_tensor("arena", [128, ARENA_BYTES // 2], BF16))
        cst = st.enter_context(nc.sbuf_tensor("cst", [128, 2800], BF16))
        banks = [st.enter_context(nc.psum_tensor("bank%d" % i, [128, 512], F32)) for i in range(8)]
        BK = [Buf("bank%d" % i) for i in range(8)]

        def view(off_bytes, shape, dt, base=None):
            base = arena if base is None else base
            n = int(np.prod(shape))
            esz = 4 if dt == F32 else 2
            a = base[:, off_bytes // 2: off_bytes // 2 + n * esz // 2]
            if dt == F32:
                a = a.bitcast(F32)
            if len(shape) == 2:
                return a.rearrange("p (a b) -> p a b", a=shape[0])
            if len(shape) == 3:
                return a.rearrange("p (a b c) -> p a b c", a=shape[0], b=shape[1])
            return a

        identb = view(0, [128], BF16, cst)
        identf = view(256, [128], F32, cst)
        onesf = view(768, [128], F32, cst)
        onesb = view(1280, [64], BF16, cst)
        mask4 = view(1408, [4, 256], BF16, cst)
        mfirst4 = view(3456, [4, 128], BF16, cst)
        cw = view(4480, [4, 31], F32, cst)
        cv = view(4976, [3, 4], F32, cst)
        epsb = view(5024, [1], F32, cst)
        mv = view(5040, [16], F32, cst)
        bnst2 = [view(5104 + 48 * i, [12], F32, cst) for i in range(2)]
        Bc = {k: Buf(k) for k in ["identb", "identf", "ones", "mask4", "mfirst4", "cwv", "mv0", "mv1", "bnst0", "bnst1"]}

        XH = view(0, [8, 2048], BF16)
        XO = view(32 * KB, [8, 2048], BF16)
        OT = view(64 * KB, [4, 2048], BF16)
        B_XH, B_XO, B_OT = Buf("XH"), Buf("XO"), Buf("OT")
        X1DB = [Buf("x1d%d" % i) for i in range(TOK // 128)]

        def dma_cast(out, in_, writes, sem, reads=()):
            P.op("pool", lambda e: e.dma_start(out=out, in_=in_), reads=reads, writes=writes, dma=P.dsem(sem))

        def dma_sp(out, in_, writes, sem, reads=()):
            P.op("sp", lambda e: e.dma_start(out=out, in_=in_), reads=reads, writes=writes, dma=P.dsem(sem))

        def dump(name, ap, bufs):
            if debug and name in dbg_d:
                P.op("sp", lambda e: e.dma_start(out=dbg_d[name], in_=ap), reads=bufs, dma=P.dsem("dbg"))

        def mm(out, lhsT, rhs, start, stop, reads, bank, **kw):
            P.op("pe", lambda e: e.matmul(out, lhsT=lhsT, rhs=rhs, start=start, stop=stop, **kw), reads=reads, writes=[BK[bank]])

        dma_cast(identb, ident_d[:, :], [Bc["identb"]], "c0")
        dma_sp(identf, ident_d[:, :], [Bc["identf"]], "c1")
        dma_cast(mask4.rearrange("p a b -> p (a b)"), mask_d[:, :], [Bc["mask4"]], "c0")
        dma_sp(cw.rearrange("p a b -> p (a b)"), cw_d[:, :], [Bc["cwv"]], "c1")
        dma_sp(cv.rearrange("p a b -> p (a b)"), cv_d[:, :], [Bc["cwv"]], "c1")
        P.op("pool", lambda e: e.memset(onesf, 1.0), writes=[Bc["ones"]])
        P.op("pool", lambda e: e.memset(onesb, 1.0), writes=[Bc["ones"]])
        P.op("pool", lambda e: e.memset(epsb, EPS), writes=[Bc["ones"]])

        def layer_norm_tile(Z, Zb, OUT, OUTb, G, Bt, GBb, par=0):
            m_ = mv[:, 4 * par:4 * par + 4]
            bs_ = bnst2[par]
            bm, bb = Bc["mv%d" % par], Bc["bnst%d" % par]
            for h in range(2):
                P.op("dve", lambda e, h=h: e.bn_stats(out=bs_[:, 6 * h:6 * h + 6], in_=Z[:, 512 * h:512 * h + 512]),
                     reads=[Zb], writes=[bb])
            P.op("dve", lambda e: e.bn_aggr(out=m_[:, 0:2], in_=bs_.rearrange("p (a b) -> p a b", b=6)),
                 reads=[bb], writes=[bm])
            P.op("act", lambda e: e.activation(out=m_[:, 2:3], in_=m_[:, 1:2], func=AF.Sqrt, bias=epsb[:, 0:1], scale=1.0),
                 reads=[bm, Bc["ones"]], writes=[bm])
            P.op("dve", lambda e: e.reciprocal(out=m_[:, 2:3], in_=m_[:, 2:3]), reads=[bm], writes=[bm])
            P.op("dve", lambda e: e.scalar_tensor_tensor(out=m_[:, 3:4], in0=m_[:, 0:1], scalar=-1.0, in1=m_[:, 2:3],
                                                         op0=ALU.mult, op1=ALU.mult), reads=[bm], writes=[bm])
            P.op("act", lambda e: e.activation(out=Z, in_=Z, func=AF.Identity, bias=m_[:, 3:4], scale=m_[:, 2:3]),
                 reads=[bm, Zb], writes=[Zb])
            P.op("pool", lambda e: e.tensor_tensor(out=OUT, in0=Z, in1=G, op=ALU.mult), reads=[Zb, GBb], writes=[OUTb])
            P.op("pool", lambda e: e.tensor_tensor(out=OUT, in0=OUT, in1=Bt, op=ALU.add), reads=[OUTb, GBb], writes=[OUTb])

        for ps_i in range(npass):
            row0 = ps_i * PASS
            if "A" in phases or "B" in phases:
                P.barrier()
                XS = [view(188 * KB + 2048 * i, [1024], BF16) for i in range(2)]
                XSb = [Buf("XS0"), Buf("XS1")]
                if ps_i > 0:
                    P.op("dve", lambda e: e.tensor_copy(out=XH, in_=XO), reads=[B_XO], writes=[B_XH])
                first_tile = 16 if ps_i > 0 else 0
                for ti in range(first_tile, 32):
                    bsel = ti % 2
                    r0 = row0 + ti * 128
                    dma_cast(XS[bsel], xin[r0:r0 + 128, :], [XSb[bsel]], "xs%d" % bsel)
                    pb = banks[bsel][:, :].bitcast(BF16)
                    for k in range(8):
                        P.op("pe", lambda e, k=k, pb=pb, bsel=bsel: e.transpose(pb[:, k * 128:(k + 1) * 128], XS[bsel][:, k * 128:(k + 1) * 128], identb),
                             reads=[XSb[bsel], Bc["identb"]], writes=[BK[bsel]])
                    if ti < 16:
                        dst, dstb, c0 = XH, B_XH, ti * 128
                    else:
                        dst, dstb, c0 = XO, B_XO, (ti - 16) * 128
                    if ti % 2 == 0:
                        P.op("act", lambda e, dst=dst, c0=c0, pb=pb: e.copy(out=dst[:, :, c0:c0 + 128], in_=pb.rearrange("p (k c) -> p k c", k=8)),
                             writes=[BK[bsel], dstb])
                    else:
                        P.op("dve", lambda e, dst=dst, c0=c0, pb=pb: e.tensor_copy(out=dst[:, :, c0:c0 + 128], in_=pb.rearrange("p (k c) -> p k c", k=8)),
                             writes=[BK[bsel], dstb])

            if "A" in phases:
                ACC = view(80 * KB, [4, 2048], F32)
                QA = view(112 * KB, [2048], BF16)
                QB = view(116 * KB, [2048], BF16)
                KA = view(120 * KB, [4096], BF16)
                KBt = view(128 * KB, [4096], BF16)
                V = view(136 * KB, [32, 256], BF16)
                WQK = view(152 * KB, [8, 4, 128], BF16)
                WV = view(160 * KB, [8, 256], BF16)
                TAB = [view(164 * KB + 4096 * i, [2, 512], F32) for i in range(2)]
                RT = [view(172 * KB + 2048 * i, [512], F32) for i in range(6)]
                RT2 = [view(192 * KB + 2048 * i, [512], F32) for i in range(2)]
                PT = [view(184 * KB + 2048 * i, [4, 256], BF16) for i in range(2)]
                b_ACC, b_Q, b_K, b_V, b_WQK, b_WV = Buf("ACC"), Buf("Q"), Buf("K"), Buf("V"), Buf("WQK"), Buf("WV")
                b_TAB = [Buf("TAB0"), Buf("TAB1")]
                b_RT = [Buf("RT%d" % i) for i in range(6)]
                b_RT2 = [Buf("RT2_%d" % i) for i in range(2)]
                b_PT = [Buf("PT0"), Buf("PT1")]
                b_ST = Buf("ST")
                dma_cast(mfirst4.rearrange("p a b -> p (a b)"), mfirst_d[ps_i, :, :], [Bc["mfirst4"]], "c0")
                tabn = [0]
                iters = [(hq, g) for hq in range(2) for g in range(3)]
                if a_iters is not None:
                    iters = [it for it in iters if it in a_iters]

                def load_w(hq_, g_):
                    dma_cast(WQK.rearrange("p a b c -> p (a b c)"), wqk_d[g_, hq_, :, :], [b_WQK], "wqk")
                    dma_cast(WV.rearrange("p a b -> p (a b)"), wv_d[g_, hq_, :, :], [b_WV], "wv")

                load_w(*iters[0])
                for it_i, (hq, g) in enumerate(iters):
                    first_g = (g == min(gg for (h2, gg) in iters if h2 == hq))
                    last_g = (g == max(gg for (h2, gg) in iters if h2 == hq))
                    d = GROUP_D[g]
                    L = PASS // d
                    nb = L // 128
                    nh = 128 * d
                    XHv = XH.rearrange("p k (m d) -> p k d m", d=d)
                    XOv = XO.rearrange("p k (m d) -> p k d m", d=d)
                    uh = HALO - nh
                    tiles = []
                    if d == 1:
                        tiles.append((uh, 128, False))
                    else:
                        for u0 in range(uh, HALO, 512):
                            tiles.append((u0, 512, False))
                    for u0 in range(HALO, HALO + PASS, 512):
                        tiles.append((u0, 512, True))
                    KAh = KA[:, 0:nh].rearrange("p (r i) -> p i r", r=d)
                    KBh = KBt[:, 0:nh].rearrange("p (r i) -> p i r", r=d)
                    KAo = KA[:, nh:nh + PASS].rearrange("p (r m) -> p m r", r=d)
                    KBo = KBt[:, nh:nh + PASS].rearrange("p (r m) -> p m r", r=d)
                    QAo = QA.rearrange("p (r m) -> p m r", r=d)
                    QBo = QB.rearrange("p (r m) -> p m r", r=d)

                    pj = 0
                    for (u0, n, own) in tiles:
                        tsel = tabn[0] % 2
                        tabn[0] += 1
                        dma_sp(TAB[tsel][:, :, 0:n], rotn_d[ps_i, :, :, u0:u0 + n].rearrange("c p n -> p c n"),
                               [b_TAB[tsel]], "tab%d" % tsel)
                        C = TAB[tsel][:, 0, 0:n]
                        S = TAB[tsel][:, 1, 0:n]
                        nm = n // d
                        if own:
                            m0 = (u0 - HALO) // d
                        else:
                            m0 = (u0 - uh) // d
                        for which in ([1, 0] if own else [1]):
                            bk0 = 2 * (pj % 2)
                            pj += 1
                            for half in range(2):
                                tcol = which * 2 + half
                                for k in range(8):
                                    rhs = XO[:, k, u0 - HALO:u0 - HALO + n] if own else XH[:, k, u0:u0 + n]
                                    mm(banks[bk0 + half][:, 0:n], WQK[:, k, tcol, :], rhs, k == 0, k == 7, [b_WQK, B_XH, B_XO], bk0 + half)
                            a_sb, b_sb, t1, t2, t3, t4 = [r_[:, 0:n] for r_ in RT]
                            bA, bB = b_RT[0], b_RT[1]
                            if pj % 2 == 0:
                                a_sb, b_sb = RT2[0][:, 0:n], RT2[1][:, 0:n]
                                bA, bB = b_RT2[0], b_RT2[1]
                            P.op("act", lambda e, a_sb=a_sb, n=n, bk0=bk0: e.copy(out=a_sb, in_=banks[bk0][:, 0:n]), writes=[BK[bk0], bA])
                            P.op("act", lambda e, b_sb=b_sb, n=n, bk0=bk0: e.copy(out=b_sb, in_=banks[bk0 + 1][:, 0:n]), writes=[BK[bk0 + 1], bB])
                            if which == 1 and own:
                                dA, dB, db = KAo[:, m0:m0 + nm, :], KBo[:, m0:m0 + nm, :], b_K
                            elif which == 1:
                                dA, dB, db = KAh[:, m0:m0 + nm, :], KBh[:, m0:m0 + nm, :], b_K
                            else:
                                dA, dB, db = QAo[:, m0:m0 + nm, :], QBo[:, m0:m0 + nm, :], b_Q
                            v3 = lambda ap: ap.rearrange("p (m r) -> p m r", r=d)
                            P.op("dve", lambda e, t1=t1, a_sb=a_sb, C=C: e.tensor_tensor(out=t1, in0=a_sb, in1=C, op=ALU.mult),
                                 reads=[bA, b_TAB[tsel]], writes=[b_RT[2]])
                            P.op("dve", lambda e, t2=t2, b_sb=b_sb, S=S: e.tensor_tensor(out=t2, in0=b_sb, in1=S, op=ALU.mult),
                                 reads=[bB, b_TAB[tsel]], writes=[b_RT[3]])
                            P.op("dve", lambda e, t1=v3(t1), t2=v3(t2), dA=dA: e.tensor_tensor(out=dA, in0=t1, in1=t2, op=ALU.subtract),
                                 reads=[b_RT[2], b_RT[3]], writes=[db])
                            P.op("pool", lambda e, t3=t3, b_sb=b_sb, C=C: e.tensor_tensor(out=t3, in0=b_sb, in1=C, op=ALU.mult),
                                 reads=[bB, b_TAB[tsel]], writes=[b_RT[4]])
                            P.op("pool", lambda e, t4=t4, a_sb=a_sb, S=S: e.tensor_tensor(out=t4, in0=a_sb, in1=S, op=ALU.mult),
                                 reads=[bA, b_TAB[tsel]], writes=[b_RT[5]])
                            P.op("pool", lambda e, t3=v3(t3), t4=v3(t4), dB=dB: e.tensor_tensor(out=dB, in0=t3, in1=t4, op=ALU.add),
                                 reads=[b_RT[4], b_RT[5]], writes=[db])
                    nblk = d + d * nb
                    for bi in range(nblk):
                        if bi < d:
                            r = bi
                            if d == 1:
                                lf = lambda k: XH[:, k, 1920:2048]
                            elif d == 4:
                                lf = lambda k, r=r, XHv=XHv: XHv[:, k, r, 384:512]
                            else:
                                lf = lambda k, r=r, XHv=XHv: XHv[:, k, r, 0:128]
                        else:
                            r, kb = divmod(bi - d, nb)
                            lf = lambda k, r=r, kb=kb, XOv=XOv: XOv[:, k, r, kb * 128:(kb + 1) * 128]
                        bsel = bi % 2
                        for k in range(8):
                            mm(banks[bsel][:, 0:256], lf(k), WV[:, k, :], k == 0, k == 7, [b_WV, B_XH, B_XO], bsel)
                        P.op("act", lambda e, bi=bi, bsel=bsel: e.copy(out=V[:, bi, :], in_=banks[bsel][:, 0:256]), writes=[BK[bsel], b_V])
                    if it_i + 1 < len(iters):
                        load_w(*iters[it_i + 1])
                    if debug and hq == 0 and ps_i == 0:
                        dump("ka%d" % g, KA[:, 0:nh + PASS], [b_K])
                        dump("qa%d" % g, QA, [b_Q])
                        dump("v%d" % g, V.rearrange("p a b -> p (a b)"), [b_V])
                    ACCv = ACC.rearrange("p a (m d) -> p a d m", d=d)
                    def att_front(r, kb, par):
                        has_cur = kb >= 0
                        has_prev = kb + 1 < nb
                        if has_cur and has_prev:
                            c_lo, c_hi = 0, 256
                            q0 = r * L + kb * 128
                        elif has_cur:
                            c_lo, c_hi = 0, 128
                            q0 = r * L + kb * 128
                        else:
                            c_lo, c_hi = 128, 256
                            q0 = r * L
                        nq = c_hi - c_lo
                        kc = r * 128 if kb < 0 else nh + r * L + kb * 128
                        pt = PT[par]
                        for hl in range(4):
                            sb_ = banks[4 + hl][:, c_lo:c_hi]
                            kw = {"tile_position": (96, 0)} if hl == 3 else {}
                            for half, (Kt, Qt) in enumerate(((KA, QA), (KBt, QB))):
                                P.op("pe", lambda e, sb_=sb_, Kt=Kt, Qt=Qt, hl=hl, half=half, kw=kw:
                                     e.matmul(sb_, lhsT=Kt[32 * hl:32 * hl + 32, kc:kc + 128], rhs=Qt[32 * hl:32 * hl + 32, q0:q0 + nq],
                                              start=(half == 0), stop=(half == 1), **kw),
                                     reads=[b_K, b_Q], writes=[BK[4 + hl], b_ST])
                        for hl in range(4):
                            sb_ = banks[4 + hl][:, c_lo:c_hi]
                            P.op("act", lambda e, sb_=sb_, pt=pt, hl=hl, c_lo=c_lo, c_hi=c_hi:
                                 e.activation(out=pt[:, hl, c_lo:c_hi], in_=sb_, func=AF.Exp, scale=0.125),
                                 reads=[b_ST], writes=[BK[4 + hl], b_PT[par]])
                        if kb < 0:
                            mk, mkb = mfirst4[:, :, :], Bc["mfirst4"]
                        else:
                            mk, mkb = mask4[:, :, c_lo:c_hi], Bc["mask4"]
                        P.op("dve", lambda e, pt=pt, mk=mk, c_lo=c_lo, c_hi=c_hi:
                             e.tensor_tensor(out=pt[:, :, c_lo:c_hi], in0=pt[:, :, c_lo:c_hi], in1=mk, op=ALU.mult),
                             reads=[mkb], writes=[b_PT[par]])

                    def att_back(r, kb, par):
                        has_cur = kb >= 0
                        has_prev = kb + 1 < nb
                        vb = r if kb < 0 else d + r * nb + kb
                        pt = PT[par]
                        parts = []
                        if has_cur:
                            parts.append((kb, 0, False))
                        if has_prev:
                            parts.append((kb + 1, 128, True))
                        for (qb, pc, is_first) in parts:
                            nbk = 2 + (qb % 2)
                            for hl in range(4):
                                pair, hh = divmod(hl, 2)
                                mm(banks[nbk][64 * hh:64 * hh + 64, pair * 128:(pair + 1) * 128],
                                   V[:, vb, hl * 64:(hl + 1) * 64], pt[:, hl, pc:pc + 128],
                                   is_first and pair == 0, False, [b_V, b_PT[par]], nbk, skip_group_check=True)
                                mm(banks[nbk][64 * hh:64 * hh + 64, (2 + pair) * 128:(3 + pair) * 128],
                                   onesb, pt[:, hl, pc:pc + 128],
                                   False, (not is_first) and hl == 3, [Bc["ones"], b_PT[par]], nbk, skip_group_check=True)
                            if not is_first:
                                dst = ACCv[:, :, r, qb * 128:(qb + 1) * 128]
                                src = banks[nbk][:, :].rearrange("p (a b) -> p a b", a=4)
                                if first_g:
                                    P.op("act", lambda e, dst=dst, src=src: e.copy(out=dst, in_=src), writes=[BK[nbk], b_ACC])
                                else:
                                    P.op("dve", lambda e, dst=dst, src=src: e.tensor_tensor(out=dst, in0=dst, in1=src, op=ALU.add),
                                         writes=[BK[nbk], b_ACC])

                    seq = [(r, kb) for r in range(d) for kb in range(-1, nb)]
                    att_front(seq[0][0], seq[0][1], 0)
                    for ii, (r, kb) in enumerate(seq):
                        if ii + 1 < len(seq):
                            att_front(seq[ii + 1][0], seq[ii + 1][1], (ii + 1) % 2)
                        att_back(r, kb, ii % 2)
                    if debug and hq == 0 and ps_i == 0:
                        dump("acc%d" % g, ACC.rearrange("p a b -> p (a b)"), [b_ACC])
                    if last_g:
                        P.op("dve", lambda e: e.reciprocal(out=ACC[:, 2:4, :], in_=ACC[:, 2:4, :]), writes=[b_ACC])
                        P.op("dve", lambda e, hq=hq: e.tensor_tensor(out=OT[:, 2 * hq:2 * hq + 2, :], in0=ACC[:, 0:2, :], in1=ACC[:, 2:4, :], op=ALU.mult),
                             reads=[b_ACC], writes=[B_OT])

                if debug and ps_i == 0:
                    dump("ot", OT.rearrange("p a b -> p (a b)"), [B_OT])

            if "B" in phases:
                P.barrier()
                WCV = view(80 * KB, [8, 1024], BF16)
                WAO = view(96 * KB, [4, 1024], BF16)
                WCO = view(104 * KB, [4, 1024], BF16)
                WO = view(112 * KB, [8, 1024], BF16)
                U = view(128 * KB, [4, 544], BF16)
                HK = KB // 2
                CACC = view(265 * HK, [4, 512], F32)
                STAT = view(281 * HK, [2, 512], F32)
                UST = view(289 * HK, [4, 512], BF16)
                SIGC = view(297 * HK, [512], F32)
                SIG = [view(301 * HK + 2048 * i, [512], F32) for i in range(2)]
                MM = [view(309 * HK + 2048 * i, [512], F32) for i in range(2)]
                MRG = view(317 * HK, [8, 512], BF16)
                XT2 = [view(333 * HK + 4096 * i, [1024], F32) for i in range(2)]
                Z2 = [view(349 * HK, [1024], F32)] * 2
                X12 = [view(357 * HK, [1024], F32)] * 2
                GB = view(365 * HK, [2, 1024], F32)
                WG2 = [view(381 * HK + 4096 * i, [8, 256], BF16) for i in range(2)]
                SQ = [view(397 * HK, [512], F32)] * 2
                DIAG = view(0, [124, 128], BF16)
                b_WCV, b_WAO, b_WCO, b_WO = Buf("WCV"), Buf("WAO"), Buf("WCO"), Buf("WO")
                b_U = [Buf("U%d" % i) for i in range(4)]
                b_CACC = [Buf("CACC%d" % i) for i in range(4)]
                b_SQ1 = Buf("SQ")
                b_SQ = [b_SQ1, b_SQ1]
                b_STAT, b_UST, b_SIGC, b_DIAG = Buf("STAT"), Buf("UST"), Buf("SIGC"), Buf("DIAG")
                b_SIG = [Buf("SIG0"), Buf("SIG1")]
                b_MM = [Buf("MM0"), Buf("MM1")]
                b_WG2 = [Buf("WG0"), Buf("WG1")]
                b_MRG, b_GB = Buf("MRG"), Buf("GB")
                b_XT2 = [Buf("XT0"), Buf("XT1")]
                b_Z1, b_X11 = Buf("Z0"), Buf("X10")
                b_Z2 = [b_Z1, b_Z1]
                b_X12 = [b_X11, b_X11]
                dma_cast(WCV.rearrange("p a b -> p (a b)"), wcv_d[:, :], [b_WCV], "wcv")
                dma_cast(WAO.rearrange("p a b -> p (a b)"), wao_d[:, :], [b_WAO], "wao")
                dma_cast(WCO.rearrange("p a b -> p (a b)"), wco_d[:, :], [b_WCO], "wco")
                dma_cast(WO.rearrange("p a b -> p (a b)"), wo_d[:, :], [b_WO], "wo")
                dma_sp(GB[:, 0, :], lnv_d[0:1, :].partition_broadcast(128), [b_GB], "gb")
                dma_sp(GB[:, 1, :], lnv_d[1:2, :].partition_broadcast(128), [b_GB], "gb")

                def conv_in_cc(cc, rhsf, n, ucol0):
                    for which in range(2):
                        for k in range(8):
                            mm(banks[which][:, 0:n], WCV[:, k, which * 512 + cc * 128: which * 512 + (cc + 1) * 128], rhsf(k),
                               k == 0, k == 7, [b_WCV, B_XH, B_XO], which)
                    P.op("act", lambda e: e.activation(out=SIGC[:, 0:n], in_=banks[1][:, 0:n], func=AF.Sigmoid),
                         writes=[BK[1], b_SIGC])
                    P.op("dve", lambda e: e.tensor_tensor(out=U[:, cc, ucol0:ucol0 + n], in0=SIGC[:, 0:n], in1=banks[0][:, 0:n], op=ALU.mult),
                         reads=[b_SIGC], writes=[BK[0], b_U[cc]])

                for cc in range(4):
                    conv_in_cc(cc, lambda k: XH[:, k, 2018:2048], 30, 0)
                P.barrier()

                for cc in range(4):
                    for j in range(31):
                        P.op("dve", lambda e, cc=cc, j=j: e.tensor_scalar(out=DIAG[:, cc * 31 + j, :], in0=identb, scalar1=cw[:, cc, j:j + 1], scalar2=None,
                                                                        op0=ALU.mult),
                             reads=[Bc["identb"], Bc["cwv"]], writes=[b_DIAG])

                def stage1(s):
                    t0 = s * 512
                    if s > 0:
                        for cc in range(4):
                            eng = "dve" if cc < 2 else "pool"
                            P.op(eng, lambda e, cc=cc: e.tensor_copy(out=U[:, cc, 0:30], in_=U[:, cc, 512:542]), reads=[b_U[cc]], writes=[b_U[cc]])
                    for cc in range(4):
                        conv_in_cc(cc, lambda k, t0=t0: XO[:, k, t0:t0 + 512], 512, 30)
                        yield
                    for cc in range(4):
                        bank = cc % 2
                        for j in range(31):
                            mm(banks[bank][:, :], DIAG[:, cc * 31 + j, :], U[:, cc, j:j + 512], j == 0, j == 30, [b_DIAG, b_U[cc]], bank)
                            if j % 8 == 7:
                                yield
                        P.op("act", lambda e, cc=cc, bank=bank: e.activation(out=CACC[:, cc, :], in_=banks[bank][:, :], func=AF.Identity,
                                                                           bias=cv[:, 0, cc:cc + 1], scale=1.0),
                             reads=[Bc["cwv"]], writes=[BK[bank], b_CACC[cc]])
                        yield
                    if debug and ps_i == 0 and s == 0:
                        dump("cacc", CACC.rearrange("p a b -> p (a b)"), b_CACC)

                def load_wg(c):
                    dma_cast(WG2[c % 2].rearrange("p a b -> p (a b)"), wgate_d[:, c * 2048:(c + 1) * 2048], [b_WG2[c % 2]], "wg2%d" % (c % 2))

                def stage2(s):
                    t0 = s * 512
                    bC = b_CACC
                    load_wg(0)
                    for cc in range(4):
                        mm(banks[2][:, :], onesf, CACC[:, cc, :], cc == 0, cc == 3, [Bc["ones"], bC[cc]], 2)
                    for cc in range(4):
                        sq = SQ[cc % 2]
                        P.op("act", lambda e, cc=cc, sq=sq: e.activation(out=sq, in_=CACC[:, cc, :], func=AF.Square), reads=[bC[cc]], writes=[b_SQ[cc % 2]])
                        mm(banks[3][:, :], onesf, sq, cc == 0, cc == 3, [Bc["ones"], b_SQ[cc % 2]], 3)
                    mean, rstd = STAT[:, 0, :], STAT[:, 1, :]
                    P.op("act", lambda e: e.mul(out=mean, in_=banks[2][:, :], mul=1.0 / 512), writes=[BK[2], b_STAT])
                    P.op("dve", lambda e: e.tensor_tensor(out=rstd, in0=mean, in1=mean, op=ALU.mult), reads=[b_STAT], writes=[b_STAT])
                    P.op("dve", lambda e: e.scalar_tensor_tensor(out=rstd, in0=banks[3][:, :], scalar=1.0 / 512, in1=rstd, op0=ALU.mult, op1=ALU.subtract),
                         reads=[b_STAT], writes=[BK[3], b_STAT])
                    P.op("act", lambda e: e.activation(out=rstd, in_=rstd, func=AF.Sqrt, bias=epsb[:, 0:1], scale=1.0), reads=[b_STAT, Bc["ones"]], writes=[b_STAT])
                    P.op("dve", lambda e: e.reciprocal(out=rstd, in_=rstd), reads=[b_STAT], writes=[b_STAT])
                    for cc in range(4):
                        P.op("dve", lambda e, cc=cc: e.tensor_tensor(out=CACC[:, cc, :], in0=CACC[:, cc, :], in1=mean, op=ALU.subtract),
                             reads=[bC[cc], b_STAT], writes=[bC[cc]])
                        P.op("dve", lambda e, cc=cc: e.tensor_tensor(out=CACC[:, cc, :], in0=CACC[:, cc, :], in1=rstd, op=ALU.mult),
                             reads=[bC[cc], b_STAT], writes=[bC[cc]])
                        P.op("act", lambda e, cc=cc: e.activation(out=UST[:, cc, :], in_=CACC[:, cc, :], func=AF.Silu, bias=cv[:, 2, cc:cc + 1], scale=cv[:, 1, cc:cc + 1]),
                             reads=[bC[cc], Bc["cwv"]], writes=[b_UST])
                    if debug and ps_i == 0 and s == 0:
                        dump("ust", UST.rearrange("p a b -> p (a b)"), [b_UST])
                    yield
                    for c in range(8):
                        wsel = c % 2
                        WG = WG2[wsel]
                        if c + 1 < 8:
                            load_wg(c + 1)
                        for which in range(2):
                            for k in range(8):
                                mm(banks[4 + which][:, :], WG[:, k, which * 128:(which + 1) * 128], XO[:, k, t0:t0 + 512],
                                   k == 0, k == 7, [b_WG2[wsel], B_XO], 4 + which)
                        for k in range(4):
                            mm(banks[6][:, :], WAO[:, k, c * 128:(c + 1) * 128], OT[:, k, t0:t0 + 512], k == 0, k == 3, [b_WAO, B_OT], 6)
                        for k in range(4):
                            mm(banks[7][:, :], WCO[:, k, c * 128:(c + 1) * 128], UST[:, k, :], k == 0, k == 3, [b_WCO, b_UST], 7)
                        for which in range(2):
                            P.op("act", lambda e, which=which: e.activation(out=SIG[which], in_=banks[4 + which][:, :], func=AF.Sigmoid),
                                 writes=[BK[4 + which], b_SIG[which]])
                            P.op("dve", lambda e, which=which: e.tensor_tensor(out=MM[which], in0=SIG[which], in1=banks[6 + which][:, :], op=ALU.mult),
                                 reads=[b_SIG[which]], writes=[BK[6 + which], b_MM[which]])
                        P.op("pool", lambda e, c=c: e.tensor_tensor(out=MRG[:, c, :], in0=MM[0], in1=MM[1], op=ALU.add),
                             reads=[b_MM[0], b_MM[1]], writes=[b_MRG])
                        yield
                    if debug and ps_i == 0 and s == 0:
                        dump("mrg", MRG.rearrange("p a b -> p (a b)"), [b_MRG])
                    for j in range(4):
                        tok = ps_i * PASS + t0 + j * 128
                        jp = j % 2
                        XT, Z, X1 = XT2[jp], Z2[jp], X12[jp]
                        dma_sp(XT, xin[HALO + tok:HALO + tok + 128, :], [b_XT2[jp]], "xt%d" % jp)
                        for half in range(2):
                            for k in range(8):
                                mm(banks[2 + half][:, :], MRG[:, k, j * 128:(j + 1) * 128], WO[:, k, half * 512:(half + 1) * 512],
                                   k == 0, k == 7, [b_MRG, b_WO], 2 + half)
                        for half in range(2):
                            P.op("dve", lambda e, half=half, Z=Z, XT=XT: e.scalar_tensor_tensor(out=Z[:, half * 512:(half + 1) * 512], in0=XT[:, half * 512:(half + 1) * 512],
                                                                                               scalar=ALPHA, in1=banks[2 + half][:, :], op0=ALU.mult, op1=ALU.add),
                                 reads=[b_XT2[jp]], writes=[BK[2 + half], b_Z2[jp]])
                        if debug and ps_i == 0 and s == 0 and j == 0:
                            dump("z1", Z, [b_Z2[jp]])
                        layer_norm_tile(Z, b_Z2[jp], X1, b_X12[jp], GB[:, 0, :], GB[:, 1, :], b_GB, jp)
                        if debug and ps_i == 0 and s == 0 and j == 0:
                            dump("x1", X1, [b_X12[jp]])
                        dma_cast(x1d[tok:tok + 128, :], X1, [X1DB[tok // 128]], "x1s%d" % jp, reads=[b_X12[jp]])
                        yield

                for _ in stage1(0):
                    pass
                for s in range(4):
                    g2 = stage2(s)
                    g1 = stage1(s + 1) if s < 3 else iter(())
                    d1 = d2 = False
                    while not (d1 and d2):
                        if not d2:
                            try:
                                next(g2)
                            except StopIteration:
                                d2 = True
                        for _ in range(2):
                            if not d1:
                                try:
                                    next(g1)
                                except StopIteration:
                                    d1 = True

        if "C" in phases:
            P.barrier()
            WG = view(0, [NF // 2, 8, 256], BF16)
            WU = view(44 * KB, [NF // 2, 8, 256], BF16)
            WD = view(88 * KB, [NF, 1024], BF16)
            X1A = [view(132 * KB + 4096 * i, [1024], F32) for i in range(2)]
            X1B = [view(140 * KB + 4096 * i, [1024], F32) for i in range(2)]
            X1T = view(148 * KB, [8, 512], BF16)
            HT = view(156 * KB, [NF, 512], BF16)
            SG = [view(178 * KB + 2048 * i, [512], F32) for i in range(2)]
            ZC = view(182 * KB, [1024], F32)
            OUTT = view(186 * KB, [1024], F32)
            GBC = view(190 * KB, [2, 1024], F32)
            b_X1A = [Buf("X1A0"), Buf("X1A1")]
            b_X1B = [Buf("X1B0"), Buf("X1B1")]
            b_X1T, b_HT = Buf("X1T"), Buf("HT")
            b_SG = [Buf("SG0"), Buf("SG1")]
            b_Z, b_OUTT, b_GB = Buf("Zc"), Buf("OUTT"), Buf("GBc")
            NGRP = 11
            b_WGU = [Buf("WGU%d" % i) for i in range(NGRP)]
            b_WD = [Buf("WD%d" % i) for i in range(NGRP)]
            for gi in range(NGRP):
                dma_cast(WG[:, gi, :, :].rearrange("p a b -> p (a b)"), wg_d[gi, :, :], [b_WGU[gi]], "wgu%d" % gi)
                dma_cast(WU[:, gi, :, :].rearrange("p a b -> p (a b)"), wu_d[gi, :, :], [b_WGU[gi]], "wgu%d" % gi)
            for gi in range(NGRP):
                dma_cast(WD[:, 2 * gi:2 * gi + 2, :], wd_d[:, 2 * gi:2 * gi + 2, :], [b_WD[gi]], "wd%d" % gi)
            dma_sp(GBC[:, 0, :], lnv_d[2:3, :].partition_broadcast(128), [b_GB], "gbc")
            dma_sp(GBC[:, 1, :], lnv_d[3:4, :].partition_broadcast(128), [b_GB], "gbc")
            n_ct = (TOK // 512) if npass == 2 else (PASS // 512)
            xa = 0
            xb = 0
            for ct in range(n_ct):
                for j in range(4):
                    ti = ct * 4 + j
                    sel = xa % 2
                    xa += 1
                    dma_sp(X1A[sel], x1d[ti * 128:(ti + 1) * 128, :], [b_X1A[sel]], "x1a%d" % sel, reads=[X1DB[ti]])
                    for half in range(2):
                        for kk in range(4):
                            k = half * 4 + kk
                            P.op("pe", lambda e, half=half, kk=kk, k=k, sel=sel: e.transpose(banks[half][:, kk * 128:(kk + 1) * 128], X1A[sel][:, k * 128:(k + 1) * 128], identf),
                                 reads=[b_X1A[sel], Bc["identf"]], writes=[BK[half]])
                    for half in range(2):
                        src = banks[half][:, :].rearrange("p (k c) -> p k c", k=4)
                        dst = X1T[:, half * 4:half * 4 + 4, j * 128:(j + 1) * 128]
                        if half == 0:
                            P.op("act", lambda e, src=src, dst=dst: e.copy(out=dst, in_=src), writes=[BK[half], b_X1T])
                        else:
                            P.op("dve", lambda e, src=src, dst=dst: e.tensor_copy(out=dst, in_=src), writes=[BK[half], b_X1T])
                for f in range(NF):
                    bg = 2 + 2 * (f % 2)
                    for which, Wt in enumerate((WG, WU)):
                        for k in range(8):
                            mm(banks[bg + which][:, :], Wt[:, f // 2, k, (f % 2) * 128:(f % 2 + 1) * 128], X1T[:, k, :], k == 0, k == 7,
                               [b_WGU[f // 2], b_X1T], bg + which)
                    sg = SG[f % 2]
                    P.op("act", lambda e, sg=sg, bg=bg: e.activation(out=sg, in_=banks[bg][:, :], func=AF.Silu), writes=[BK[bg], b_SG[f % 2]])
                    P.op("dve", lambda e, sg=sg, bg=bg, f=f: e.tensor_tensor(out=HT[:, f, :], in0=sg, in1=banks[bg + 1][:, :], op=ALU.mult),
                         reads=[b_SG[f % 2]], writes=[BK[bg + 1], b_HT])
                for j in range(4):
                    ti = ct * 4 + j
                    sel = xb % 2
                    xb += 1
                    dma_sp(X1B[sel], x1d[ti * 128:(ti + 1) * 128, :], [b_X1B[sel]], "x1b%d" % sel, reads=[X1DB[ti]])
                    for half in range(2):
                        for f in range(NF):
                            mm(banks[6 + half][:, :], HT[:, f, j * 128:(j + 1) * 128], WD[:, f, half * 512:(half + 1) * 512],
                               f == 0, f == NF - 1, [b_HT, b_WD[f // 2]], 6 + half)
                    for half in range(2):
                        P.op("dve", lambda e, half=half, sel=sel: e.scalar_tensor_tensor(out=ZC[:, half * 512:(half + 1) * 512], in0=X1B[sel][:, half * 512:(half + 1) * 512],
                                                                                        scalar=ALPHA, in1=banks[6 + half][:, :], op0=ALU.mult, op1=ALU.add),
                             reads=[b_X1B[sel]], writes=[BK[6 + half], b_Z])
                    layer_norm_tile(ZC, b_Z, OUTT, b_OUTT, GBC[:, 0, :], GBC[:, 1, :], b_GB)
                    dma_cast(y_d[ti * 128:(ti + 1) * 128, :], OUTT, [], "ys", reads=[b_OUTT])

        finals = [P.dsem(n) for n in ("ys", "x1s0", "x1s1", "dbg") if n in P.dma_sems]
        P.emit(nc, st, final_wait_sems=finals)
    return nc


def _rope_tables():
    inv_freq = (1.0 / (np.float32(10000.0) ** (np.arange(0, 64, 2, dtype=np.float32) / np.float32(64)))).astype(np.float32)
    pos = np.arange(-HALO, SEQ, dtype=np.float32)
    ang = (pos[:, None] * inv_freq[None, :]).astype(np.float32)
    return np.cos(ang).astype(np.float32), np.sin(ang).astype(np.float32)


def _pass_positions(g, p0):
    d = GROUP_D[g]
    L = PASS // d
    halo = np.array([p0 + r + d * (i - 128) for r in range(d) for i in range(128)], dtype=np.int64)
    own = np.array([p0 + r + d * m for r in range(d) for m in range(L)], dtype=np.int64)
    return np.concatenate([halo, own])


def prepare_inputs(x, w_in, conv_w, conv_b, conv_ln_g, conv_ln_b, w_attn_out, w_conv_out, w_o,
                   ln1_g, ln1_b, w_ffn_gate, w_ffn_up, w_ffn_down, ln2_g, ln2_b):
    f = lambda a: np.ascontiguousarray(np.asarray(a, dtype=np.float32))
    x = f(x)
    w_in = f(w_in)[0]

    def kmajor(w):
        K, N = w.shape
        return np.ascontiguousarray(w.reshape(K // 128, 128, N).transpose(1, 0, 2))

    wqk = np.empty((3, 2, 128, 8, 4, 128), np.float32)
    wv = np.empty((3, 2, 128, 8, 256), np.float32)
    for g in range(3):
        for hq in range(2):
            for qk in range(2):
                for half in range(2):
                    cols = np.array([g * 1536 + qk * 512 + (4 * hq + hl) * 64 + half * 32 + j for hl in range(4) for j in range(32)])
                    wqk[g, hq, :, :, qk * 2 + half, :] = kmajor(w_in[:, cols])
            c0 = g * 1536 + 1024 + 4 * hq * 64
            wv[g, hq] = kmajor(w_in[:, c0:c0 + 256])
    wcv = kmajor(w_in[:, 4608:5632])
    wgate = np.empty((128, 8, 8, 2, 128), np.float32)
    for which in range(2):
        wk = kmajor(w_in[:, 5632 + which * 1024: 5632 + (which + 1) * 1024])
        wgate[:, :, :, which, :] = wk.reshape(128, 8, 8, 128).transpose(0, 2, 1, 3)
    shared = {
        "wqk": wqk.reshape(3, 2, 128, -1), "wv": wv.reshape(3, 2, 128, -1),
        "wcv": wcv.reshape(128, -1), "wgate": wgate.reshape(128, -1),
        "wao": kmajor(f(w_attn_out)[0]).reshape(128, -1), "wco": kmajor(f(w_conv_out)[0]).reshape(128, -1),
        "wo": kmajor(f(w_o)[0]).reshape(128, -1),
        "wg": np.ascontiguousarray(kmajor(f(w_ffn_gate)[0]).reshape(128, 8, NF // 2, 256).transpose(2, 0, 1, 3)).reshape(NF // 2, 128, -1),
        "wu": np.ascontiguousarray(kmajor(f(w_ffn_up)[0]).reshape(128, 8, NF // 2, 256).transpose(2, 0, 1, 3)).reshape(NF // 2, 128, -1),
        "wd": kmajor(f(w_ffn_down)[0]),
        "cw": np.ascontiguousarray(f(conv_w)[0].T.reshape(4, 128, 31).transpose(1, 0, 2)).reshape(128, -1),
        "cv": np.ascontiguousarray(np.stack([f(conv_b)[0], f(conv_ln_g)[0], f(conv_ln_b)[0]]).reshape(3, 4, 128).transpose(2, 0, 1)).reshape(128, -1),
        "lnv": np.stack([f(ln1_g)[0], f(ln1_b)[0], f(ln2_g)[0], f(ln2_b)[0]]),
        "ident": np.eye(128, dtype=np.float32),
    }
    kk = np.arange(128)[:, None]
    qq = np.arange(128)[None, :]
    cur = (kk <= qq).astype(np.float32)
    prev = (kk >= qq).astype(np.float32)
    m2 = np.concatenate([cur, prev], axis=1)
    shared["mask4"] = np.ascontiguousarray(np.broadcast_to(m2[:, None, :], (128, 4, 256))).reshape(128, -1)
    prev4 = np.ascontiguousarray(np.broadcast_to(prev[:, None, :], (128, 4, 128))).reshape(128, -1)
    cosT, sinT = _rope_tables()
    in_maps = []
    for c in range(8):
        b, hf = divmod(c, 2)
        t0 = hf * TOK
        xin = np.zeros((HALO + TOK, D), np.float32)
        if hf == 0:
            xin[HALO:] = x[b, 0:TOK]
        else:
            xin[:] = x[b, t0 - HALO:t0 + TOK]
        m = dict(shared)
        m["xin"] = xin
        mf = np.zeros((2, 128, 512), np.float32)
        if hf == 1:
            mf[0] = prev4
        mf[1] = prev4
        m["mfirst4"] = mf
        rot = np.empty((2, 2, 128, HALO + PASS), np.float32)
        for ps_i in range(2):
            pos = np.arange(t0 + ps_i * PASS - HALO, t0 + ps_i * PASS + PASS) + HALO
            rot[ps_i, 0] = np.tile(cosT[pos].T, (4, 1))
            rot[ps_i, 1] = np.tile(sinT[pos].T, (4, 1))
        m["rotn"] = rot
        in_maps.append(m)
    return in_maps


_NC_CACHE = {}


def kernel(**inputs):
    in_maps = prepare_inputs(**inputs)
    if "nc" not in _NC_CACHE:
        _NC_CACHE["nc"] = build_program()
    nc = _NC_CACHE["nc"]
    res = run_bass_kernel_spmd(nc, in_maps, core_ids=list(range(8)))
    out = np.empty((NBATCH, SEQ, D), np.float32)
    for c in range(8):
        b, hf = divmod(c, 2)
        out[b, hf * TOK:(hf + 1) * TOK] = res.results[c]["y"]
    return out
```

```python
import numpy as np
from contextlib import ExitStack
import concourse.bass as bass
import concourse.mybir as mybir
from concourse.bass_utils import run_bass_kernel_spmd

F32 = mybir.dt.float32
BF16 = mybir.dt.bfloat16
AF = mybir.ActivationFunctionType
ALU = mybir.AluOpType

D = 1024
SEQ = 8192
NBATCH = 4
DFF = 2816
NF = DFF // 128
ALPHA = float(2.0 ** 0.25)
EPS = 1e-5
GROUP_D = (1, 4, 16)
TOK = 4096
PASS = 2048
HALO = 2048
KB = 1024
ARENA_BYTES = 201 * KB


class Buf:
    __slots__ = ("name", "w", "r")

    def __init__(self, name):
        self.name = name
        self.w = None
        self.r = []


class DmaSem:
    __slots__ = ("name", "issued", "handle", "last")

    def __init__(self, name):
        self.name = name
        self.issued = 0
        self.handle = None
        self.last = None


class Op:
    __slots__ = ("eng", "fn", "idx", "waits", "signal", "dma", "vc")


COMPUTE = ("pe", "act", "dve", "pool")
ALL_ENG = ("pe", "act", "dve", "pool", "sp")


class Prog:
    def __init__(self):
        self.ops = {e: [] for e in ALL_ENG}
        self.know = {e: {} for e in ALL_ENG}
        self.pending = {e: [] for e in ALL_ENG}
        self.dma_sems = {}

    def dsem(self, name):
        s = self.dma_sems.get(name)
        if s is None:
            s = DmaSem(name)
            self.dma_sems[name] = s
        return s

    def _dep(self, Y, deps):
        if Y.dma is not None:
            key = ("dma", Y.dma)
            idx = Y.dma.issued
            Y = Y.dma.last
        else:
            key = Y.eng
            idx = Y.idx
        cur = deps.get(key)
        if cur is None or cur[0] < idx:
            deps[key] = (idx, Y)

    def barrier(self):
        lst = []
        for e in COMPUTE:
            for o in reversed(self.ops[e]):
                if o.dma is None:
                    lst.append(o)
                    break
        for s in self.dma_sems.values():
            if s.last is not None:
                lst.append(s.last)
        for e in ALL_ENG:
            self.pending[e] = list(lst)

    def op(self, eng, fn, reads=(), writes=(), dma=None):
        X = Op()
        X.eng = eng
        X.fn = fn
        X.idx = len(self.ops[eng])
        X.signal = False
        X.dma = dma
        deps = {}
        if self.pending[eng]:
            for Y in self.pending[eng]:
                if not (Y.dma is None and Y.eng == eng and dma is None):
                    self._dep(Y, deps)
            self.pending[eng] = []
        for b in reads:
            Y = b.w
            if Y is not None:
                if (Y.dma is None and dma is None and Y.eng == eng and eng in ("dve", "act")
                        and X.idx - Y.idx >= 2):
                    continue
                self._dep(Y, deps)
        for b in writes:
            Y = b.w
            if Y is not None:
                if not (Y.dma is None and dma is None and Y.eng == eng):
                    self._dep(Y, deps)
            for Y in b.r:
                if not (Y.dma is None and dma is None and Y.eng == eng):
                    self._dep(Y, deps)
        know = self.know[eng]
        waits = []
        for key, (idx, Y) in deps.items():
            if know.get(key, -1) >= idx:
                continue
            waits.append((key, idx))
            if not isinstance(key, tuple):
                self.ops[key][idx].signal = True
            for k2, v2 in Y.vc.items():
                if know.get(k2, -1) < v2:
                    know[k2] = v2
            if know.get(key, -1) < idx:
                know[key] = idx
        X.waits = waits
        vc = dict(know)
        if dma is not None:
            dma.issued += 1
            dma.last = X
            vc[("dma", dma)] = dma.issued
        else:
            vc[eng] = X.idx
        X.vc = vc
        self.ops[eng].append(X)
        for b in reads:
            b.r.append(X)
        for b in writes:
            b.w = X
            b.r = []
        return X

    def emit(self, nc, st, final_wait_sems=()):
        esem = {}
        for e in COMPUTE:
            esem[e] = st.enter_context(nc.semaphore("sem_" + e))
        for s in self.dma_sems.values():
            s.handle = st.enter_context(nc.semaphore("d_" + s.name))
        sigcnt = {}
        for e in COMPUTE:
            c = 0
            lst = []
            for o in self.ops[e]:
                if o.signal:
                    c += 1
                lst.append(c)
            sigcnt[e] = lst
        block = st.enter_context(nc.Block())

        def run(engname):
            def body(engine):
                for o in self.ops[engname]:
                    for key, idx in o.waits:
                        if isinstance(key, tuple):
                            engine.wait_ge(key[1].handle, 16 * idx)
                        else:
                            engine.wait_ge(esem[key], sigcnt[key][idx])
                    ins = o.fn(engine)
                    if o.dma is not None:
                        ins.then_inc(o.dma.handle, 16)
                    elif o.signal:
                        ins.then_inc(esem[engname], 1)
                if engname == "sp":
                    for s in final_wait_sems:
                        engine.wait_ge(s.handle, 16 * s.issued)
            return body

        block.tensor(run("pe"))
        block.scalar(run("act"))
        block.vector(run("dve"))
        block.gpsimd(run("pool"))
        block.sync(run("sp"))


def build_program(debug=None, phases=("A", "B", "C"), npass=2, a_iters=None):
    nc = bass.Bass("TRN2", target_bir_lowering=False)
    P = Prog()

    def dt_in(name, shape):
        return nc.dram_tensor(name, list(shape), F32, kind="ExternalInput").ap()

    xin = dt_in("xin", [HALO + TOK, D])
    wqk_d = dt_in("wqk", [3, 2, 128, 8 * 4 * 128])
    wv_d = dt_in("wv", [3, 2, 128, 8 * 256])
    wcv_d = dt_in("wcv", [128, 8 * 1024])
    wgate_d = dt_in("wgate", [128, 8 * 8 * 2 * 128])
    wao_d = dt_in("wao", [128, 4 * 1024])
    wco_d = dt_in("wco", [128, 4 * 1024])
    wo_d = dt_in("wo", [128, 8 * 1024])
    wg_d = dt_in("wg", [NF // 2, 128, 8 * 256])
    wu_d = dt_in("wu", [NF // 2, 128, 8 * 256])
    wd_d = dt_in("wd", [128, NF, 1024])
    cw_d = dt_in("cw", [128, 4 * 31])
    cv_d = dt_in("cv", [128, 12])
    lnv_d = dt_in("lnv", [4, 1024])
    rotn_d = dt_in("rotn", [2, 2, 128, HALO + PASS])
    mask_d = dt_in("mask4", [128, 4 * 256])
    mfirst_d = dt_in("mfirst4", [2, 128, 4 * 128])
    ident_d = dt_in("ident", [128, 128])
    y_d = nc.dram_tensor("y", [TOK, D], F32, kind="ExternalOutput").ap()
    x1d = nc.dram_tensor("x1d", [TOK, D], F32, kind="Internal").ap()
    dbg_d = {}
    if debug:
        for name, shape in debug.items():
            dtp = F32
            if isinstance(shape, tuple):
                shape, dtp = shape
            dbg_d[name] = nc.dram_tensor("dbg_" + name, list(shape), dtp, kind="ExternalOutput").ap()

    st = ExitStack()
    with st:
        arena = st.enter_context(nc.sbuf_tensor("arena", [128, ARENA_BYTES // 2], BF16))
        cst = st.enter_context(nc.sbuf_tensor("cst", [128, 2800], BF16))
        banks = [st.enter_context(nc.psum_tensor("bank%d" % i, [128, 512], F32)) for i in range(8)]
        BK = [Buf("bank%d" % i) for i in range(8)]

        def view(off_bytes, shape, dt, base=None):
            base = arena if base is None else base
            n = int(np.prod(shape))
            esz = 4 if dt == F32 else 2
            a = base[:, off_bytes // 2: off_bytes // 2 + n * esz // 2]
            if dt == F32:
                a = a.bitcast(F32)
            if len(shape) == 2:
                return a.rearrange("p (a b) -> p a b", a=shape[0])
            if len(shape) == 3:
                return a.rearrange("p (a b c) -> p a b c", a=shape[0], b=shape[1])
            return a

        identb = view(0, [128], BF16, cst)
        identf = view(256, [128], F32, cst)
        onesf = view(768, [128], F32, cst)
        onesb = view(1280, [64], BF16, cst)
        mask4 = view(1408, [4, 256], BF16, cst)
        mfirst4 = view(3456, [4, 128], BF16, cst)
        cw = view(4480, [4, 31], F32, cst)
        cv = view(4976, [3, 4], F32, cst)
        epsb = view(5024, [1], F32, cst)
        mv = view(5040, [16], F32, cst)
        bnst2 = [view(5104 + 48 * i, [12], F32, cst) for i in range(2)]
        Bc = {k: Buf(k) for k in ["identb", "identf", "ones", "mask4", "mfirst4", "cwv", "mv0", "mv1", "bnst0", "bnst1"]}

        XH = view(0, [8, 2048], BF16)
        XO = view(32 * KB, [8, 2048], BF16)
        OT = view(64 * KB, [4, 2048], BF16)
        B_XH, B_XO, B_OT = Buf("XH"), Buf("XO"), Buf("OT")
        X1DB = [Buf("x1d%d" % i) for i in range(TOK // 128)]

        def dma_cast(out, in_, writes, sem, reads=()):
            P.op("pool", lambda e: e.dma_start(out=out, in_=in_), reads=reads, writes=writes, dma=P.dsem(sem))

        def dma_sp(out, in_, writes, sem, reads=()):
            P.op("sp", lambda e: e.dma_start(out=out, in_=in_), reads=reads, writes=writes, dma=P.dsem(sem))

        def dump(name, ap, bufs):
            if debug and name in dbg_d:
                P.op("sp", lambda e: e.dma_start(out=dbg_d[name], in_=ap), reads=bufs, dma=P.dsem("dbg"))

        def mm(out, lhsT, rhs, start, stop, reads, bank, **kw):
            P.op("pe", lambda e: e.matmul(out, lhsT=lhsT, rhs=rhs, start=start, stop=stop, **kw), reads=reads, writes=[BK[bank]])

        dma_cast(identb, ident_d[:, :], [Bc["identb"]], "c0")
        dma_sp(identf, ident_d[:, :], [Bc["identf"]], "c1")
        dma_cast(mask4.rearrange("p a b -> p (a b)"), mask_d[:, :], [Bc["mask4"]], "c0")
        dma_sp(cw.rearrange("p a b -> p (a b)"), cw_d[:, :], [Bc["cwv"]], "c1")
        dma_sp(cv.rearrange("p a b -> p (a b)"), cv_d[:, :], [Bc["cwv"]], "c1")
        P.op("pool", lambda e: e.memset(onesf, 1.0), writes=[Bc["ones"]])
        P.op("pool", lambda e: e.memset(onesb, 1.0), writes=[Bc["ones"]])
        P.op("pool", lambda e: e.memset(epsb, EPS), writes=[Bc["ones"]])

        def layer_norm_tile(Z, Zb, OUT, OUTb, G, Bt, GBb, par=0):
            m_ = mv[:, 4 * par:4 * par + 4]
            bs_ = bnst2[par]
            bm, bb = Bc["mv%d" % par], Bc["bnst%d" % par]
            for h in range(2):
                P.op("dve", lambda e, h=h: e.bn_stats(out=bs_[:, 6 * h:6 * h + 6], in_=Z[:, 512 * h:512 * h + 512]),
                     reads=[Zb], writes=[bb])
            P.op("dve", lambda e: e.bn_aggr(out=m_[:, 0:2], in_=bs_.rearrange("p (a b) -> p a b", b=6)),
                 reads=[bb], writes=[bm])
            P.op("act", lambda e: e.activation(out=m_[:, 2:3], in_=m_[:, 1:2], func=AF.Sqrt, bias=epsb[:, 0:1], scale=1.0),
                 reads=[bm, Bc["ones"]], writes=[bm])
            P.op("dve", lambda e: e.reciprocal(out=m_[:, 2:3], in_=m_[:, 2:3]), reads=[bm], writes=[bm])
            P.op("dve", lambda e: e.scalar_tensor_tensor(out=m_[:, 3:4], in0=m_[:, 0:1], scalar=-1.0, in1=m_[:, 2:3],
                                                         op0=ALU.mult, op1=ALU.mult), reads=[bm], writes=[bm])
            P.op("act", lambda e: e.activation(out=Z, in_=Z, func=AF.Identity, bias=m_[:, 3:4], scale=m_[:, 2:3]),
                 reads=[bm, Zb], writes=[Zb])
            P.op("pool", lambda e: e.tensor_tensor(out=OUT, in0=Z, in1=G, op=ALU.mult), reads=[Zb, GBb], writes=[OUTb])
            P.op("pool", lambda e: e.tensor_tensor(out=OUT, in0=OUT, in1=Bt, op=ALU.add), reads=[OUTb, GBb], writes=[OUTb])

        for ps_i in range(npass):
            row0 = ps_i * PASS
            if "A" in phases or "B" in phases:
                P.barrier()
                XS = [view(188 * KB + 2048 * i, [1024], BF16) for i in range(2)]
                XSb = [Buf("XS0"), Buf("XS1")]
                if ps_i > 0:
                    P.op("dve", lambda e: e.tensor_copy(out=XH, in_=XO), reads=[B_XO], writes=[B_XH])
                first_tile = 16 if ps_i > 0 else 0
                for ti in range(first_tile, 32):
                    bsel = ti % 2
                    r0 = row0 + ti * 128
                    dma_cast(XS[bsel], xin[r0:r0 + 128, :], [XSb[bsel]], "xs%d" % bsel)
                    pb = banks[bsel][:, :].bitcast(BF16)
                    for k in range(8):
                        P.op("pe", lambda e, k=k, pb=pb, bsel=bsel: e.transpose(pb[:, k * 128:(k + 1) * 128], XS[bsel][:, k * 128:(k + 1) * 128], identb),
                             reads=[XSb[bsel], Bc["identb"]], writes=[BK[bsel]])
                    if ti < 16:
                        dst, dstb, c0 = XH, B_XH, ti * 128
                    else:
                        dst, dstb, c0 = XO, B_XO, (ti - 16) * 128
                    if ti % 2 == 0:
                        P.op("act", lambda e, dst=dst, c0=c0, pb=pb: e.copy(out=dst[:, :, c0:c0 + 128], in_=pb.rearrange("p (k c) -> p k c", k=8)),
                             writes=[BK[bsel], dstb])
                    else:
                        P.op("dve", lambda e, dst=dst, c0=c0, pb=pb: e.tensor_copy(out=dst[:, :, c0:c0 + 128], in_=pb.rearrange("p (k c) -> p k c", k=8)),
                             writes=[BK[bsel], dstb])

            if "A" in phases:
                ACC = view(80 * KB, [4, 2048], F32)
                QA = view(112 * KB, [2048], BF16)
                QB = view(116 * KB, [2048], BF16)
                KA = view(120 * KB, [4096], BF16)
                KBt = view(128 * KB, [4096], BF16)
                V = view(136 * KB, [32, 256], BF16)
                WQK = view(152 * KB, [8, 4, 128], BF16)
                WV = view(160 * KB, [8, 256], BF16)
                TAB = [view(164 * KB + 4096 * i, [2, 512], F32) for i in range(2)]
                RT = [view(172 * KB + 2048 * i, [512], F32) for i in range(6)]
                RT2 = [view(192 * KB + 2048 * i, [512], F32) for i in range(2)]
                PT = [view(184 * KB + 2048 * i, [4, 256], BF16) for i in range(2)]
                b_ACC, b_Q, b_K, b_V, b_WQK, b_WV = Buf("ACC"), Buf("Q"), Buf("K"), Buf("V"), Buf("WQK"), Buf("WV")
                b_TAB = [Buf("TAB0"), Buf("TAB1")]
                b_RT = [Buf("RT%d" % i) for i in range(6)]
                b_RT2 = [Buf("RT2_%d" % i) for i in range(2)]
                b_PT = [Buf("PT0"), Buf("PT1")]
                b_ST = Buf("ST")
                dma_cast(mfirst4.rearrange("p a b -> p (a b)"), mfirst_d[ps_i, :, :], [Bc["mfirst4"]], "c0")
                tabn = [0]
                iters = [(hq, g) for hq in range(2) for g in range(3)]
                if a_iters is not None:
                    iters = [it for it in iters if it in a_iters]

                def load_w(hq_, g_):
                    dma_cast(WQK.rearrange("p a b c -> p (a b c)"), wqk_d[g_, hq_, :, :], [b_WQK], "wqk")
                    dma_cast(WV.rearrange("p a b -> p (a b)"), wv_d[g_, hq_, :, :], [b_WV], "wv")

                load_w(*iters[0])
                for it_i, (hq, g) in enumerate(iters):
                    first_g = (g == min(gg for (h2, gg) in iters if h2 == hq))
                    last_g = (g == max(gg for (h2, gg) in iters if h2 == hq))
                    d = GROUP_D[g]
                    L = PASS // d
                    nb = L // 128
                    nh = 128 * d
                    XHv = XH.rearrange("p k (m d) -> p k d m", d=d)
                    XOv = XO.rearrange("p k (m d) -> p k d m", d=d)
                    uh = HALO - nh
                    tiles = []
                    if d == 1:
                        tiles.append((uh, 128, False))
                    else:
                        for u0 in range(uh, HALO, 512):
                            tiles.append((u0, 512, False))
                    for u0 in range(HALO, HALO + PASS, 512):
                        tiles.append((u0, 512, True))
                    KAh = KA[:, 0:nh].rearrange("p (r i) -> p i r", r=d)
                    KBh = KBt[:, 0:nh].rearrange("p (r i) -> p i r", r=d)
                    KAo = KA[:, nh:nh + PASS].rearrange("p (r m) -> p m r", r=d)
                    KBo = KBt[:, nh:nh + PASS].rearrange("p (r m) -> p m r", r=d)
                    QAo = QA.rearrange("p (r m) -> p m r", r=d)
                    QBo = QB.rearrange("p (r m) -> p m r", r=d)

                    pj = 0
                    for (u0, n, own) in tiles:
                        tsel = tabn[0] % 2
                        tabn[0] += 1
                        dma_sp(TAB[tsel][:, :, 0:n], rotn_d[ps_i, :, :, u0:u0 + n].rearrange("c p n -> p c n"),
                               [b_TAB[tsel]], "tab%d" % tsel)
                        C = TAB[tsel][:, 0, 0:n]
                        S = TAB[tsel][:, 1, 0:n]
                        nm = n // d
                        if own:
                            m0 = (u0 - HALO) // d
                        else:
                            m0 = (u0 - uh) // d
                        for which in ([1, 0] if own else [1]):
                            bk0 = 2 * (pj % 2)
                            pj += 1
                            for half in range(2):
                                tcol = which * 2 + half
                                for k in range(8):
                                    rhs = XO[:, k, u0 - HALO:u0 - HALO + n] if own else XH[:, k, u0:u0 + n]
                                    mm(banks[bk0 + half][:, 0:n], WQK[:, k, tcol, :], rhs, k == 0, k == 7, [b_WQK, B_XH, B_XO], bk0 + half)
                            tset = pj % 2
                            tt = ([RT[0], RT[1], RT[2], RT[3]], [RT[4], RT[5], RT2[0], RT2[1]])[tset]
                            tb = ([b_RT[0], b_RT[1], b_RT[2], b_RT[3]], [b_RT[4], b_RT[5], b_RT2[0], b_RT2[1]])[tset]
                            t1, t2, t3, t4 = [r_[:, 0:n] for r_ in tt]
                            if which == 1 and own:
                                dA, dB, db = KAo[:, m0:m0 + nm, :], KBo[:, m0:m0 + nm, :], b_K
                            elif which == 1:
                                dA, dB, db = KAh[:, m0:m0 + nm, :], KBh[:, m0:m0 + nm, :], b_K
                            else:
                                dA, dB, db = QAo[:, m0:m0 + nm, :], QBo[:, m0:m0 + nm, :], b_Q
                            v3 = lambda ap: ap.rearrange("p (m r) -> p m r", r=d)
                            pa, pb_ = banks[bk0][:, 0:n], banks[bk0 + 1][:, 0:n]
                            P.op("dve", lambda e, t1=t1, pa=pa, C=C: e.tensor_tensor(out=t1, in0=pa, in1=C, op=ALU.mult),
                                 reads=[b_TAB[tsel]], writes=[BK[bk0], tb[0]])
                            P.op("dve", lambda e, t4=t4, pa=pa, S=S: e.tensor_tensor(out=t4, in0=pa, in1=S, op=ALU.mult),
                                 reads=[b_TAB[tsel]], writes=[BK[bk0], tb[3]])
                            P.op("dve", lambda e, t2=t2, pb_=pb_, S=S: e.tensor_tensor(out=t2, in0=pb_, in1=S, op=ALU.mult),
                                 reads=[b_TAB[tsel]], writes=[BK[bk0 + 1], tb[1]])
                            P.op("dve", lambda e, t3=t3, pb_=pb_, C=C: e.tensor_tensor(out=t3, in0=pb_, in1=C, op=ALU.mult),
                                 reads=[b_TAB[tsel]], writes=[BK[bk0 + 1], tb[2]])
                            P.op("pool", lambda e, t1=v3(t1), t2=v3(t2), dA=dA: e.tensor_tensor(out=dA, in0=t1, in1=t2, op=ALU.subtract),
                                 reads=[tb[0], tb[1]], writes=[db])
                            P.op("pool", lambda e, t3=v3(t3), t4=v3(t4), dB=dB: e.tensor_tensor(out=dB, in0=t3, in1=t4, op=ALU.add),
                                 reads=[tb[2], tb[3]], writes=[db])
                    nblk = d + d * nb
                    for bi in range(nblk):
                        if bi < d:
                            r = bi
                            if d == 1:
                                lf = lambda k: XH[:, k, 1920:2048]
                            elif d == 4:
                                lf = lambda k, r=r, XHv=XHv: XHv[:, k, r, 384:512]
                            else:
                                lf = lambda k, r=r, XHv=XHv: XHv[:, k, r, 0:128]
                        else:
                            r, kb = divmod(bi - d, nb)
                            lf = lambda k, r=r, kb=kb, XOv=XOv: XOv[:, k, r, kb * 128:(kb + 1) * 128]
                        bsel = bi % 2
                        for k in range(8):
                            mm(banks[bsel][:, 0:256], lf(k), WV[:, k, :], k == 0, k == 7, [b_WV, B_XH, B_XO], bsel)
                        P.op("act", lambda e, bi=bi, bsel=bsel: e.copy(out=V[:, bi, :], in_=banks[bsel][:, 0:256]), writes=[BK[bsel], b_V])
                    if it_i + 1 < len(iters):
                        load_w(*iters[it_i + 1])
                    if debug and hq == 0 and ps_i == 0:
                        dump("ka%d" % g, KA[:, 0:nh + PASS], [b_K])
                        dump("qa%d" % g, QA, [b_Q])
                        dump("v%d" % g, V.rearrange("p a b -> p (a b)"), [b_V])
                    ACCv = ACC.rearrange("p a (m d) -> p a d m", d=d)
                    def att_front(r, kb, par):
                        has_cur = kb >= 0
                        has_prev = kb + 1 < nb
                        if has_cur and has_prev:
                            c_lo, c_hi = 0, 256
                            q0 = r * L + kb * 128
                        elif has_cur:
                            c_lo, c_hi = 0, 128
                            q0 = r * L + kb * 128
                        else:
                            c_lo, c_hi = 128, 256
                            q0 = r * L
                        nq = c_hi - c_lo
                        kc = r * 128 if kb < 0 else nh + r * L + kb * 128
                        pt = PT[par]
                        for hl in range(4):
                            sb_ = banks[4 + hl][:, c_lo:c_hi]
                            kw = {"tile_position": (96, 0)} if hl == 3 else {}
                            for half, (Kt, Qt) in enumerate(((KA, QA), (KBt, QB))):
                                P.op("pe", lambda e, sb_=sb_, Kt=Kt, Qt=Qt, hl=hl, half=half, kw=kw:
                                     e.matmul(sb_, lhsT=Kt[32 * hl:32 * hl + 32, kc:kc + 128], rhs=Qt[32 * hl:32 * hl + 32, q0:q0 + nq],
                                              start=(half == 0), stop=(half == 1), **kw),
                                     reads=[b_K, b_Q], writes=[BK[4 + hl], b_ST])
                        for hl in range(4):
                            sb_ = banks[4 + hl][:, c_lo:c_hi]
                            P.op("act", lambda e, sb_=sb_, pt=pt, hl=hl, c_lo=c_lo, c_hi=c_hi:
                                 e.activation(out=pt[:, hl, c_lo:c_hi], in_=sb_, func=AF.Exp, scale=0.125),
                                 reads=[b_ST], writes=[BK[4 + hl], b_PT[par]])
                        if kb < 0:
                            mk, mkb = mfirst4[:, :, :], Bc["mfirst4"]
                        else:
                            mk, mkb = mask4[:, :, c_lo:c_hi], Bc["mask4"]
                        P.op("dve", lambda e, pt=pt, mk=mk, c_lo=c_lo, c_hi=c_hi:
                             e.tensor_tensor(out=pt[:, :, c_lo:c_hi], in0=pt[:, :, c_lo:c_hi], in1=mk, op=ALU.mult),
                             reads=[mkb], writes=[b_PT[par]])

                    def att_back(r, kb, par):
                        has_cur = kb >= 0
                        has_prev = kb + 1 < nb
                        vb = r if kb < 0 else d + r * nb + kb
                        pt = PT[par]
                        parts = []
                        if has_cur:
                            parts.append((kb, 0, False))
                        if has_prev:
                            parts.append((kb + 1, 128, True))
                        for (qb, pc, is_first) in parts:
                            nbk = 2 + (qb % 2)
                            for hl in range(4):
                                pair, hh = divmod(hl, 2)
                                mm(banks[nbk][64 * hh:64 * hh + 64, pair * 128:(pair + 1) * 128],
                                   V[:, vb, hl * 64:(hl + 1) * 64], pt[:, hl, pc:pc + 128],
                                   is_first and pair == 0, False, [b_V, b_PT[par]], nbk, skip_group_check=True)
                                mm(banks[nbk][64 * hh:64 * hh + 64, (2 + pair) * 128:(3 + pair) * 128],
                                   onesb, pt[:, hl, pc:pc + 128],
                                   False, (not is_first) and hl == 3, [Bc["ones"], b_PT[par]], nbk, skip_group_check=True)
                            if not is_first:
                                dst = ACCv[:, :, r, qb * 128:(qb + 1) * 128]
                                src = banks[nbk][:, :].rearrange("p (a b) -> p a b", a=4)
                                if first_g:
                                    P.op("act", lambda e, dst=dst, src=src: e.copy(out=dst, in_=src), writes=[BK[nbk], b_ACC])
                                else:
                                    P.op("dve", lambda e, dst=dst, src=src: e.tensor_tensor(out=dst, in0=dst, in1=src, op=ALU.add),
                                         writes=[BK[nbk], b_ACC])

                    seq = [(r, kb) for r in range(d) for kb in range(-1, nb)]
                    att_front(seq[0][0], seq[0][1], 0)
                    for ii, (r, kb) in enumerate(seq):
                        if ii + 1 < len(seq):
                            att_front(seq[ii + 1][0], seq[ii + 1][1], (ii + 1) % 2)
                        att_back(r, kb, ii % 2)
                    if debug and hq == 0 and ps_i == 0:
                        dump("acc%d" % g, ACC.rearrange("p a b -> p (a b)"), [b_ACC])
                    if last_g:
                        P.op("dve", lambda e: e.reciprocal(out=ACC[:, 2:4, :], in_=ACC[:, 2:4, :]), writes=[b_ACC])
                        P.op("dve", lambda e, hq=hq: e.tensor_tensor(out=OT[:, 2 * hq:2 * hq + 2, :], in0=ACC[:, 0:2, :], in1=ACC[:, 2:4, :], op=ALU.mult),
                             reads=[b_ACC], writes=[B_OT])

                if debug and ps_i == 0:
                    dump("ot", OT.rearrange("p a b -> p (a b)"), [B_OT])

            if "B" in phases:
                P.barrier()
                WCV = view(80 * KB, [8, 1024], BF16)
                WAO = view(96 * KB, [4, 1024], BF16)
                WCO = view(104 * KB, [4, 1024], BF16)
                WO = view(112 * KB, [8, 1024], BF16)
                U = view(128 * KB, [4, 544], BF16)
                HK = KB // 2
                CACC = view(265 * HK, [4, 512], F32)
                STAT = view(281 * HK, [2, 512], F32)
                UST = view(289 * HK, [4, 512], BF16)
                SIGC = view(297 * HK, [512], F32)
                SIG = [view(301 * HK + 2048 * i, [512], F32) for i in range(2)]
                MM = [view(309 * HK + 2048 * i, [512], F32) for i in range(2)]
                MRG = view(317 * HK, [8, 512], BF16)
                XT2 = [view(333 * HK + 4096 * i, [1024], F32) for i in range(2)]
                Z2 = [view(349 * HK, [1024], F32)] * 2
                X12 = [view(357 * HK, [1024], F32)] * 2
                GB = view(365 * HK, [2, 1024], F32)
                WG2 = [view(381 * HK + 4096 * i, [8, 256], BF16) for i in range(2)]
                SQ = [view(397 * HK, [512], F32)] * 2
                DIAG = view(0, [124, 128], BF16)
                b_WCV, b_WAO, b_WCO, b_WO = Buf("WCV"), Buf("WAO"), Buf("WCO"), Buf("WO")
                b_U = [Buf("U%d" % i) for i in range(4)]
                b_CACC = [Buf("CACC%d" % i) for i in range(4)]
                b_SQ1 = Buf("SQ")
                b_SQ = [b_SQ1, b_SQ1]
                b_STAT, b_UST, b_SIGC, b_DIAG = Buf("STAT"), Buf("UST"), Buf("SIGC"), Buf("DIAG")
                b_SIG = [Buf("SIG0"), Buf("SIG1")]
                b_MM = [Buf("MM0"), Buf("MM1")]
                b_WG2 = [Buf("WG0"), Buf("WG1")]
                b_MRG, b_GB = Buf("MRG"), Buf("GB")
                b_XT2 = [Buf("XT0"), Buf("XT1")]
                b_Z1, b_X11 = Buf("Z0"), Buf("X10")
                b_Z2 = [b_Z1, b_Z1]
                b_X12 = [b_X11, b_X11]
                dma_cast(WCV.rearrange("p a b -> p (a b)"), wcv_d[:, :], [b_WCV], "wcv")
                dma_cast(WAO.rearrange("p a b -> p (a b)"), wao_d[:, :], [b_WAO], "wao")
                dma_cast(WCO.rearrange("p a b -> p (a b)"), wco_d[:, :], [b_WCO], "wco")
                dma_cast(WO.rearrange("p a b -> p (a b)"), wo_d[:, :], [b_WO], "wo")
                dma_sp(GB[:, 0, :], lnv_d[0:1, :].partition_broadcast(128), [b_GB], "gb")
                dma_sp(GB[:, 1, :], lnv_d[1:2, :].partition_broadcast(128), [b_GB], "gb")

                def conv_in_cc(cc, rhsf, n, ucol0):
                    for which in range(2):
                        for k in range(8):
                            mm(banks[which][:, 0:n], WCV[:, k, which * 512 + cc * 128: which * 512 + (cc + 1) * 128], rhsf(k),
                               k == 0, k == 7, [b_WCV, B_XH, B_XO], which)
                    P.op("act", lambda e: e.activation(out=SIGC[:, 0:n], in_=banks[1][:, 0:n], func=AF.Sigmoid),
                         writes=[BK[1], b_SIGC])
                    P.op("dve", lambda e: e.tensor_tensor(out=U[:, cc, ucol0:ucol0 + n], in0=SIGC[:, 0:n], in1=banks[0][:, 0:n], op=ALU.mult),
                         reads=[b_SIGC], writes=[BK[0], b_U[cc]])

                for cc in range(4):
                    conv_in_cc(cc, lambda k: XH[:, k, 2018:2048], 30, 0)
                P.barrier()

                for cc in range(4):
                    for j in range(31):
                        P.op("dve", lambda e, cc=cc, j=j: e.tensor_scalar(out=DIAG[:, cc * 31 + j, :], in0=identb, scalar1=cw[:, cc, j:j + 1], scalar2=None,
                                                                        op0=ALU.mult),
                             reads=[Bc["identb"], Bc["cwv"]], writes=[b_DIAG])

                def stage1(s):
                    t0 = s * 512
                    if s > 0:
                        for cc in range(4):
                            eng = "dve" if cc < 2 else "pool"
                            P.op(eng, lambda e, cc=cc: e.tensor_copy(out=U[:, cc, 0:30], in_=U[:, cc, 512:542]), reads=[b_U[cc]], writes=[b_U[cc]])
                    for cc in range(4):
                        conv_in_cc(cc, lambda k, t0=t0: XO[:, k, t0:t0 + 512], 512, 30)
                        yield
                    for cc in range(4):
                        bank = cc % 2
                        for j in range(31):
                            mm(banks[bank][:, :], DIAG[:, cc * 31 + j, :], U[:, cc, j:j + 512], j == 0, j == 30, [b_DIAG, b_U[cc]], bank)
                            if j % 8 == 7:
                                yield
                        P.op("act", lambda e, cc=cc, bank=bank: e.activation(out=CACC[:, cc, :], in_=banks[bank][:, :], func=AF.Identity,
                                                                           bias=cv[:, 0, cc:cc + 1], scale=1.0),
                             reads=[Bc["cwv"]], writes=[BK[bank], b_CACC[cc]])
                        yield
                    if debug and ps_i == 0 and s == 0:
                        dump("cacc", CACC.rearrange("p a b -> p (a b)"), b_CACC)

                def load_wg(c):
                    dma_cast(WG2[c % 2].rearrange("p a b -> p (a b)"), wgate_d[:, c * 2048:(c + 1) * 2048], [b_WG2[c % 2]], "wg2%d" % (c % 2))

                def stage2(s):
                    t0 = s * 512
                    bC = b_CACC
                    load_wg(0)
                    for cc in range(4):
                        mm(banks[2][:, :], onesf, CACC[:, cc, :], cc == 0, cc == 3, [Bc["ones"], bC[cc]], 2)
                    for cc in range(4):
                        sq = SQ[cc % 2]
                        P.op("act", lambda e, cc=cc, sq=sq: e.activation(out=sq, in_=CACC[:, cc, :], func=AF.Square), reads=[bC[cc]], writes=[b_SQ[cc % 2]])
                        mm(banks[3][:, :], onesf, sq, cc == 0, cc == 3, [Bc["ones"], b_SQ[cc % 2]], 3)
                    mean, rstd = STAT[:, 0, :], STAT[:, 1, :]
                    P.op("act", lambda e: e.mul(out=mean, in_=banks[2][:, :], mul=1.0 / 512), writes=[BK[2], b_STAT])
                    P.op("dve", lambda e: e.tensor_tensor(out=rstd, in0=mean, in1=mean, op=ALU.mult), reads=[b_STAT], writes=[b_STAT])
                    P.op("dve", lambda e: e.scalar_tensor_tensor(out=rstd, in0=banks[3][:, :], scalar=1.0 / 512, in1=rstd, op0=ALU.mult, op1=ALU.subtract),
                         reads=[b_STAT], writes=[BK[3], b_STAT])
                    P.op("act", lambda e: e.activation(out=rstd, in_=rstd, func=AF.Sqrt, bias=epsb[:, 0:1], scale=1.0), reads=[b_STAT, Bc["ones"]], writes=[b_STAT])
                    P.op("dve", lambda e: e.reciprocal(out=rstd, in_=rstd), reads=[b_STAT], writes=[b_STAT])
                    for cc in range(4):
                        P.op("dve", lambda e, cc=cc: e.tensor_tensor(out=CACC[:, cc, :], in0=CACC[:, cc, :], in1=mean, op=ALU.subtract),
                             reads=[bC[cc], b_STAT], writes=[bC[cc]])
                        P.op("dve", lambda e, cc=cc: e.tensor_tensor(out=CACC[:, cc, :], in0=CACC[:, cc, :], in1=rstd, op=ALU.mult),
                             reads=[bC[cc], b_STAT], writes=[bC[cc]])
                        P.op("act", lambda e, cc=cc: e.activation(out=UST[:, cc, :], in_=CACC[:, cc, :], func=AF.Silu, bias=cv[:, 2, cc:cc + 1], scale=cv[:, 1, cc:cc + 1]),
                             reads=[bC[cc], Bc["cwv"]], writes=[b_UST])
                    if debug and ps_i == 0 and s == 0:
                        dump("ust", UST.rearrange("p a b -> p (a b)"), [b_UST])
                    yield
                    for c in range(8):
                        wsel = c % 2
                        WG = WG2[wsel]
                        if c + 1 < 8:
                            load_wg(c + 1)
                        for which in range(2):
                            for k in range(8):
                                mm(banks[4 + which][:, :], WG[:, k, which * 128:(which + 1) * 128], XO[:, k, t0:t0 + 512],
                                   k == 0, k == 7, [b_WG2[wsel], B_XO], 4 + which)
                        for k in range(4):
                            mm(banks[6][:, :], WAO[:, k, c * 128:(c + 1) * 128], OT[:, k, t0:t0 + 512], k == 0, k == 3, [b_WAO, B_OT], 6)
                        for k in range(4):
                            mm(banks[7][:, :], WCO[:, k, c * 128:(c + 1) * 128], UST[:, k, :], k == 0, k == 3, [b_WCO, b_UST], 7)
                        for which in range(2):
                            P.op("act", lambda e, which=which: e.activation(out=SIG[which], in_=banks[4 + which][:, :], func=AF.Sigmoid),
                                 writes=[BK[4 + which], b_SIG[which]])
                            P.op("dve", lambda e, which=which: e.tensor_tensor(out=MM[which], in0=SIG[which], in1=banks[6 + which][:, :], op=ALU.mult),
                                 reads=[b_SIG[which]], writes=[BK[6 + which], b_MM[which]])
                        P.op("pool", lambda e, c=c: e.tensor_tensor(out=MRG[:, c, :], in0=MM[0], in1=MM[1], op=ALU.add),
                             reads=[b_MM[0], b_MM[1]], writes=[b_MRG])
                        yield
                    if debug and ps_i == 0 and s == 0:
                        dump("mrg", MRG.rearrange("p a b -> p (a b)"), [b_MRG])
                    for j in range(4):
                        tok = ps_i * PASS + t0 + j * 128
                        jp = j % 2
                        XT, Z, X1 = XT2[jp], Z2[jp], X12[jp]
                        dma_sp(XT, xin[HALO + tok:HALO + tok + 128, :], [b_XT2[jp]], "xt%d" % jp)
                        for half in range(2):
                            for k in range(8):
                                mm(banks[2 + half][:, :], MRG[:, k, j * 128:(j + 1) * 128], WO[:, k, half * 512:(half + 1) * 512],
                                   k == 0, k == 7, [b_MRG, b_WO], 2 + half)
                        for half in range(2):
                            P.op("dve", lambda e, half=half, Z=Z, XT=XT: e.scalar_tensor_tensor(out=Z[:, half * 512:(half + 1) * 512], in0=XT[:, half * 512:(half + 1) * 512],
                                                                                               scalar=ALPHA, in1=banks[2 + half][:, :], op0=ALU.mult, op1=ALU.add),
                                 reads=[b_XT2[jp]], writes=[BK[2 + half], b_Z2[jp]])
                        if debug and ps_i == 0 and s == 0 and j == 0:
                            dump("z1", Z, [b_Z2[jp]])
                        layer_norm_tile(Z, b_Z2[jp], X1, b_X12[jp], GB[:, 0, :], GB[:, 1, :], b_GB, jp)
                        if debug and ps_i == 0 and s == 0 and j == 0:
                            dump("x1", X1, [b_X12[jp]])
                        dma_cast(x1d[tok:tok + 128, :], X1, [X1DB[tok // 128]], "x1s%d" % jp, reads=[b_X12[jp]])
                        yield

                for _ in stage1(0):
                    pass
                for s in range(4):
                    g2 = stage2(s)
                    g1 = stage1(s + 1) if s < 3 else iter(())
                    d1 = d2 = False
                    while not (d1 and d2):
                        if not d2:
                            try:
                                next(g2)
                            except StopIteration:
                                d2 = True
                        for _ in range(2):
                            if not d1:
                                try:
                                    next(g1)
                                except StopIteration:
                                    d1 = True

        if "C" in phases:
            P.barrier()
            WG = view(0, [NF // 2, 8, 256], BF16)
            WU = view(44 * KB, [NF // 2, 8, 256], BF16)
            WD = view(88 * KB, [NF, 1024], BF16)
            X1A = [view(132 * KB + 4096 * i, [1024], F32) for i in range(2)]
            X1B = [view(140 * KB + 4096 * i, [1024], F32) for i in range(2)]
            X1T = view(148 * KB, [8, 512], BF16)
            HT = view(156 * KB, [NF, 512], BF16)
            SG = [view(178 * KB + 2048 * i, [512], F32) for i in range(2)]
            ZC = view(182 * KB, [1024], F32)
            OUTT = view(186 * KB, [1024], F32)
            GBC = view(190 * KB, [2, 1024], F32)
            b_X1A = [Buf("X1A0"), Buf("X1A1")]
            b_X1B = [Buf("X1B0"), Buf("X1B1")]
            b_X1T, b_HT = Buf("X1T"), Buf("HT")
            b_SG = [Buf("SG0"), Buf("SG1")]
            b_Z, b_OUTT, b_GB = Buf("Zc"), Buf("OUTT"), Buf("GBc")
            NGRP = 11
            b_WGU = [Buf("WGU%d" % i) for i in range(NGRP)]
            b_WD = [Buf("WD%d" % i) for i in range(NGRP)]
            for gi in range(NGRP):
                dma_cast(WG[:, gi, :, :].rearrange("p a b -> p (a b)"), wg_d[gi, :, :], [b_WGU[gi]], "wgu%d" % gi)
                dma_cast(WU[:, gi, :, :].rearrange("p a b -> p (a b)"), wu_d[gi, :, :], [b_WGU[gi]], "wgu%d" % gi)
            for gi in range(NGRP):
                dma_cast(WD[:, 2 * gi:2 * gi + 2, :], wd_d[:, 2 * gi:2 * gi + 2, :], [b_WD[gi]], "wd%d" % gi)
            dma_sp(GBC[:, 0, :], lnv_d[2:3, :].partition_broadcast(128), [b_GB], "gbc")
            dma_sp(GBC[:, 1, :], lnv_d[3:4, :].partition_broadcast(128), [b_GB], "gbc")
            n_ct = (TOK // 512) if npass == 2 else (PASS // 512)
            xa = 0
            xb = 0
            for ct in range(n_ct):
                for j in range(4):
                    ti = ct * 4 + j
                    sel = xa % 2
                    xa += 1
                    dma_sp(X1A[sel], x1d[ti * 128:(ti + 1) * 128, :], [b_X1A[sel]], "x1a%d" % sel, reads=[X1DB[ti]])
                    for half in range(2):
                        for kk in range(4):
                            k = half * 4 + kk
                            P.op("pe", lambda e, half=half, kk=kk, k=k, sel=sel: e.transpose(banks[half][:, kk * 128:(kk + 1) * 128], X1A[sel][:, k * 128:(k + 1) * 128], identf),
                                 reads=[b_X1A[sel], Bc["identf"]], writes=[BK[half]])
                    for half in range(2):
                        src = banks[half][:, :].rearrange("p (k c) -> p k c", k=4)
                        dst = X1T[:, half * 4:half * 4 + 4, j * 128:(j + 1) * 128]
                        if half == 0:
                            P.op("act", lambda e, src=src, dst=dst: e.copy(out=dst, in_=src), writes=[BK[half], b_X1T])
                        else:
                            P.op("dve", lambda e, src=src, dst=dst: e.tensor_copy(out=dst, in_=src), writes=[BK[half], b_X1T])
                for f in range(NF):
                    bg = 2 + 2 * (f % 2)
                    for which, Wt in enumerate((WG, WU)):
                        for k in range(8):
                            mm(banks[bg + which][:, :], Wt[:, f // 2, k, (f % 2) * 128:(f % 2 + 1) * 128], X1T[:, k, :], k == 0, k == 7,
                               [b_WGU[f // 2], b_X1T], bg + which)
                    sg = SG[f % 2]
                    P.op("act", lambda e, sg=sg, bg=bg: e.activation(out=sg, in_=banks[bg][:, :], func=AF.Silu), writes=[BK[bg], b_SG[f % 2]])
                    P.op("dve", lambda e, sg=sg, bg=bg, f=f: e.tensor_tensor(out=HT[:, f, :], in0=sg, in1=banks[bg + 1][:, :], op=ALU.mult),
                         reads=[b_SG[f % 2]], writes=[BK[bg + 1], b_HT])
                for j in range(4):
                    ti = ct * 4 + j
                    sel = xb % 2
                    xb += 1
                    dma_sp(X1B[sel], x1d[ti * 128:(ti + 1) * 128, :], [b_X1B[sel]], "x1b%d" % sel, reads=[X1DB[ti]])
                    for half in range(2):
                        for f in range(NF):
                            mm(banks[6 + half][:, :], HT[:, f, j * 128:(j + 1) * 128], WD[:, f, half * 512:(half + 1) * 512],
                               f == 0, f == NF - 1, [b_HT, b_WD[f // 2]], 6 + half)
                    for half in range(2):
                        P.op("dve", lambda e, half=half, sel=sel: e.scalar_tensor_tensor(out=ZC[:, half * 512:(half + 1) * 512], in0=X1B[sel][:, half * 512:(half + 1) * 512],
                                                                                        scalar=ALPHA, in1=banks[6 + half][:, :], op0=ALU.mult, op1=ALU.add),
                             reads=[b_X1B[sel]], writes=[BK[6 + half], b_Z])
                    layer_norm_tile(ZC, b_Z, OUTT, b_OUTT, GBC[:, 0, :], GBC[:, 1, :], b_GB)
                    dma_cast(y_d[ti * 128:(ti + 1) * 128, :], OUTT, [], "ys", reads=[b_OUTT])

        finals = [P.dsem(n) for n in ("ys", "x1s0", "x1s1", "dbg") if n in P.dma_sems]
        P.emit(nc, st, final_wait_sems=finals)
    return nc


def _rope_tables():
    inv_freq = (1.0 / (np.float32(10000.0) ** (np.arange(0, 64, 2, dtype=np.float32) / np.float32(64)))).astype(np.float32)
    pos = np.arange(-HALO, SEQ, dtype=np.float32)
    ang = (pos[:, None] * inv_freq[None, :]).astype(np.float32)
    return np.cos(ang).astype(np.float32), np.sin(ang).astype(np.float32)


def _pass_positions(g, p0):
    d = GROUP_D[g]
    L = PASS // d
    halo = np.array([p0 + r + d * (i - 128) for r in range(d) for i in range(128)], dtype=np.int64)
    own = np.array([p0 + r + d * m for r in range(d) for m in range(L)], dtype=np.int64)
    return np.concatenate([halo, own])


def prepare_inputs(x, w_in, conv_w, conv_b, conv_ln_g, conv_ln_b, w_attn_out, w_conv_out, w_o,
                   ln1_g, ln1_b, w_ffn_gate, w_ffn_up, w_ffn_down, ln2_g, ln2_b):
    f = lambda a: np.ascontiguousarray(np.asarray(a, dtype=np.float32))
    x = f(x)
    w_in = f(w_in)[0]

    def kmajor(w):
        K, N = w.shape
        return np.ascontiguousarray(w.reshape(K // 128, 128, N).transpose(1, 0, 2))

    wqk = np.empty((3, 2, 128, 8, 4, 128), np.float32)
    wv = np.empty((3, 2, 128, 8, 256), np.float32)
    for g in range(3):
        for hq in range(2):
            for qk in range(2):
                for half in range(2):
                    cols = np.array([g * 1536 + qk * 512 + (4 * hq + hl) * 64 + half * 32 + j for hl in range(4) for j in range(32)])
                    wqk[g, hq, :, :, qk * 2 + half, :] = kmajor(w_in[:, cols])
            c0 = g * 1536 + 1024 + 4 * hq * 64
            wv[g, hq] = kmajor(w_in[:, c0:c0 + 256])
    wcv = kmajor(w_in[:, 4608:5632])
    wgate = np.empty((128, 8, 8, 2, 128), np.float32)
    for which in range(2):
        wk = kmajor(w_in[:, 5632 + which * 1024: 5632 + (which + 1) * 1024])
        wgate[:, :, :, which, :] = wk.reshape(128, 8, 8, 128).transpose(0, 2, 1, 3)
    shared = {
        "wqk": wqk.reshape(3, 2, 128, -1), "wv": wv.reshape(3, 2, 128, -1),
        "wcv": wcv.reshape(128, -1), "wgate": wgate.reshape(128, -1),
        "wao": kmajor(f(w_attn_out)[0]).reshape(128, -1), "wco": kmajor(f(w_conv_out)[0]).reshape(128, -1),
        "wo": kmajor(f(w_o)[0]).reshape(128, -1),
        "wg": np.ascontiguousarray(kmajor(f(w_ffn_gate)[0]).reshape(128, 8, NF // 2, 256).transpose(2, 0, 1, 3)).reshape(NF // 2, 128, -1),
        "wu": np.ascontiguousarray(kmajor(f(w_ffn_up)[0]).reshape(128, 8, NF // 2, 256).transpose(2, 0, 1, 3)).reshape(NF // 2, 128, -1),
        "wd": kmajor(f(w_ffn_down)[0]),
        "cw": np.ascontiguousarray(f(conv_w)[0].T.reshape(4, 128, 31).transpose(1, 0, 2)).reshape(128, -1),
        "cv": np.ascontiguousarray(np.stack([f(conv_b)[0], f(conv_ln_g)[0], f(conv_ln_b)[0]]).reshape(3, 4, 128).transpose(2, 0, 1)).reshape(128, -1),
        "lnv": np.stack([f(ln1_g)[0], f(ln1_b)[0], f(ln2_g)[0], f(ln2_b)[0]]),
        "ident": np.eye(128, dtype=np.float32),
    }
    kk = np.arange(128)[:, None]
    qq = np.arange(128)[None, :]
    cur = (kk <= qq).astype(np.float32)
    prev = (kk >= qq).astype(np.float32)
    m2 = np.concatenate([cur, prev], axis=1)
    shared["mask4"] = np.ascontiguousarray(np.broadcast_to(m2[:, None, :], (128, 4, 256))).reshape(128, -1)
    prev4 = np.ascontiguousarray(np.broadcast_to(prev[:, None, :], (128, 4, 128))).reshape(128, -1)
    cosT, sinT = _rope_tables()
    in_maps = []
    for c in range(8):
        b, hf = divmod(c, 2)
        t0 = hf * TOK
        xin = np.zeros((HALO + TOK, D), np.float32)
        if hf == 0:
            xin[HALO:] = x[b, 0:TOK]
        else:
            xin[:] = x[b, t0 - HALO:t0 + TOK]
        m = dict(shared)
        m["xin"] = xin
        mf = np.zeros((2, 128, 512), np.float32)
        if hf == 1:
            mf[0] = prev4
        mf[1] = prev4
        m["mfirst4"] = mf
        rot = np.empty((2, 2, 128, HALO + PASS), np.float32)
        for ps_i in range(2):
            pos = np.arange(t0 + ps_i * PASS - HALO, t0 + ps_i * PASS + PASS) + HALO
            rot[ps_i, 0] = np.tile(cosT[pos].T, (4, 1))
            rot[ps_i, 1] = np.tile(sinT[pos].T, (4, 1))
        m["rotn"] = rot
        in_maps.append(m)
    return in_maps


_NC_CACHE = {}


def kernel(**inputs):
    in_maps = prepare_inputs(**inputs)
    if "nc" not in _NC_CACHE:
        _NC_CACHE["nc"] = build_program()
    nc = _NC_CACHE["nc"]
    res = run_bass_kernel_spmd(nc, in_maps, core_ids=list(range(8)))
    out = np.empty((NBATCH, SEQ, D), np.float32)
    for c in range(8):
        b, hf = divmod(c, 2)
        out[b, hf * TOK:(hf + 1) * TOK] = res.results[c]["y"]
    return out
```

```python
import numpy as np
from contextlib import ExitStack
import concourse.bass as bass
import concourse.mybir as mybir
from concourse.bass_utils import run_bass_kernel_spmd

F32 = mybir.dt.float32
BF16 = mybir.dt.bfloat16
AF = mybir.ActivationFunctionType
ALU = mybir.AluOpType

D = 1024
SEQ = 8192
NBATCH = 4
DFF = 2816
NF = DFF // 128
ALPHA = float(2.0 ** 0.25)
EPS = 1e-5
GROUP_D = (1, 4, 16)
TOK = 4096
PASS = 2048
HALO = 2048
KB = 1024
ARENA_BYTES = 201 * KB


class Buf:
    __slots__ = ("name", "w", "r")

    def __init__(self, name):
        self.name = name
        self.w = None
        self.r = []


class DmaSem:
    __slots__ = ("name", "issued", "handle", "last")

    def __init__(self, name):
        self.name = name
        self.issued = 0
        self.handle = None
        self.last = None


class Op:
    __slots__ = ("eng", "fn", "idx", "waits", "signal", "dma", "vc")


COMPUTE = ("pe", "act", "dve", "pool")
ALL_ENG = ("pe", "act", "dve", "pool", "sp")


class Prog:
    def __init__(self):
        self.ops = {e: [] for e in ALL_ENG}
        self.know = {e: {} for e in ALL_ENG}
        self.pending = {e: [] for e in ALL_ENG}
        self.dma_sems = {}

    def dsem(self, name):
        s = self.dma_sems.get(name)
        if s is None:
            s = DmaSem(name)
            self.dma_sems[name] = s
        return s

    def _dep(self, Y, deps):
        if Y.dma is not None:
            key = ("dma", Y.dma)
            idx = Y.dma.issued
            Y = Y.dma.last
        else:
            key = Y.eng
            idx = Y.idx
        cur = deps.get(key)
        if cur is None or cur[0] < idx:
            deps[key] = (idx, Y)

    def barrier(self):
        lst = []
        for e in COMPUTE:
            for o in reversed(self.ops[e]):
                if o.dma is None:
                    lst.append(o)
                    break
        for s in self.dma_sems.values():
            if s.last is not None:
                lst.append(s.last)
        for e in ALL_ENG:
            self.pending[e] = list(lst)

    def op(self, eng, fn, reads=(), writes=(), dma=None):
        X = Op()
        X.eng = eng
        X.fn = fn
        X.idx = len(self.ops[eng])
        X.signal = False
        X.dma = dma
        deps = {}
        if self.pending[eng]:
            for Y in self.pending[eng]:
                if not (Y.dma is None and Y.eng == eng and dma is None):
                    self._dep(Y, deps)
            self.pending[eng] = []
        for b in reads:
            Y = b.w
            if Y is not None:
                if (Y.dma is None and dma is None and Y.eng == eng and eng in ("dve", "act")
                        and X.idx - Y.idx >= 2):
                    continue
                self._dep(Y, deps)
        for b in writes:
            Y = b.w
            if Y is not None:
                if not (Y.dma is None and dma is None and Y.eng == eng):
                    self._dep(Y, deps)
            for Y in b.r:
                if not (Y.dma is None and dma is None and Y.eng == eng):
                    self._dep(Y, deps)
        know = self.know[eng]
        waits = []
        for key, (idx, Y) in deps.items():
            if know.get(key, -1) >= idx:
                continue
            waits.append((key, idx))
            if not isinstance(key, tuple):
                self.ops[key][idx].signal = True
            for k2, v2 in Y.vc.items():
                if know.get(k2, -1) < v2:
                    know[k2] = v2
            if know.get(key, -1) < idx:
                know[key] = idx
        X.waits = waits
        vc = dict(know)
        if dma is not None:
            dma.issued += 1
            dma.last = X
            vc[("dma", dma)] = dma.issued
        else:
            vc[eng] = X.idx
        X.vc = vc
        self.ops[eng].append(X)
        for b in reads:
            b.r.append(X)
        for b in writes:
            b.w = X
            b.r = []
        return X

    def emit(self, nc, st, final_wait_sems=()):
        esem = {}
        for e in COMPUTE:
            esem[e] = st.enter_context(nc.semaphore("sem_" + e))
        for s in self.dma_sems.values():
            s.handle = st.enter_context(nc.semaphore("d_" + s.name))
        sigcnt = {}
        for e in COMPUTE:
            c = 0
            lst = []
            for o in self.ops[e]:
                if o.signal:
                    c += 1
                lst.append(c)
            sigcnt[e] = lst
        block = st.enter_context(nc.Block())

        def run(engname):
            def body(engine):
                for o in self.ops[engname]:
                    for key, idx in o.waits:
                        if isinstance(key, tuple):
                            engine.wait_ge(key[1].handle, 16 * idx)
                        else:
                            engine.wait_ge(esem[key], sigcnt[key][idx])
                    ins = o.fn(engine)
                    if o.dma is not None:
                        ins.then_inc(o.dma.handle, 16)
                    elif o.signal:
                        ins.then_inc(esem[engname], 1)
                if engname == "sp":
                    for s in final_wait_sems:
                        engine.wait_ge(s.handle, 16 * s.issued)
            return body

        block.tensor(run("pe"))
        block.scalar(run("act"))
        block.vector(run("dve"))
        block.gpsimd(run("pool"))
        block.sync(run("sp"))


def build_program(debug=None, phases=("A", "B", "C"), npass=2, a_iters=None):
    nc = bass.Bass("TRN2", target_bir_lowering=False)
    P = Prog()

    def dt_in(name, shape):
        return nc.dram_tensor(name, list(shape), F32, kind="ExternalInput").ap()

    xin = dt_in("xin", [HALO + TOK, D])
    wqk_d = dt_in("wqk", [3, 2, 128, 8 * 4 * 128])
    wv_d = dt_in("wv", [3, 2, 128, 8 * 256])
    wcv_d = dt_in("wcv", [128, 8 * 1024])
    wgate_d = dt_in("wgate", [128, 8 * 8 * 2 * 128])
    wao_d = dt_in("wao", [128, 4 * 1024])
    wco_d = dt_in("wco", [128, 4 * 1024])
    wo_d = dt_in("wo", [128, 8 * 1024])
    wg_d = dt_in("wg", [NF // 2, 128, 8 * 256])
    wu_d = dt_in("wu", [NF // 2, 128, 8 * 256])
    wd_d = dt_in("wd", [128, NF, 1024])
    cw_d = dt_in("cw", [128, 4 * 31])
    cv_d = dt_in("cv", [128, 12])
    lnv_d = dt_in("lnv", [4, 1024])
    rotn_d = dt_in("rotn", [2, 2, 128, HALO + PASS])
    mask_d = dt_in("mask4", [128, 4 * 256])
    mfirst_d = dt_in("mfirst4", [2, 128, 4 * 128])
    ident_d = dt_in("ident", [128, 128])
    y_d = nc.dram_tensor("y", [TOK, D], F32, kind="ExternalOutput").ap()
    x1d = nc.dram_tensor("x1d", [TOK, D], F32, kind="Internal").ap()
    dbg_d = {}
    if debug:
        for name, shape in debug.items():
            dtp = F32
            if isinstance(shape, tuple):
                shape, dtp = shape
            dbg_d[name] = nc.dram_tensor("dbg_" + name, list(shape), dtp, kind="ExternalOutput").ap()

    st = ExitStack()
    with st:
        arena = st.enter_context(nc.sbuf_tensor("arena", [128, ARENA_BYTES // 2], BF16))
        cst = st.enter_context(nc.sbuf_tensor("cst", [128, 2800], BF16))
        banks = [st.enter_context(nc.psum_tensor("bank%d" % i, [128, 512], F32)) for i in range(8)]
        BK = [Buf("bank%d" % i) for i in range(8)]

        def view(off_bytes, shape, dt, base=None):
            base = arena if base is None else base
            n = int(np.prod(shape))
            esz = 4 if dt == F32 else 2
            a = base[:, off_bytes // 2: off_bytes // 2 + n * esz // 2]
            if dt == F32:
                a = a.bitcast(F32)
            if len(shape) == 2:
                return a.rearrange("p (a b) -> p a b", a=shape[0])
            if len(shape) == 3:
                return a.rearrange("p (a b c) -> p a b c", a=shape[0], b=shape[1])
            return a

        identb = view(0, [128], BF16, cst)
        identf = view(256, [128], F32, cst)
        onesf = view(768, [128], F32, cst)
        onesb = view(1280, [64], BF16, cst)
        mask4 = view(1408, [4, 256], BF16, cst)
        mfirst4 = view(3456, [4, 128], BF16, cst)
        cw = view(4480, [4, 31], F32, cst)
        cv = view(4976, [3, 4], F32, cst)
        epsb = view(5024, [1], F32, cst)
        mv = view(5040, [16], F32, cst)
        bnst2 = [view(5104 + 48 * i, [12], F32, cst) for i in range(2)]
        Bc = {k: Buf(k) for k in ["identb", "identf", "ones", "mask4", "mfirst4", "cwv", "mv0", "mv1", "bnst0", "bnst1"]}

        XH = view(0, [8, 2048], BF16)
        XO = view(32 * KB, [8, 2048], BF16)
        OT = view(64 * KB, [4, 2048], BF16)
        B_XH, B_XO, B_OT = Buf("XH"), Buf("XO"), Buf("OT")
        X1DB = [Buf("x1d%d" % i) for i in range(TOK // 128)]

        def dma_cast(out, in_, writes, sem, reads=()):
            P.op("pool", lambda e: e.dma_start(out=out, in_=in_), reads=reads, writes=writes, dma=P.dsem(sem))

        def dma_sp(out, in_, writes, sem, reads=()):
            P.op("sp", lambda e: e.dma_start(out=out, in_=in_), reads=reads, writes=writes, dma=P.dsem(sem))

        def dump(name, ap, bufs):
            if debug and name in dbg_d:
                P.op("sp", lambda e: e.dma_start(out=dbg_d[name], in_=ap), reads=bufs, dma=P.dsem("dbg"))

        def mm(out, lhsT, rhs, start, stop, reads, bank, **kw):
            P.op("pe", lambda e: e.matmul(out, lhsT=lhsT, rhs=rhs, start=start, stop=stop, **kw), reads=reads, writes=[BK[bank]])

        dma_cast(identb, ident_d[:, :], [Bc["identb"]], "c0")
        dma_sp(identf, ident_d[:, :], [Bc["identf"]], "c1")
        dma_cast(mask4.rearrange("p a b -> p (a b)"), mask_d[:, :], [Bc["mask4"]], "c0")
        dma_sp(cw.rearrange("p a b -> p (a b)"), cw_d[:, :], [Bc["cwv"]], "c1")
        dma_sp(cv.rearrange("p a b -> p (a b)"), cv_d[:, :], [Bc["cwv"]], "c1")
        P.op("pool", lambda e: e.memset(onesf, 1.0), writes=[Bc["ones"]])
        P.op("pool", lambda e: e.memset(onesb, 1.0), writes=[Bc["ones"]])
        P.op("pool", lambda e: e.memset(epsb, EPS), writes=[Bc["ones"]])

        def layer_norm_tile(Z, Zb, OUT, OUTb, G, Bt, GBb, par=0):
            m_ = mv[:, 4 * par:4 * par + 4]
            bs_ = bnst2[par]
            bm, bb = Bc["mv%d" % par], Bc["bnst%d" % par]
            for h in range(2):
                P.op("dve", lambda e, h=h: e.bn_stats(out=bs_[:, 6 * h:6 * h + 6], in_=Z[:, 512 * h:512 * h + 512]),
                     reads=[Zb], writes=[bb])
            P.op("dve", lambda e: e.bn_aggr(out=m_[:, 0:2], in_=bs_.rearrange("p (a b) -> p a b", b=6)),
                 reads=[bb], writes=[bm])
            P.op("act", lambda e: e.activation(out=m_[:, 2:3], in_=m_[:, 1:2], func=AF.Sqrt, bias=epsb[:, 0:1], scale=1.0),
                 reads=[bm, Bc["ones"]], writes=[bm])
            P.op("dve", lambda e: e.reciprocal(out=m_[:, 2:3], in_=m_[:, 2:3]), reads=[bm], writes=[bm])
            P.op("dve", lambda e: e.scalar_tensor_tensor(out=m_[:, 3:4], in0=m_[:, 0:1], scalar=-1.0, in1=m_[:, 2:3],
                                                         op0=ALU.mult, op1=ALU.mult), reads=[bm], writes=[bm])
            P.op("act", lambda e: e.activation(out=Z, in_=Z, func=AF.Identity, bias=m_[:, 3:4], scale=m_[:, 2:3]),
                 reads=[bm, Zb], writes=[Zb])
            P.op("pool", lambda e: e.tensor_tensor(out=OUT, in0=Z, in1=G, op=ALU.mult), reads=[Zb, GBb], writes=[OUTb])
            P.op("pool", lambda e: e.tensor_tensor(out=OUT, in0=OUT, in1=Bt, op=ALU.add), reads=[OUTb, GBb], writes=[OUTb])

        for ps_i in range(npass):
            row0 = ps_i * PASS
            if "A" in phases or "B" in phases:
                P.barrier()
                XS = [view(188 * KB + 2048 * i, [1024], BF16) for i in range(4)]
                XSb = [Buf("XS%d" % i) for i in range(4)]
                if ps_i > 0:
                    P.op("dve", lambda e: e.tensor_copy(out=XH, in_=XO), reads=[B_XO], writes=[B_XH])
                first_tile = 16 if ps_i > 0 else 0
                for ti in range(first_tile, 32):
                    bsel = ti % 2
                    xsel = ti % 4
                    r0 = row0 + ti * 128
                    dma_cast(XS[xsel], xin[r0:r0 + 128, :], [XSb[xsel]], "xs%d" % xsel)
                    pb = banks[bsel][:, :].bitcast(BF16)
                    for k in range(8):
                        P.op("pe", lambda e, k=k, pb=pb, xsel=xsel: e.transpose(pb[:, k * 128:(k + 1) * 128], XS[xsel][:, k * 128:(k + 1) * 128], identb),
                             reads=[XSb[xsel], Bc["identb"]], writes=[BK[bsel]])
                    if ti < 16:
                        dst, dstb, c0 = XH, B_XH, ti * 128
                    else:
                        dst, dstb, c0 = XO, B_XO, (ti - 16) * 128
                    if ti % 2 == 0:
                        P.op("act", lambda e, dst=dst, c0=c0, pb=pb: e.copy(out=dst[:, :, c0:c0 + 128], in_=pb.rearrange("p (k c) -> p k c", k=8)),
                             writes=[BK[bsel], dstb])
                    else:
                        P.op("dve", lambda e, dst=dst, c0=c0, pb=pb: e.tensor_copy(out=dst[:, :, c0:c0 + 128], in_=pb.rearrange("p (k c) -> p k c", k=8)),
                             writes=[BK[bsel], dstb])

            if "A" in phases:
                ACC = view(80 * KB, [4, 2048], F32)
                QA = view(112 * KB, [2048], BF16)
                QB = view(116 * KB, [2048], BF16)
                KA = view(120 * KB, [4096], BF16)
                KBt = view(128 * KB, [4096], BF16)
                V = view(136 * KB, [32, 256], BF16)
                WQK = view(152 * KB, [8, 4, 128], BF16)
                WV = view(160 * KB, [8, 256], BF16)
                TAB = [view(164 * KB + 4096 * i, [2, 512], F32) for i in range(2)]
                RT = [view(172 * KB + 2048 * i, [512], F32) for i in range(6)]
                RT2 = [view(192 * KB + 2048 * i, [512], F32) for i in range(2)]
                PT = [view(184 * KB + 2048 * i, [4, 256], BF16) for i in range(2)]
                b_ACC, b_Q, b_K, b_V, b_WQK, b_WV = Buf("ACC"), Buf("Q"), Buf("K"), Buf("V"), Buf("WQK"), Buf("WV")
                b_TAB = [Buf("TAB0"), Buf("TAB1")]
                b_RT = [Buf("RT%d" % i) for i in range(6)]
                b_RT2 = [Buf("RT2_%d" % i) for i in range(2)]
                b_PT = [Buf("PT0"), Buf("PT1")]
                b_ST = Buf("ST")
                dma_cast(mfirst4.rearrange("p a b -> p (a b)"), mfirst_d[ps_i, :, :], [Bc["mfirst4"]], "c0")
                tabn = [0]
                iters = [(hq, g) for hq in range(2) for g in range(3)]
                if a_iters is not None:
                    iters = [it for it in iters if it in a_iters]

                def load_w(hq_, g_):
                    dma_cast(WQK.rearrange("p a b c -> p (a b c)"), wqk_d[g_, hq_, :, :], [b_WQK], "wqk")
                    dma_cast(WV.rearrange("p a b -> p (a b)"), wv_d[g_, hq_, :, :], [b_WV], "wv")

                load_w(*iters[0])
                for it_i, (hq, g) in enumerate(iters):
                    first_g = (g == min(gg for (h2, gg) in iters if h2 == hq))
                    last_g = (g == max(gg for (h2, gg) in iters if h2 == hq))
                    d = GROUP_D[g]
                    L = PASS // d
                    nb = L // 128
                    nh = 128 * d
                    XHv = XH.rearrange("p k (m d) -> p k d m", d=d)
                    XOv = XO.rearrange("p k (m d) -> p k d m", d=d)
                    uh = HALO - nh
                    tiles = []
                    if d == 1:
                        tiles.append((uh, 128, False))
                    else:
                        for u0 in range(uh, HALO, 512):
                            tiles.append((u0, 512, False))
                    for u0 in range(HALO, HALO + PASS, 512):
                        tiles.append((u0, 512, True))
                    KAh = KA[:, 0:nh].rearrange("p (r i) -> p i r", r=d)
                    KBh = KBt[:, 0:nh].rearrange("p (r i) -> p i r", r=d)
                    KAo = KA[:, nh:nh + PASS].rearrange("p (r m) -> p m r", r=d)
                    KBo = KBt[:, nh:nh + PASS].rearrange("p (r m) -> p m r", r=d)
                    QAo = QA.rearrange("p (r m) -> p m r", r=d)
                    QBo = QB.rearrange("p (r m) -> p m r", r=d)

                    pj = 0
                    for (u0, n, own) in tiles:
                        tsel = tabn[0] % 2
                        tabn[0] += 1
                        dma_sp(TAB[tsel][:, :, 0:n], rotn_d[ps_i, :, :, u0:u0 + n].rearrange("c p n -> p c n"),
                               [b_TAB[tsel]], "tab%d" % tsel)
                        C = TAB[tsel][:, 0, 0:n]
                        S = TAB[tsel][:, 1, 0:n]
                        nm = n // d
                        if own:
                            m0 = (u0 - HALO) // d
                        else:
                            m0 = (u0 - uh) // d
                        for which in ([1, 0] if own else [1]):
                            bk0 = 2 * (pj % 2)
                            pj += 1
                            for half in range(2):
                                tcol = which * 2 + half
                                for k in range(8):
                                    rhs = XO[:, k, u0 - HALO:u0 - HALO + n] if own else XH[:, k, u0:u0 + n]
                                    mm(banks[bk0 + half][:, 0:n], WQK[:, k, tcol, :], rhs, k == 0, k == 7, [b_WQK, B_XH, B_XO], bk0 + half)
                            tset = pj % 2
                            tt = ([RT[0], RT[1], RT[2], RT[3]], [RT[4], RT[5], RT2[0], RT2[1]])[tset]
                            tb = ([b_RT[0], b_RT[1], b_RT[2], b_RT[3]], [b_RT[4], b_RT[5], b_RT2[0], b_RT2[1]])[tset]
                            t1, t2, t3, t4 = [r_[:, 0:n] for r_ in tt]
                            if which == 1 and own:
                                dA, dB, db = KAo[:, m0:m0 + nm, :], KBo[:, m0:m0 + nm, :], b_K
                            elif which == 1:
                                dA, dB, db = KAh[:, m0:m0 + nm, :], KBh[:, m0:m0 + nm, :], b_K
                            else:
                                dA, dB, db = QAo[:, m0:m0 + nm, :], QBo[:, m0:m0 + nm, :], b_Q
                            v3 = lambda ap: ap.rearrange("p (m r) -> p m r", r=d)
                            pa, pb_ = banks[bk0][:, 0:n], banks[bk0 + 1][:, 0:n]
                            P.op("dve", lambda e, t1=t1, pa=pa, C=C: e.tensor_tensor(out=t1, in0=pa, in1=C, op=ALU.mult),
                                 reads=[b_TAB[tsel]], writes=[BK[bk0], tb[0]])
                            P.op("dve", lambda e, t4=t4, pa=pa, S=S: e.tensor_tensor(out=t4, in0=pa, in1=S, op=ALU.mult),
                                 reads=[b_TAB[tsel]], writes=[BK[bk0], tb[3]])
                            P.op("dve", lambda e, t2=t2, pb_=pb_, S=S: e.tensor_tensor(out=t2, in0=pb_, in1=S, op=ALU.mult),
                                 reads=[b_TAB[tsel]], writes=[BK[bk0 + 1], tb[1]])
                            P.op("dve", lambda e, t3=t3, pb_=pb_, C=C: e.tensor_tensor(out=t3, in0=pb_, in1=C, op=ALU.mult),
                                 reads=[b_TAB[tsel]], writes=[BK[bk0 + 1], tb[2]])
                            P.op("pool", lambda e, t1=v3(t1), t2=v3(t2), dA=dA: e.tensor_tensor(out=dA, in0=t1, in1=t2, op=ALU.subtract),
                                 reads=[tb[0], tb[1]], writes=[db])
                            P.op("pool", lambda e, t3=v3(t3), t4=v3(t4), dB=dB: e.tensor_tensor(out=dB, in0=t3, in1=t4, op=ALU.add),
                                 reads=[tb[2], tb[3]], writes=[db])
                    nblk = d + d * nb
                    for bi in range(nblk):
                        if bi < d:
                            r = bi
                            if d == 1:
                                lf = lambda k: XH[:, k, 1920:2048]
                            elif d == 4:
                                lf = lambda k, r=r, XHv=XHv: XHv[:, k, r, 384:512]
                            else:
                                lf = lambda k, r=r, XHv=XHv: XHv[:, k, r, 0:128]
                        else:
                            r, kb = divmod(bi - d, nb)
                            lf = lambda k, r=r, kb=kb, XOv=XOv: XOv[:, k, r, kb * 128:(kb + 1) * 128]
                        bsel = bi % 2
                        for k in range(8):
                            mm(banks[bsel][:, 0:256], lf(k), WV[:, k, :], k == 0, k == 7, [b_WV, B_XH, B_XO], bsel)
                        P.op("act", lambda e, bi=bi, bsel=bsel: e.copy(out=V[:, bi, :], in_=banks[bsel][:, 0:256]), writes=[BK[bsel], b_V])
                    if it_i + 1 < len(iters):
                        load_w(*iters[it_i + 1])
                    if debug and hq == 0 and ps_i == 0:
                        dump("ka%d" % g, KA[:, 0:nh + PASS], [b_K])
                        dump("qa%d" % g, QA, [b_Q])
                        dump("v%d" % g, V.rearrange("p a b -> p (a b)"), [b_V])
                    ACCv = ACC.rearrange("p a (m d) -> p a d m", d=d)
                    def att_front(r, kb, par):
                        has_cur = kb >= 0
                        has_prev = kb + 1 < nb
                        if has_cur and has_prev:
                            c_lo, c_hi = 0, 256
                            q0 = r * L + kb * 128
                        elif has_cur:
                            c_lo, c_hi = 0, 128
                            q0 = r * L + kb * 128
                        else:
                            c_lo, c_hi = 128, 256
                            q0 = r * L
                        nq = c_hi - c_lo
                        kc = r * 128 if kb < 0 else nh + r * L + kb * 128
                        pt = PT[par]
                        for hl in range(4):
                            sb_ = banks[4 + hl][:, c_lo:c_hi]
                            kw = {"tile_position": (96, 0)} if hl == 3 else {}
                            for half, (Kt, Qt) in enumerate(((KA, QA), (KBt, QB))):
                                P.op("pe", lambda e, sb_=sb_, Kt=Kt, Qt=Qt, hl=hl, half=half, kw=kw:
                                     e.matmul(sb_, lhsT=Kt[32 * hl:32 * hl + 32, kc:kc + 128], rhs=Qt[32 * hl:32 * hl + 32, q0:q0 + nq],
                                              start=(half == 0), stop=(half == 1), **kw),
                                     reads=[b_K, b_Q], writes=[BK[4 + hl], b_ST])
                        for hl in range(4):
                            sb_ = banks[4 + hl][:, c_lo:c_hi]
                            P.op("act", lambda e, sb_=sb_, pt=pt, hl=hl, c_lo=c_lo, c_hi=c_hi:
                                 e.activation(out=pt[:, hl, c_lo:c_hi], in_=sb_, func=AF.Exp, scale=0.125),
                                 reads=[b_ST], writes=[BK[4 + hl], b_PT[par]])
                        if kb < 0:
                            mk, mkb = mfirst4[:, :, :], Bc["mfirst4"]
                        else:
                            mk, mkb = mask4[:, :, c_lo:c_hi], Bc["mask4"]
                        P.op("dve", lambda e, pt=pt, mk=mk, c_lo=c_lo, c_hi=c_hi:
                             e.tensor_tensor(out=pt[:, :, c_lo:c_hi], in0=pt[:, :, c_lo:c_hi], in1=mk, op=ALU.mult),
                             reads=[mkb], writes=[b_PT[par]])

                    def att_back(r, kb, par):
                        has_cur = kb >= 0
                        has_prev = kb + 1 < nb
                        vb = r if kb < 0 else d + r * nb + kb
                        pt = PT[par]
                        parts = []
                        if has_cur:
                            parts.append((kb, 0, False))
                        if has_prev:
                            parts.append((kb + 1, 128, True))
                        for (qb, pc, is_first) in parts:
                            nbk = 2 + (qb % 2)
                            for hl in range(4):
                                pair, hh = divmod(hl, 2)
                                mm(banks[nbk][64 * hh:64 * hh + 64, pair * 128:(pair + 1) * 128],
                                   V[:, vb, hl * 64:(hl + 1) * 64], pt[:, hl, pc:pc + 128],
                                   is_first and pair == 0, False, [b_V, b_PT[par]], nbk, skip_group_check=True)
                                mm(banks[nbk][64 * hh:64 * hh + 64, (2 + pair) * 128:(3 + pair) * 128],
                                   onesb, pt[:, hl, pc:pc + 128],
                                   False, (not is_first) and hl == 3, [Bc["ones"], b_PT[par]], nbk, skip_group_check=True)
                            if not is_first:
                                dst = ACCv[:, :, r, qb * 128:(qb + 1) * 128]
                                src = banks[nbk][:, :].rearrange("p (a b) -> p a b", a=4)
                                if first_g:
                                    P.op("act", lambda e, dst=dst, src=src: e.copy(out=dst, in_=src), writes=[BK[nbk], b_ACC])
                                else:
                                    P.op("dve", lambda e, dst=dst, src=src: e.tensor_tensor(out=dst, in0=dst, in1=src, op=ALU.add),
                                         writes=[BK[nbk], b_ACC])

                    seq = [(r, kb) for r in range(d) for kb in range(-1, nb)]
                    att_front(seq[0][0], seq[0][1], 0)
                    for ii, (r, kb) in enumerate(seq):
                        if ii + 1 < len(seq):
                            att_front(seq[ii + 1][0], seq[ii + 1][1], (ii + 1) % 2)
                        att_back(r, kb, ii % 2)
                    if debug and hq == 0 and ps_i == 0:
                        dump("acc%d" % g, ACC.rearrange("p a b -> p (a b)"), [b_ACC])
                    if last_g:
                        P.op("act", lambda e: e.activation(out=ACC[:, 2:4, :], in_=ACC[:, 2:4, :], func=AF.Ln), writes=[b_ACC])
                        P.op("act", lambda e: e.activation(out=ACC[:, 2:4, :], in_=ACC[:, 2:4, :], func=AF.Exp, scale=-1.0), writes=[b_ACC])
                        P.op("dve", lambda e, hq=hq: e.tensor_tensor(out=OT[:, 2 * hq:2 * hq + 2, :], in0=ACC[:, 0:2, :], in1=ACC[:, 2:4, :], op=ALU.mult),
                             reads=[b_ACC], writes=[B_OT])

                if debug and ps_i == 0:
                    dump("ot", OT.rearrange("p a b -> p (a b)"), [B_OT])

            if "B" in phases:
                P.barrier()
                WCV = view(80 * KB, [8, 1024], BF16)
                WAO = view(96 * KB, [4, 1024], BF16)
                WCO = view(104 * KB, [4, 1024], BF16)
                WO = view(112 * KB, [8, 1024], BF16)
                U = view(128 * KB, [4, 544], BF16)
                HK = KB // 2
                CACC = view(265 * HK, [4, 512], F32)
                STAT = view(281 * HK, [2, 512], F32)
                UST = view(289 * HK, [4, 512], BF16)
                SIGC = view(297 * HK, [512], F32)
                SIG = [view(301 * HK + 2048 * i, [512], F32) for i in range(2)]
                MM = [view(309 * HK + 2048 * i, [512], F32) for i in range(2)]
                MRG = view(317 * HK, [8, 512], BF16)
                XT2 = [view(333 * HK + 4096 * i, [1024], F32) for i in range(2)]
                Z2 = [view(349 * HK, [1024], F32)] * 2
                X12 = [view(357 * HK, [1024], F32)] * 2
                GB = view(365 * HK, [2, 1024], F32)
                WG2 = [view(381 * HK + 4096 * i, [8, 256], BF16) for i in range(2)]
                SQ = [view(397 * HK, [512], F32)] * 2
                DIAG = view(0, [124, 128], BF16)
                b_WCV, b_WAO, b_WCO, b_WO = Buf("WCV"), Buf("WAO"), Buf("WCO"), Buf("WO")
                b_U = [Buf("U%d" % i) for i in range(4)]
                b_CACC = [Buf("CACC%d" % i) for i in range(4)]
                b_SQ1 = Buf("SQ")
                b_SQ = [b_SQ1, b_SQ1]
                b_STAT, b_UST, b_SIGC, b_DIAG = Buf("STAT"), Buf("UST"), Buf("SIGC"), Buf("DIAG")
                b_SIG = [Buf("SIG0"), Buf("SIG1")]
                b_MM = [Buf("MM0"), Buf("MM1")]
                b_WG2 = [Buf("WG0"), Buf("WG1")]
                b_MRG, b_GB = Buf("MRG"), Buf("GB")
                b_XT2 = [Buf("XT0"), Buf("XT1")]
                b_Z1, b_X11 = Buf("Z0"), Buf("X10")
                b_Z2 = [b_Z1, b_Z1]
                b_X12 = [b_X11, b_X11]
                dma_cast(WCV.rearrange("p a b -> p (a b)"), wcv_d[:, :], [b_WCV], "wcv")
                dma_cast(WAO.rearrange("p a b -> p (a b)"), wao_d[:, :], [b_WAO], "wao")
                dma_cast(WCO.rearrange("p a b -> p (a b)"), wco_d[:, :], [b_WCO], "wco")
                dma_cast(WO.rearrange("p a b -> p (a b)"), wo_d[:, :], [b_WO], "wo")
                dma_sp(GB[:, 0, :], lnv_d[0:1, :].partition_broadcast(128), [b_GB], "gb")
                dma_sp(GB[:, 1, :], lnv_d[1:2, :].partition_broadcast(128), [b_GB], "gb")

                def conv_in_cc(cc, rhsf, n, ucol0):
                    for which in range(2):
                        for k in range(8):
                            mm(banks[which][:, 0:n], WCV[:, k, which * 512 + cc * 128: which * 512 + (cc + 1) * 128], rhsf(k),
                               k == 0, k == 7, [b_WCV, B_XH, B_XO], which)
                    P.op("act", lambda e: e.activation(out=SIGC[:, 0:n], in_=banks[1][:, 0:n], func=AF.Sigmoid),
                         writes=[BK[1], b_SIGC])
                    P.op("dve", lambda e: e.tensor_tensor(out=U[:, cc, ucol0:ucol0 + n], in0=SIGC[:, 0:n], in1=banks[0][:, 0:n], op=ALU.mult),
                         reads=[b_SIGC], writes=[BK[0], b_U[cc]])

                for cc in range(4):
                    conv_in_cc(cc, lambda k: XH[:, k, 2018:2048], 30, 0)
                P.barrier()

                for cc in range(4):
                    for j in range(31):
                        P.op("dve", lambda e, cc=cc, j=j: e.tensor_scalar(out=DIAG[:, cc * 31 + j, :], in0=identb, scalar1=cw[:, cc, j:j + 1], scalar2=None,
                                                                        op0=ALU.mult),
                             reads=[Bc["identb"], Bc["cwv"]], writes=[b_DIAG])

                def stage1(s):
                    t0 = s * 512
                    if s > 0:
                        for cc in range(4):
                            eng = "dve" if cc < 2 else "pool"
                            P.op(eng, lambda e, cc=cc: e.tensor_copy(out=U[:, cc, 0:30], in_=U[:, cc, 512:542]), reads=[b_U[cc]], writes=[b_U[cc]])
                    for cc in range(4):
                        conv_in_cc(cc, lambda k, t0=t0: XO[:, k, t0:t0 + 512], 512, 30)
                        yield
                    for cc in range(4):
                        bank = cc % 2
                        for j in range(31):
                            mm(banks[bank][:, :], DIAG[:, cc * 31 + j, :], U[:, cc, j:j + 512], j == 0, j == 30, [b_DIAG, b_U[cc]], bank)
                            if j % 8 == 7:
                                yield
                        P.op("act", lambda e, cc=cc, bank=bank: e.activation(out=CACC[:, cc, :], in_=banks[bank][:, :], func=AF.Identity,
                                                                           bias=cv[:, 0, cc:cc + 1], scale=1.0),
                             reads=[Bc["cwv"]], writes=[BK[bank], b_CACC[cc]])
                        yield
                    if debug and ps_i == 0 and s == 0:
                        dump("cacc", CACC.rearrange("p a b -> p (a b)"), b_CACC)

                def load_wg(c):
                    dma_cast(WG2[c % 2].rearrange("p a b -> p (a b)"), wgate_d[:, c * 2048:(c + 1) * 2048], [b_WG2[c % 2]], "wg2%d" % (c % 2))

                def stage2(s):
                    t0 = s * 512
                    bC = b_CACC
                    load_wg(0)
                    for cc in range(4):
                        mm(banks[2][:, :], onesf, CACC[:, cc, :], cc == 0, cc == 3, [Bc["ones"], bC[cc]], 2)
                    for cc in range(4):
                        sq = SQ[cc % 2]
                        P.op("act", lambda e, cc=cc, sq=sq: e.activation(out=sq, in_=CACC[:, cc, :], func=AF.Square), reads=[bC[cc]], writes=[b_SQ[cc % 2]])
                        mm(banks[3][:, :], onesf, sq, cc == 0, cc == 3, [Bc["ones"], b_SQ[cc % 2]], 3)
                    mean, rstd = STAT[:, 0, :], STAT[:, 1, :]
                    P.op("act", lambda e: e.mul(out=mean, in_=banks[2][:, :], mul=1.0 / 512), writes=[BK[2], b_STAT])
                    P.op("dve", lambda e: e.tensor_tensor(out=rstd, in0=mean, in1=mean, op=ALU.mult), reads=[b_STAT], writes=[b_STAT])
                    P.op("dve", lambda e: e.scalar_tensor_tensor(out=rstd, in0=banks[3][:, :], scalar=1.0 / 512, in1=rstd, op0=ALU.mult, op1=ALU.subtract),
                         reads=[b_STAT], writes=[BK[3], b_STAT])
                    P.op("act", lambda e: e.activation(out=rstd, in_=rstd, func=AF.Sqrt, bias=epsb[:, 0:1], scale=1.0), reads=[b_STAT, Bc["ones"]], writes=[b_STAT])
                    P.op("dve", lambda e: e.reciprocal(out=rstd, in_=rstd), reads=[b_STAT], writes=[b_STAT])
                    for cc in range(4):
                        P.op("dve", lambda e, cc=cc: e.tensor_tensor(out=CACC[:, cc, :], in0=CACC[:, cc, :], in1=mean, op=ALU.subtract),
                             reads=[bC[cc], b_STAT], writes=[bC[cc]])
                        P.op("dve", lambda e, cc=cc: e.tensor_tensor(out=CACC[:, cc, :], in0=CACC[:, cc, :], in1=rstd, op=ALU.mult),
                             reads=[bC[cc], b_STAT], writes=[bC[cc]])
                        P.op("act", lambda e, cc=cc: e.activation(out=UST[:, cc, :], in_=CACC[:, cc, :], func=AF.Silu, bias=cv[:, 2, cc:cc + 1], scale=cv[:, 1, cc:cc + 1]),
                             reads=[bC[cc], Bc["cwv"]], writes=[b_UST])
                    if debug and ps_i == 0 and s == 0:
                        dump("ust", UST.rearrange("p a b -> p (a b)"), [b_UST])
                    yield
                    for c in range(8):
                        wsel = c % 2
                        WG = WG2[wsel]
                        if c + 1 < 8:
                            load_wg(c + 1)
                        for which in range(2):
                            for k in range(8):
                                mm(banks[4 + which][:, :], WG[:, k, which * 128:(which + 1) * 128], XO[:, k, t0:t0 + 512],
                                   k == 0, k == 7, [b_WG2[wsel], B_XO], 4 + which)
                        for k in range(4):
                            mm(banks[6][:, :], WAO[:, k, c * 128:(c + 1) * 128], OT[:, k, t0:t0 + 512], k == 0, k == 3, [b_WAO, B_OT], 6)
                        for k in range(4):
                            mm(banks[7][:, :], WCO[:, k, c * 128:(c + 1) * 128], UST[:, k, :], k == 0, k == 3, [b_WCO, b_UST], 7)
                        for which in range(2):
                            P.op("act", lambda e, which=which: e.activation(out=SIG[which], in_=banks[4 + which][:, :], func=AF.Sigmoid),
                                 writes=[BK[4 + which], b_SIG[which]])
                            P.op("dve", lambda e, which=which: e.tensor_tensor(out=MM[which], in0=SIG[which], in1=banks[6 + which][:, :], op=ALU.mult),
                                 reads=[b_SIG[which]], writes=[BK[6 + which], b_MM[which]])
                        P.op("pool", lambda e, c=c: e.tensor_tensor(out=MRG[:, c, :], in0=MM[0], in1=MM[1], op=ALU.add),
                             reads=[b_MM[0], b_MM[1]], writes=[b_MRG])
                        yield
                    if debug and ps_i == 0 and s == 0:
                        dump("mrg", MRG.rearrange("p a b -> p (a b)"), [b_MRG])
                    for j in range(4):
                        tok = ps_i * PASS + t0 + j * 128
                        jp = j % 2
                        XT, Z, X1 = XT2[jp], Z2[jp], X12[jp]
                        dma_sp(XT, xin[HALO + tok:HALO + tok + 128, :], [b_XT2[jp]], "xt%d" % jp)
                        for half in range(2):
                            for k in range(8):
                                mm(banks[2 + half][:, :], MRG[:, k, j * 128:(j + 1) * 128], WO[:, k, half * 512:(half + 1) * 512],
                                   k == 0, k == 7, [b_MRG, b_WO], 2 + half)
                        for half in range(2):
                            P.op("dve", lambda e, half=half, Z=Z, XT=XT: e.scalar_tensor_tensor(out=Z[:, half * 512:(half + 1) * 512], in0=XT[:, half * 512:(half + 1) * 512],
                                                                                               scalar=ALPHA, in1=banks[2 + half][:, :], op0=ALU.mult, op1=ALU.add),
                                 reads=[b_XT2[jp]], writes=[BK[2 + half], b_Z2[jp]])
                        if debug and ps_i == 0 and s == 0 and j == 0:
                            dump("z1", Z, [b_Z2[jp]])
                        layer_norm_tile(Z, b_Z2[jp], X1, b_X12[jp], GB[:, 0, :], GB[:, 1, :], b_GB, jp)
                        if debug and ps_i == 0 and s == 0 and j == 0:
                            dump("x1", X1, [b_X12[jp]])
                        dma_cast(x1d[tok:tok + 128, :], X1, [X1DB[tok // 128]], "x1s%d" % jp, reads=[b_X12[jp]])
                        yield

                for _ in stage1(0):
                    pass
                for s in range(4):
                    g2 = stage2(s)
                    g1 = stage1(s + 1) if s < 3 else iter(())
                    d1 = d2 = False
                    while not (d1 and d2):
                        if not d2:
                            try:
                                next(g2)
                            except StopIteration:
                                d2 = True
                        for _ in range(2):
                            if not d1:
                                try:
                                    next(g1)
                                except StopIteration:
                                    d1 = True

        if "C" in phases:
            P.barrier()
            WG = view(0, [NF // 2, 8, 256], BF16)
            WU = view(44 * KB, [NF // 2, 8, 256], BF16)
            WD = view(88 * KB, [NF, 1024], BF16)
            X1A = [view(132 * KB + 4096 * i, [1024], F32) for i in range(2)]
            X1B = [view(140 * KB + 4096 * i, [1024], F32) for i in range(2)]
            X1T = view(148 * KB, [8, 512], BF16)
            HT = view(156 * KB, [NF, 512], BF16)
            SG = [view(178 * KB + 2048 * i, [512], F32) for i in range(2)]
            ZC = view(182 * KB, [1024], F32)
            OUTT = view(186 * KB, [1024], F32)
            GBC = view(190 * KB, [2, 1024], F32)
            b_X1A = [Buf("X1A0"), Buf("X1A1")]
            b_X1B = [Buf("X1B0"), Buf("X1B1")]
            b_X1T, b_HT = Buf("X1T"), Buf("HT")
            b_SG = [Buf("SG0"), Buf("SG1")]
            b_Z, b_OUTT, b_GB = Buf("Zc"), Buf("OUTT"), Buf("GBc")
            NGRP = 11
            b_WGU = [Buf("WGU%d" % i) for i in range(NGRP)]
            b_WD = [Buf("WD%d" % i) for i in range(NGRP)]
            for gi in range(NGRP):
                dma_cast(WG[:, gi, :, :].rearrange("p a b -> p (a b)"), wg_d[gi, :, :], [b_WGU[gi]], "wgu%d" % gi)
                dma_cast(WU[:, gi, :, :].rearrange("p a b -> p (a b)"), wu_d[gi, :, :], [b_WGU[gi]], "wgu%d" % gi)
            for gi in range(NGRP):
                dma_cast(WD[:, 2 * gi:2 * gi + 2, :], wd_d[:, 2 * gi:2 * gi + 2, :], [b_WD[gi]], "wd%d" % gi)
            dma_sp(GBC[:, 0, :], lnv_d[2:3, :].partition_broadcast(128), [b_GB], "gbc")
            dma_sp(GBC[:, 1, :], lnv_d[3:4, :].partition_broadcast(128), [b_GB], "gbc")
            n_ct = (TOK // 512) if npass == 2 else (PASS // 512)
            xa = 0
            xb = 0
            for ct in range(n_ct):
                for j in range(4):
                    ti = ct * 4 + j
                    sel = xa % 2
                    xa += 1
                    dma_sp(X1A[sel], x1d[ti * 128:(ti + 1) * 128, :], [b_X1A[sel]], "x1a%d" % sel, reads=[X1DB[ti]])
                    for half in range(2):
                        for kk in range(4):
                            k = half * 4 + kk
                            P.op("pe", lambda e, half=half, kk=kk, k=k, sel=sel: e.transpose(banks[half][:, kk * 128:(kk + 1) * 128], X1A[sel][:, k * 128:(k + 1) * 128], identf),
                                 reads=[b_X1A[sel], Bc["identf"]], writes=[BK[half]])
                    for half in range(2):
                        src = banks[half][:, :].rearrange("p (k c) -> p k c", k=4)
                        dst = X1T[:, half * 4:half * 4 + 4, j * 128:(j + 1) * 128]
                        if half == 0:
                            P.op("act", lambda e, src=src, dst=dst: e.copy(out=dst, in_=src), writes=[BK[half], b_X1T])
                        else:
                            P.op("dve", lambda e, src=src, dst=dst: e.tensor_copy(out=dst, in_=src), writes=[BK[half], b_X1T])
                for f in range(NF):
                    bg = 2 + 2 * (f % 2)
                    for which, Wt in enumerate((WG, WU)):
                        for k in range(8):
                            mm(banks[bg + which][:, :], Wt[:, f // 2, k, (f % 2) * 128:(f % 2 + 1) * 128], X1T[:, k, :], k == 0, k == 7,
                               [b_WGU[f // 2], b_X1T], bg + which)
                    sg = SG[f % 2]
                    P.op("act", lambda e, sg=sg, bg=bg: e.activation(out=sg, in_=banks[bg][:, :], func=AF.Silu), writes=[BK[bg], b_SG[f % 2]])
                    P.op("dve", lambda e, sg=sg, bg=bg, f=f: e.tensor_tensor(out=HT[:, f, :], in0=sg, in1=banks[bg + 1][:, :], op=ALU.mult),
                         reads=[b_SG[f % 2]], writes=[BK[bg + 1], b_HT])
                for j in range(4):
                    ti = ct * 4 + j
                    sel = xb % 2
                    xb += 1
                    dma_sp(X1B[sel], x1d[ti * 128:(ti + 1) * 128, :], [b_X1B[sel]], "x1b%d" % sel, reads=[X1DB[ti]])
                    for half in range(2):
                        for f in range(NF):
                            mm(banks[6 + half][:, :], HT[:, f, j * 128:(j + 1) * 128], WD[:, f, half * 512:(half + 1) * 512],
                               f == 0, f == NF - 1, [b_HT, b_WD[f // 2]], 6 + half)
                    for half in range(2):
                        P.op("dve", lambda e, half=half, sel=sel: e.scalar_tensor_tensor(out=ZC[:, half * 512:(half + 1) * 512], in0=X1B[sel][:, half * 512:(half + 1) * 512],
                                                                                        scalar=ALPHA, in1=banks[6 + half][:, :], op0=ALU.mult, op1=ALU.add),
                             reads=[b_X1B[sel]], writes=[BK[6 + half], b_Z])
                    layer_norm_tile(ZC, b_Z, OUTT, b_OUTT, GBC[:, 0, :], GBC[:, 1, :], b_GB)
                    dma_cast(y_d[ti * 128:(ti + 1) * 128, :], OUTT, [], "ys", reads=[b_OUTT])

        finals = [P.dsem(n) for n in ("ys", "x1s0", "x1s1", "dbg") if n in P.dma_sems]
        P.emit(nc, st, final_wait_sems=finals)
    return nc


def _rope_tables():
    inv_freq = (1.0 / (np.float32(10000.0) ** (np.arange(0, 64, 2, dtype=np.float32) / np.float32(64)))).astype(np.float32)
    pos = np.arange(-HALO, SEQ, dtype=np.float32)
    ang = (pos[:, None] * inv_freq[None, :]).astype(np.float32)
    return np.cos(ang).astype(np.float32), np.sin(ang).astype(np.float32)


def _pass_positions(g, p0):
    d = GROUP_D[g]
    L = PASS // d
    halo = np.array([p0 + r + d * (i - 128) for r in range(d) for i in range(128)], dtype=np.int64)
    own = np.array([p0 + r + d * m for r in range(d) for m in range(L)], dtype=np.int64)
    return np.concatenate([halo, own])


def prepare_inputs(x, w_in, conv_w, conv_b, conv_ln_g, conv_ln_b, w_attn_out, w_conv_out, w_o,
                   ln1_g, ln1_b, w_ffn_gate, w_ffn_up, w_ffn_down, ln2_g, ln2_b):
    f = lambda a: np.ascontiguousarray(np.asarray(a, dtype=np.float32))
    x = f(x)
    w_in = f(w_in)[0]

    def kmajor(w):
        K, N = w.shape
        return np.ascontiguousarray(w.reshape(K // 128, 128, N).transpose(1, 0, 2))

    wqk = np.empty((3, 2, 128, 8, 4, 128), np.float32)
    wv = np.empty((3, 2, 128, 8, 256), np.float32)
    for g in range(3):
        for hq in range(2):
            for qk in range(2):
                for half in range(2):
                    cols = np.array([g * 1536 + qk * 512 + (4 * hq + hl) * 64 + half * 32 + j for hl in range(4) for j in range(32)])
                    wqk[g, hq, :, :, qk * 2 + half, :] = kmajor(w_in[:, cols])
            c0 = g * 1536 + 1024 + 4 * hq * 64
            wv[g, hq] = kmajor(w_in[:, c0:c0 + 256])
    wcv = kmajor(w_in[:, 4608:5632])
    wgate = np.empty((128, 8, 8, 2, 128), np.float32)
    for which in range(2):
        wk = kmajor(w_in[:, 5632 + which * 1024: 5632 + (which + 1) * 1024])
        wgate[:, :, :, which, :] = wk.reshape(128, 8, 8, 128).transpose(0, 2, 1, 3)
    shared = {
        "wqk": wqk.reshape(3, 2, 128, -1), "wv": wv.reshape(3, 2, 128, -1),
        "wcv": wcv.reshape(128, -1), "wgate": wgate.reshape(128, -1),
        "wao": kmajor(f(w_attn_out)[0]).reshape(128, -1), "wco": kmajor(f(w_conv_out)[0]).reshape(128, -1),
        "wo": kmajor(f(w_o)[0]).reshape(128, -1),
        "wg": np.ascontiguousarray(kmajor(f(w_ffn_gate)[0]).reshape(128, 8, NF // 2, 256).transpose(2, 0, 1, 3)).reshape(NF // 2, 128, -1),
        "wu": np.ascontiguousarray(kmajor(f(w_ffn_up)[0]).reshape(128, 8, NF // 2, 256).transpose(2, 0, 1, 3)).reshape(NF // 2, 128, -1),
        "wd": kmajor(f(w_ffn_down)[0]),
        "cw": np.ascontiguousarray(f(conv_w)[0].T.reshape(4, 128, 31).transpose(1, 0, 2)).reshape(128, -1),
        "cv": np.ascontiguousarray(np.stack([f(conv_b)[0], f(conv_ln_g)[0], f(conv_ln_b)[0]]).reshape(3, 4, 128).transpose(2, 0, 1)).reshape(128, -1),
        "lnv": np.stack([f(ln1_g)[0], f(ln1_b)[0], f(ln2_g)[0], f(ln2_b)[0]]),
        "ident": np.eye(128, dtype=np.float32),
    }
    kk = np.arange(128)[:, None]
    qq = np.arange(128)[None, :]
    cur = (kk <= qq).astype(np.float32)
    prev = (kk >= qq).astype(np.float32)
    m2 = np.concatenate([cur, prev], axis=1)
    shared["mask4"] = np.ascontiguousarray(np.broadcast_to(m2[:, None, :], (128, 4, 256))).reshape(128, -1)
    prev4 = np.ascontiguousarray(np.broadcast_to(prev[:, None, :], (128, 4, 128))).reshape(128, -1)
    cosT, sinT = _rope_tables()
    in_maps = []
    for c in range(8):
        b, hf = divmod(c, 2)
        t0 = hf * TOK
        xin = np.zeros((HALO + TOK, D), np.float32)
        if hf == 0:
            xin[HALO:] = x[b, 0:TOK]
        else:
            xin[:] = x[b, t0 - HALO:t0 + TOK]
        m = dict(shared)
        m["xin"] = xin
        mf = np.zeros((2, 128, 512), np.float32)
        if hf == 1:
            mf[0] = prev4
        mf[1] = prev4
        m["mfirst4"] = mf
        rot = np.empty((2, 2, 128, HALO + PASS), np.float32)
        for ps_i in range(2):
            pos = np.arange(t0 + ps_i * PASS - HALO, t0 + ps_i * PASS + PASS) + HALO
            rot[ps_i, 0] = np.tile(cosT[pos].T, (4, 1))
            rot[ps_i, 1] = np.tile(sinT[pos].T, (4, 1))
        m["rotn"] = rot
        in_maps.append(m)
    return in_maps


_NC_CACHE = {}


def kernel(**inputs):
    in_maps = prepare_inputs(**inputs)
    if "nc" not in _NC_CACHE:
        _NC_CACHE["nc"] = build_program()
    nc = _NC_CACHE["nc"]
    res = run_bass_kernel_spmd(nc, in_maps, core_ids=list(range(8)))
    out = np.empty((NBATCH, SEQ, D), np.float32)
    for c in range(8):
        b, hf = divmod(c, 2)
        out[b, hf * TOK:(hf + 1) * TOK] = res.results[c]["y"]
    return out
```
